# Optimizing a Trainium2 kernel written in Bass

```python
import jax
import jax.numpy as jnp
from jax import lax
import numpy as np

D_MODEL = 1024
BATCH = 16
SEQ = 2048
DEPTH = 4

GRID_W = 64
CTX_LEN = 256
N_MIXERS = 4
MIX_CONV = 0
MIX_POOL = 1
MIX_MLA = 2
MIX_CHUNK = 3
BRANCH = D_MODEL
EPS = 1e-6
CONV_WIDTH = 31
POOL_WINDOWS = (2, 4, 8, 16)
POOL_GROUP = BRANCH // len(POOL_WINDOWS)
MLA_HEADS = D_MODEL // 128
MLA_NOPE = 128
MLA_ROPE = 64
MLA_V = 128
MLA_Q_RANK = 3 * D_MODEL // 8
MLA_KV_RANK = D_MODEL // 4
MLA_KVC = MLA_KV_RANK + MLA_ROPE
MLA_SCALE = (MLA_NOPE + MLA_ROPE) ** -0.5
ROPE_THETA = 10000.0
Q_BLOCK = 128
CHUNK = 128
CHUNK_GROUPS = 8
CHUNK_GC = BRANCH // CHUNK_GROUPS

kernel_name = 'hybrid_interleaved_diffusion_block'


def _n_layers_of(kind):
    return len(range(kind, DEPTH, N_MIXERS))


def _rms(x, g):
    xf = x.astype(jnp.float32)
    y = xf * lax.rsqrt(jnp.mean(xf * xf, axis=-1, keepdims=True) + EPS)
    return (y * g.astype(jnp.float32)).astype(x.dtype)


def _layernorm(x, g, b):
    xf = x.astype(jnp.float32)
    mu = jnp.mean(xf, axis=-1, keepdims=True)
    var = jnp.mean(jnp.square(xf - mu), axis=-1, keepdims=True)
    y = (xf - mu) * lax.rsqrt(var + EPS)
    return (y * g.astype(jnp.float32) + b.astype(jnp.float32)).astype(x.dtype)


def _rope_tables(rows):
    row_id = jnp.repeat(jnp.arange(rows), GRID_W).astype(jnp.float32)
    col_id = jnp.tile(jnp.arange(GRID_W), rows).astype(jnp.float32)
    axis_dim = MLA_ROPE // 2
    freqs = ROPE_THETA ** (-jnp.arange(0, axis_dim, 2, dtype=jnp.float32) / axis_dim)
    ar = row_id[:, None] * freqs
    ac = col_id[:, None] * freqs
    return (jnp.cos(ar), jnp.sin(ar), jnp.cos(ac), jnp.sin(ac))


def _rope1d(x, cos, sin):
    x1, x2 = jnp.split(x, 2, axis=-1)
    return jnp.concatenate([x1 * cos - x2 * sin, x1 * sin + x2 * cos], axis=-1)


def _rope2d(x, tabs):
    cr, sr, cc, sc = [t.astype(x.dtype) for t in tabs]
    if x.ndim == 4:
        cr, sr, cc, sc = cr[:, None], sr[:, None], cc[:, None], sc[:, None]
    xr, xc = jnp.split(x, 2, axis=-1)
    return jnp.concatenate([_rope1d(xr, cr, sr), _rope1d(xc, cc, sc)], axis=-1)


def _conv_mixer(h, w_in, dw, db, ln_g, ln_b, w_out):
    a, b, g = jnp.split(h @ w_in, 3, axis=-1)
    y = a * jax.nn.sigmoid(b)
    y = lax.conv_general_dilated(
        y, dw[:, None, :].astype(y.dtype), window_strides=(1,),
        padding=[(CONV_WIDTH // 2, CONV_WIDTH // 2)],
        dimension_numbers=('NWC', 'WIO', 'NWC'),
        feature_group_count=BRANCH) + db
    y = jax.nn.silu(_layernorm(y, ln_g, ln_b)) * jax.nn.silu(g)
    return y @ w_out


def _window_mean(v, w):
    L = v.shape[1]
    cs = jnp.pad(jnp.cumsum(v.astype(jnp.float32), axis=1), ((0, 0), (1, 0), (0, 0)))
    t = jnp.arange(L)
    start = jnp.clip(t - w // 2, 0, L)
    end = jnp.clip(t + (w - w // 2), 0, L)
    total = jnp.take(cs, end, axis=1) - jnp.take(cs, start, axis=1)
    cnt = (end - start).astype(jnp.float32)
    return (total / cnt[None, :, None]).astype(v.dtype)


def _pool_mixer(h, w_in, w_grp, scale, w_out):
    v, g = jnp.split(h @ w_in, 2, axis=-1)
    B, L, _ = v.shape
    vg = v.reshape(B, L, len(POOL_WINDOWS), POOL_GROUP)
    pooled = jnp.stack([_window_mean(vg[:, :, k], w) for k, w in enumerate(POOL_WINDOWS)], axis=2) - vg
    y = jnp.einsum('blgc,gcd->blgd', pooled, w_grp).reshape(B, L, BRANCH) * scale
    return (y * jax.nn.silu(g)) @ w_out


def _mla_keys(pkv, kv_norm, w_ukv, k_nope_g, k_rope_g, tabs):
    ckv, kr = jnp.split(pkv, [MLA_KV_RANK], axis=-1)
    B, L, _ = ckv.shape
    kv = (_rms(ckv, kv_norm) @ w_ukv).reshape(B, L, MLA_HEADS, MLA_NOPE + MLA_V)
    kn, v = jnp.split(kv, [MLA_NOPE], axis=-1)
    kn = _rms(kn, k_nope_g)
    kr = _rms(kr, k_rope_g)
    if tabs is not None:
        kr = _rope2d(kr, tabs)
    return kn, kr, v


def _mla_queries(cq, q_norm, w_uq, q_nope_g, q_rope_g, tabs):
    B, L, _ = cq.shape
    q = (_rms(cq, q_norm) @ w_uq).reshape(B, L, MLA_HEADS, MLA_NOPE + MLA_ROPE)
    qn, qr = jnp.split(q, [MLA_NOPE], axis=-1)
    qn = _rms(qn, q_nope_g)
    qr = _rms(qr, q_rope_g)
    if tabs is not None:
        qr = _rope2d(qr, tabs)
    return qn, qr


def _attend(qn, qr, kn, kr, v):
    s = (jnp.einsum('bqhd,bkhd->bhqk', qn, kn)
         + jnp.einsum('bqhr,bkr->bhqk', qr, kr)).astype(jnp.float32) * MLA_SCALE
    p = jax.nn.softmax(s, axis=-1).astype(v.dtype)
    return jnp.einsum('bhqk,bkhd->bqhd', p, v)


def _attend_blocks(qn, qr, kn, kr, v):
    B, L, H, _ = qn.shape
    nb = L // Q_BLOCK

    def blk(t):
        return jnp.moveaxis(t.reshape((B, nb, Q_BLOCK) + t.shape[2:]), 1, 0)

    out = lax.map(lambda q: _attend(q[0], q[1], kn, kr, v), (blk(qn), blk(qr)))
    return jnp.moveaxis(out, 0, 1).reshape(B, L, H * MLA_V)


def _mla_mixer(h, hc, with_ctx_out, tabs, w_in, q_norm, kv_norm, w_uq, w_ukv, nope_g, rope_g, w_out):
    pkv, cq, g = jnp.split(h @ w_in, [MLA_KVC, MLA_KVC + MLA_Q_RANK], axis=-1)
    kn, kr, v = _mla_keys(pkv, kv_norm, w_ukv, nope_g[1], rope_g[1], tabs)
    qn, qr = _mla_queries(cq, q_norm, w_uq, nope_g[0], rope_g[0], tabs)
    pc = hc @ (w_in if with_ctx_out else w_in[:, :MLA_KVC])
    knc, krc, vc = _mla_keys(pc[..., :MLA_KVC], kv_norm, w_ukv, nope_g[1], rope_g[1], None)
    o = _attend_blocks(qn, qr,
                       jnp.concatenate([kn, knc], axis=1),
                       jnp.concatenate([kr, krc], axis=1),
                       jnp.concatenate([v, vc], axis=1))
    out = (o * jax.nn.silu(g)) @ w_out
    out_c = None
    if with_ctx_out:
        cqc, gc = jnp.split(pc[..., MLA_KVC:], [MLA_Q_RANK], axis=-1)
        qnc, qrc = _mla_queries(cqc, q_norm, w_uq, nope_g[0], rope_g[0], None)
        Bc, Lc = hc.shape[0], hc.shape[1]
        oc = _attend(qnc, qrc, knc, krc, vc).reshape(Bc, Lc, MLA_HEADS * MLA_V)
        out_c = (oc * jax.nn.silu(gc)) @ w_out
    return out, out_c


def _chunk_mixer(h, w_in, ln_g, ln_b, w_s, b_s, w_out):
    u, v, g = jnp.split(h @ w_in, 3, axis=-1)
    B, L, _ = v.shape
    v = _layernorm(v, ln_g, ln_b).reshape(B, L // CHUNK, CHUNK, CHUNK_GROUPS, CHUNK_GC)
    s = jnp.einsum('gpq,bnqgc->bnpgc', w_s, v) + b_s[:, :, None]
    y = u * s.reshape(B, L, BRANCH) * jax.nn.silu(g)
    return y @ w_out


def setup_inputs(seed: int = 0) -> dict:
    key = jax.random.key(seed)
    ks = iter(jax.random.split(key, 40))

    def nrm(shape, s):
        return jax.random.normal(next(ks), shape, jnp.float32) * s

    nA, nB, nC, nD = (_n_layers_of(k) for k in range(N_MIXERS))
    D, E = D_MODEL, BRANCH
    HQK = MLA_HEADS * (MLA_NOPE + MLA_ROPE)
    HKV = MLA_HEADS * (MLA_NOPE + MLA_V)
    HV = MLA_HEADS * MLA_V
    return {
        'x': nrm((BATCH, SEQ, D), 1.0),
        'c': nrm((BATCH, D), 1.0),
        'ctx': nrm((BATCH, CTX_LEN, D), 1.0),
        'c_ctx': nrm((D,), 1.0),
        'norm_g': 1.0 + nrm((DEPTH, D), 0.05),
        'w_mod': nrm((DEPTH, D, 3 * D), 0.5 * D ** -0.5),
        'b_mod': nrm((DEPTH, 3 * D), 0.01),
        'cv_w_in': nrm((nA, D, 3 * E), D ** -0.5),
        'cv_dw': nrm((nA, CONV_WIDTH, E), CONV_WIDTH ** -0.5),
        'cv_db': nrm((nA, E), 0.01),
        'cv_ln_g': 1.0 + nrm((nA, E), 0.05),
        'cv_ln_b': nrm((nA, E), 0.01),
        'cv_w_out': nrm((nA, E, D), E ** -0.5),
        'pl_w_in': nrm((nB, D, 2 * E), D ** -0.5),
        'pl_w_grp': nrm((nB, len(POOL_WINDOWS), POOL_GROUP, POOL_GROUP), POOL_GROUP ** -0.5),
        'pl_scale': 1.0 + nrm((nB, E), 0.05),
        'pl_w_out': nrm((nB, E, D), E ** -0.5),
        'ml_w_in': nrm((nC, D, MLA_KVC + MLA_Q_RANK + HV), D ** -0.5),
        'ml_q_norm': 1.0 + nrm((nC, MLA_Q_RANK), 0.05),
        'ml_kv_norm': 1.0 + nrm((nC, MLA_KV_RANK), 0.05),
        'ml_w_uq': nrm((nC, MLA_Q_RANK, HQK), MLA_Q_RANK ** -0.5),
        'ml_w_ukv': nrm((nC, MLA_KV_RANK, HKV), MLA_KV_RANK ** -0.5),
        'ml_nope_norm': 1.0 + nrm((nC, 2, MLA_NOPE), 0.05),
        'ml_rope_norm': 1.0 + nrm((nC, 2, MLA_ROPE), 0.05),
        'ml_w_out': nrm((nC, HV, D), HV ** -0.5),
        'ch_w_in': nrm((nD, D, 3 * E), D ** -0.5),
        'ch_ln_g': 1.0 + nrm((nD, E), 0.05),
        'ch_ln_b': nrm((nD, E), 0.01),
        'ch_w_s': nrm((nD, CHUNK_GROUPS, CHUNK, CHUNK), CHUNK ** -0.5),
        'ch_b_s': 1.0 + nrm((nD, CHUNK, CHUNK_GROUPS), 0.05),
        'ch_w_out': nrm((nD, E, D), E ** -0.5),
    }


def reference(x, c, ctx, c_ctx, norm_g, w_mod, b_mod,
              cv_w_in, cv_dw, cv_db, cv_ln_g, cv_ln_b, cv_w_out,
              pl_w_in, pl_w_grp, pl_scale, pl_w_out,
              ml_w_in, ml_q_norm, ml_kv_norm, ml_w_uq, ml_w_ukv, ml_nope_norm, ml_rope_norm, ml_w_out,
              ch_w_in, ch_ln_g, ch_ln_b, ch_w_s, ch_b_s, ch_w_out):
    L = x.shape[1]
    ROWS = L // GRID_W
    tabs = _rope_tables(ROWS)
    s_lat = jax.nn.silu(c)
    s_ctx = jax.nn.silu(c_ctx)
    cx = ctx
    for i in range(DEPTH):
        kind, j = i % N_MIXERS, i // N_MIXERS
        ctx_out = any(k % N_MIXERS == MIX_MLA for k in range(i + 1, DEPTH))
        ctx_in = ctx_out or kind == MIX_MLA
        sh, sc, gt = jnp.split((s_lat @ w_mod[i] + b_mod[i])[:, None, :], 3, axis=-1)
        h = _rms(x, norm_g[i]) * (1.0 + sc) + sh
        if ctx_in:
            shc, scc, gtc = jnp.split(s_ctx @ w_mod[i] + b_mod[i], 3, axis=-1)
            hc = _rms(cx, norm_g[i]) * (1.0 + scc) + shc
        if kind == MIX_CONV:
            args = (cv_w_in[j], cv_dw[j], cv_db[j], cv_ln_g[j], cv_ln_b[j], cv_w_out[j])
            o = _conv_mixer(h, *args)
            oc = _conv_mixer(hc, *args) if ctx_out else None
        elif kind == MIX_POOL:
            args = (pl_w_in[j], pl_w_grp[j], pl_scale[j], pl_w_out[j])
            o = _pool_mixer(h, *args)
            oc = _pool_mixer(hc, *args) if ctx_out else None
        elif kind == MIX_MLA:
            o, oc = _mla_mixer(h, hc, ctx_out, tabs, ml_w_in[j], ml_q_norm[j], ml_kv_norm[j],
                               ml_w_uq[j], ml_w_ukv[j], ml_nope_norm[j], ml_rope_norm[j], ml_w_out[j])
        else:
            args = (ch_w_in[j], ch_ln_g[j], ch_ln_b[j], ch_w_s[j], ch_b_s[j], ch_w_out[j])
            o = _chunk_mixer(h, *args)
            oc = _chunk_mixer(hc, *args) if ctx_out else None
        x = x + gt * o
        if ctx_out:
            cx = cx + gtc * oc
    return x
```

```python
import numpy as np
from contextlib import ExitStack
import concourse.bass as bass
import concourse.mybir as mybir
from concourse.bass_utils import run_bass_kernel_spmd

F32 = mybir.dt.float32
BF16 = mybir.dt.bfloat16
AF = mybir.ActivationFunctionType
ALU = mybir.AluOpType

D = 1024
SEQ = 2048
CTX = 256
NB = 2
TT = 256
EPS = 1e-6
ENG_NAMES = ("pe", "act", "dve", "pool", "sp")
RELAX = False
RELAX_WAW = ("act", "dve", "pool")
RELAX_WAR = ("act", "dve", "pool")


class Buf:
    __slots__ = ("name", "lw", "rd")

    def __init__(self, name=""):
        self.name = name
        self.lw = []
        self.rd = []


class V:
    __slots__ = ("ap", "bufs")

    def __init__(self, ap, bufs=None):
        self.ap = ap
        if bufs is None:
            bufs = [Buf()]
        self.bufs = bufs if isinstance(bufs, (list, tuple)) else [bufs]

    def __getitem__(self, idx):
        return V(self.ap[idx], self.bufs)


class Op:
    __slots__ = ("eng", "fn", "deps", "sig", "sigval", "is_dma", "dsem", "dval", "prev_dma")

    def __init__(self, eng, fn):
        self.eng = eng
        self.fn = fn
        self.deps = []
        self.sig = False
        self.sigval = 0
        self.is_dma = False
        self.dsem = None
        self.dval = 0
        self.prev_dma = None


class Sched:
    def __init__(self, nc, n_dma_sems=16):
        self.nc = nc
        self.ops = {e: [] for e in ENG_NAMES}
        self.n_dma_sems = n_dma_sems
        self.dma_rr = {e: 0 for e in ENG_NAMES}
        self.dma_last = {}
        self.dma_cnt = {}
        self.pending = {e: [] for e in ENG_NAMES}

    def barrier(self):
        lasts = []
        for e in ENG_NAMES:
            for o in reversed(self.ops[e]):
                if not o.is_dma:
                    lasts.append(o)
                    break
        dmas = list(self.dma_last.values())
        for e in ENG_NAMES:
            self.pending[e] = [o for o in lasts if o.eng != e] + dmas

    def op(self, eng, fn, reads=(), writes=(), dma=False):
        o = Op(eng, fn)
        deps = {}
        rb = []
        for r in reads:
            if r is None:
                continue
            rb.extend(r.bufs if isinstance(r, V) else [r])
        wb = []
        for w in writes:
            wb.extend(w.bufs if isinstance(w, V) else [w])
        for b in rb:
            for w_ in b.lw:
                deps[id(w_)] = w_
        rbs = set(id(b) for b in rb)
        for b in wb:
            if not (dma and b.lw and all(w_.is_dma for w_ in b.lw)):
                for w_ in b.lw:
                    if RELAX and (eng in RELAX_WAW) and (not dma) and (not w_.is_dma) and w_.eng == eng and id(b) not in rbs:
                        continue
                    deps[id(w_)] = w_
            for r in b.rd:
                if RELAX and (eng in RELAX_WAR) and (not dma) and (not r.is_dma) and r.eng == eng:
                    continue
                deps[id(r)] = r
        for d in deps.values():
            if (not d.is_dma) and (not dma) and d.eng == eng and eng == "pe":
                continue
            o.deps.append(d)
        if self.pending[eng]:
            o.deps.extend(self.pending[eng])
            self.pending[eng] = []
        if dma:
            o.is_dma = True
        for b in rb:
            if not dma:
                b.rd = [r for r in b.rd if r.is_dma or r.eng != eng]
            b.rd.append(o)
        for b in wb:
            if dma and b.lw and all(w_.is_dma for w_ in b.lw):
                b.lw = b.lw + [o]
            else:
                b.lw = [o]
            b.rd = []
        if dma:
            slot = self.dma_rr[eng] % self.n_dma_sems
            self.dma_rr[eng] += 1
            key = (eng, slot)
            o.prev_dma = self.dma_last.get(key)
            self.dma_cnt[key] = self.dma_cnt.get(key, 0) + 1
            o.dsem = key
            o.dval = 16 * self.dma_cnt[key]
            self.dma_last[key] = o
        self.ops[eng].append(o)
        return o

    def emit(self, stack, final_dmas):
        nc = self.nc
        for e in ENG_NAMES:
            for o in self.ops[e]:
                for d in o.deps:
                    if not d.is_dma:
                        d.sig = True
        for e in ENG_NAMES:
            c = 0
            for o in self.ops[e]:
                if o.sig and not o.is_dma:
                    c += 1
                    o.sigval = c
        esem = {e: stack.enter_context(nc.semaphore("s_" + e)) for e in ENG_NAMES}
        dsem = {}
        for key in self.dma_cnt:
            dsem[key] = stack.enter_context(nc.semaphore("d_%s_%d" % key))
        block = stack.enter_context(nc.Block())
        sched = self

        def run(e, engh):
            waited = {}

            def wait(key, sem, val):
                if waited.get(key, 0) >= val:
                    return
                waited[key] = val
                engh.wait_ge(sem, val)

            for o in sched.ops[e]:
                for d in o.deps:
                    if d.is_dma:
                        wait(d.dsem, dsem[d.dsem], d.dval)
                    else:
                        wait(d.eng, esem[d.eng], d.sigval)
                if o.is_dma and o.prev_dma is not None:
                    wait(o.dsem, dsem[o.dsem], o.prev_dma.dval)
                ins = o.fn(engh)
                if o.is_dma:
                    ins.then_inc(dsem[o.dsem], 16)
                elif o.sig:
                    ins.then_inc(esem[e], 1)
            if e == "sp":
                for d in final_dmas:
                    wait(d.dsem, dsem[d.dsem], d.dval)
                for d in sched.dma_last.values():
                    wait(d.dsem, dsem[d.dsem], d.dval)

        @block.tensor
        def _(pe):
            run("pe", pe)

        @block.scalar
        def _(act):
            run("act", act)

        @block.vector
        def _(dve):
            run("dve", dve)

        @block.gpsimd
        def _(pool):
            run("pool", pool)

        @block.sync
        def _(sp):
            run("sp", sp)


class K:
    def __init__(self, nc, st):
        self.nc = nc
        self.st = st
        self.S = Sched(nc)
        self.nps = 0

    def sb(self, name, shape, dt):
        return self.st.enter_context(self.nc.sbuf_tensor(name, shape, dt))

    def vsb(self, name, shape, dt, nbuf=None):
        t = self.sb(name, shape, dt)
        return V(t, [Buf(name)])

    def mm(self, o, lhsT, rhs, start, stop):
        self.S.op("pe", lambda e: e.matmul(o.ap, lhsT=lhsT.ap, rhs=rhs.ap, start=start, stop=stop),
                  reads=[lhsT, rhs], writes=[o])

    def tr(self, o, i, ident):
        self.S.op("pe", lambda e: e.transpose(o.ap, i.ap, ident.ap), reads=[i, ident], writes=[o])

    def act(self, o, i, func, bias=None, scale=None, accum=None, eng="act"):
        kw = {}
        rd = [i]
        wr = [o]
        if bias is not None:
            if isinstance(bias, V):
                kw["bias"] = bias.ap
                rd.append(bias)
            else:
                kw["bias"] = bias
        if scale is not None:
            if isinstance(scale, V):
                kw["scale"] = scale.ap
                rd.append(scale)
            else:
                kw["scale"] = scale
        if accum is not None:
            kw["accum_out"] = accum.ap
            wr.append(accum)
        self.S.op("act", lambda e: e.activation(out=o.ap, in_=i.ap, func=func, **kw), reads=rd, writes=wr)

    def tt(self, o, a, b, op, eng="dve"):
        self.S.op(eng, lambda e: e.tensor_tensor(out=o.ap, in0=a.ap, in1=b.ap, op=op), reads=[a, b], writes=[o])

    def ts(self, o, a, s1, s2, op0, op1=None, eng="dve"):
        rd = [a]
        a1 = s1.ap if isinstance(s1, V) else s1
        a2 = s2.ap if isinstance(s2, V) else s2
        if isinstance(s1, V):
            rd.append(s1)
        if isinstance(s2, V):
            rd.append(s2)
        if op1 is None:
            self.S.op(eng, lambda e: e.tensor_scalar(out=o.ap, in0=a.ap, scalar1=a1, scalar2=None, op0=op0),
                      reads=rd, writes=[o])
        else:
            self.S.op(eng, lambda e: e.tensor_scalar(out=o.ap, in0=a.ap, scalar1=a1, scalar2=a2, op0=op0, op1=op1),
                      reads=rd, writes=[o])

    def stt(self, o, a, s, b, op0, op1, eng="dve"):
        rd = [a, b]
        a1 = s.ap if isinstance(s, V) else s
        if isinstance(s, V):
            rd.append(s)
        self.S.op(eng, lambda e: e.scalar_tensor_tensor(out=o.ap, in0=a.ap, scalar=a1, in1=b.ap, op0=op0, op1=op1),
                  reads=rd, writes=[o])

    def copy(self, o, i, eng="dve"):
        self.S.op(eng, lambda e: e.tensor_copy(out=o.ap, in_=i.ap), reads=[i], writes=[o])

    def recip(self, o, i):
        self.S.op("dve", lambda e: e.reciprocal(out=o.ap, in_=i.ap), reads=[i], writes=[o])

    def memset(self, o, val, eng="dve"):
        self.S.op(eng, lambda e: e.memset(o.ap, val), writes=[o])

    def dma(self, o, i, eng="sp", reads=(), writes=()):
        rd = list(reads)
        wr = list(writes)
        if isinstance(i, V):
            rd.append(i)
            iap = i.ap
        else:
            iap = i
        if isinstance(o, V):
            wr.append(o)
            oap = o.ap
        else:
            oap = o
        return self.S.op(eng, lambda e: e.dma_start(out=oap, in_=iap), reads=rd, writes=wr, dma=True)


def rope_tables():
    rows = SEQ // 64
    row_id = np.repeat(np.arange(rows), 64).astype(np.float32)
    col_id = np.tile(np.arange(64), rows).astype(np.float32)
    axis_dim = 32
    freqs = (np.float32(10000.0) ** (-np.arange(0, axis_dim, 2, dtype=np.float32) / np.float32(axis_dim))).astype(np.float32)
    ar = row_id[:, None] * freqs[None, :]
    ac = col_id[:, None] * freqs[None, :]
    C = np.zeros((64, SEQ), np.float32)
    Sn = np.zeros((64, SEQ), np.float32)
    for d in range(64):
        ang = ar if d < 32 else ac
        f = d % 16
        C[d] = np.cos(ang[:, f])
        Sn[d] = np.sin(ang[:, f])
    C2 = np.concatenate([C, C], axis=0)
    S2 = np.concatenate([Sn, Sn], axis=0)
    P = np.zeros((128, 128), np.float32)
    for m in range(128):
        if m % 32 < 16:
            P[m, m + 16] = -1.0
        else:
            P[m, m - 16] = 1.0
    blk = np.zeros((128, 128), np.float32)
    blk[:64, :64] = 1.0
    blk[64:, 64:] = 1.0
    return C2, S2, np.ascontiguousarray(P.T), blk


def pool_icnt():
    out = np.zeros((3, 4, 256), np.float32)
    for kind, (L, t0) in enumerate([(SEQ, 0), (SEQ, SEQ - 256), (CTX, 0)]):
        for g, w in enumerate((2, 4, 8, 16)):
            t = np.arange(t0, t0 + 256)
            start = np.clip(t - w // 2, 0, L)
            end = np.clip(t + (w - w // 2), 0, L)
            out[kind, g] = 1.0 / (end - start).astype(np.float32)
    return out


DBG = {"on": False, "outs": {}}


def build(layers=(0, 1, 2, 3)):
    nc = bass.Bass("TRN2", target_bir_lowering=False)
    DBG["outs"] = {}

    def din(name, shape, dt=F32):
        return nc.dram_tensor(name, list(shape), dt, kind="ExternalInput").ap()

    x_in = din("x", [NB, SEQ, D])
    cx_in = din("ctx", [NB, CTX, D])
    cT = din("cT", [128, 8, 3])
    w_mod = din("w_mod", [4, D, 3 * D])
    b_mod3 = din("b_mod3", [4, 3, 3 * D])
    ng_rep = din("ng_rep", [4, 128, D])
    ident_in = din("ident", [128, 128])
    if 0 in layers:
        cv_w_in = din("cv_w_in", [D, 3 * D])
        cv_dw = din("cv_dw", [128, 8, 31])
        cv_vec = din("cv_vec", [128, 3, 8])
        cv_w_out = din("cv_w_out", [D, D])
    if 1 in layers:
        pl_w_in = din("pl_w_in", [D, 2 * D])
        pl_w_grp = din("pl_w_grp", [4, 256, 256])
        pl_scale = din("pl_scale", [128, 8])
        pl_icnt = din("pl_icnt", [128, 3, 4, 256])
        pl_w_out = din("pl_w_out", [D, D])
    if 2 in layers:
        ml_w_in = din("ml_w_in", [D, 1728])
        ml_w_uq = din("ml_w_uq", [384, 1536])
        ml_w_ukv = din("ml_w_ukv", [256, 2048])
        ml_vec = din("ml_vec", [128, 9])
        ml_w_out = din("ml_w_out", [D, D])
        rp_c = din("rp_c", [128, SEQ])
        rp_s = din("rp_s", [128, SEQ])
        rp_pt = din("rp_pt", [128, 128])
        rp_blk = din("rp_blk", [128, 128])
    if 3 in layers:
        ch_w_in = din("ch_w_in", [D, 3 * D])
        ch_lng = din("ch_lng", [128, D])
        ch_lnb = din("ch_lnb", [128, D])
        ch_wsT = din("ch_wsT", [128, 8, 128])
        ch_bsb = din("ch_bsb", [128, 8, 256])
        ch_w_out = din("ch_w_out", [D, D])
    out = nc.dram_tensor("out", [NB, SEQ, D], F32, kind="ExternalOutput").ap()
    modrow = nc.dram_tensor("modrow", [4, 3, 3 * D], F32).ap()
    xs = [nc.dram_tensor("xs%d" % i, [NB, SEQ, D], F32).ap() for i in range(2)]
    cxs = [nc.dram_tensor("cxs%d" % i, [NB, CTX, D], F32).ap() for i in range(2)]
    modrow_b = Buf("modrow")
    xs_b = [[Buf() for _ in range(NB)] for _ in range(2)]
    cxs_b = [[Buf() for _ in range(NB)] for _ in range(2)]
    NK = SEQ + CTX
    if 2 in layers:
        K_d = nc.dram_tensor("K_d", [NB, 8, 128, NK], BF16).ap()
        KR_d = nc.dram_tensor("KR_d", [NB, 64, NK], BF16).ap()
        V_d = nc.dram_tensor("V_d", [NB, NK // 128, 128, D], BF16).ap()
        kv_b = [Buf("kv%d" % i) for i in range(NB)]

    with ExitStack() as st:
        k = K(nc, st)
        S = k.S
        ident_f = k.vsb("ident_f", [128, 128], F32)
        ident_b = k.vsb("ident_b", [128, 128], BF16)
        ones_b = k.vsb("ones_b", [128, 128], BF16)
        epsc = k.vsb("epsc", [128, 1], F32)
        k.dma(ident_f[:], ident_in)
        k.copy(ident_b[:], ident_f[:])
        k.memset(epsc[:], EPS)
        k.memset(ones_b[:], 1.0)
        psf = [V(st.enter_context(nc.psum_tensor("psf%d" % i, [128, 512], F32)), [Buf("psf%d" % i)]) for i in range(6)]
        psb = [V(st.enter_context(nc.psum_tensor("psb%d" % i, [128, 1024], BF16)), [Buf("psb%d" % i)]) for i in range(2)]
        cnt = {"f": 0, "b": 0}
        pinned = set()

        def bank(pin=False):
            while True:
                cnt["f"] += 1
                i = cnt["f"] % 6
                if i not in pinned:
                    break
            if pin:
                pinned.add(i)
            return psf[i]

        def bankb():
            cnt["b"] += 1
            return psb[cnt["b"] % 2]

        w_in_t = k.sb("w_in", [128, 8, 3 * D], BF16)
        w_out_t = k.sb("w_out", [128, 8, D], BF16)
        w_in = V(w_in_t, [Buf("w_in")])
        w_out = V(w_out_t, [Buf("w_out")])
        G = [k.vsb("G%d" % i, [128, D], F32) for i in range(2)]
        SH = [k.vsb("SH%d" % i, [128, D], F32) for i in range(2)]
        GT = [k.vsb("GT%d" % i, [128, D], F32) for i in range(2)]
        xt = k.vsb("xt", [128, 3, D], F32)
        xr = [k.vsb("xr%d" % i, [128, 2, D], F32) for i in range(2)]
        tmpf = [k.vsb("tmpf%d" % i, [128, D], F32) for i in range(2)]
        hb = [k.vsb("hb%d" % i, [128, D], BF16) for i in range(3)]
        HW = 288
        htile_t = k.sb("htile", [128, 8, HW], BF16)
        htile = [V(htile_t[:, c, :], [Buf("h%d" % c)]) for c in range(8)]
        hall_bufs = [b for h in htile for b in h.bufs]
        st1 = k.vsb("st1", [128, 8], F32)
        st2 = k.vsb("st2", [128, 8], F32)
        junk = k.vsb("junk", [128, D], BF16)
        def chunked(name, width, dt=BF16, kk_=None):
            t = (kk_ or k).sb(name, [128, 8, width], dt)
            return t, [V(t[:, c, :], [Buf("%s%d" % (name, c))]) for c in range(8)]
        sg_t, sg = chunked("sg", 272)
        ya_t, ya = chunked("ya", 288)
        yb_t, yb = chunked("yb", 256)

        with ExitStack() as pst:
            kp = K(nc, pst)
            kp.S = S
            cTs = kp.vsb("cTs", [128, 8, 3], F32)
            sTs = kp.vsb("sTs", [128, 8, 3], F32)
            k.dma(cTs[:], cT)
            k.act(sTs[:], cTs[:], AF.Silu)
            NWB = 3
            CGW = 384
            wmb = [kp.vsb("wmb%d" % i, [128, 8, CGW], F32) for i in range(NWB)]
            bm3 = [kp.vsb("bm3%d" % i, [3, CGW], F32) for i in range(NWB)]
            mrow = [kp.vsb("mrow%d" % i, [3, CGW], F32) for i in range(NWB)]
            gi = 0
            for l in layers:
                for cg in range(3 * D // CGW):
                    wb_ = wmb[gi % NWB]
                    k.dma(bm3[gi % NWB][:], b_mod3[l, :, cg * CGW:(cg + 1) * CGW], eng="act")
                    k.dma(wb_[:], w_mod[l, :, cg * CGW:(cg + 1) * CGW].rearrange("(kc p) n -> p kc n", p=128), eng="act")
                    ps = bank()
                    for kc in range(8):
                        k.mm(ps[0:3, 0:CGW], sTs[:, kc, :], wb_[:, kc, :], kc == 0, kc == 7)
                    mr = mrow[gi % NWB]
                    k.tt(mr[:], ps[0:3, 0:CGW], bm3[gi % NWB][:], ALU.add)
                    k.dma(modrow[l, :, cg * CGW:(cg + 1) * CGW], mr[:], writes=[modrow_b])
                    gi += 1
            S.barrier()

        def load_mod(l, slot, r):
            k.dma(SH[slot][:], modrow[l, r:r + 1, 0:D].partition_broadcast(128), reads=[modrow_b])
            k.dma(G[slot][:], modrow[l, r:r + 1, D:2 * D].partition_broadcast(128), reads=[modrow_b])
            k.dma(GT[slot][:], modrow[l, r:r + 1, 2 * D:3 * D].partition_broadcast(128), reads=[modrow_b])
            k.dma(tmpf[1][:], ng_rep[l])
            k.stt(G[slot][:], G[slot][:], 1.0, tmpf[1][:], ALU.add, ALU.mult)

        def load_w(dst_v, dst_ap, src_ap):
            k.dma(V(dst_ap, dst_v.bufs), src_ap, eng="pool")

        RSTD = {"mode": "sqrt"}

        def rstd_small(dst, src, n):
            if RSTD["mode"] == "sqrt":
                k.act(dst[:, 0:n], src[:, 0:n], AF.Sqrt, bias=epsc[:])
                k.recip(dst[:, 0:n], dst[:, 0:n])
            else:
                k.act(dst[:, 0:n], src[:, 0:n], AF.Ln, bias=epsc[:])
                k.act(dst[:, 0:n], dst[:, 0:n], AF.Exp, scale=-0.5)

        def front_a(src_ap, src_bufs, slot, ntok):
            nfull = ntok // 128
            rem = ntok % 128
            if nfull:
                k.dma(xt[:, 0:nfull, :], src_ap[0:nfull * 128, :].rearrange("(j p) d -> p j d", p=128), reads=src_bufs)
            if rem:
                k.dma(xt[0:rem, nfull, :], src_ap[nfull * 128:ntok, :], reads=src_bufs)
            blks = [(j, 128) for j in range(nfull)] + ([(nfull, rem)] if rem else [])
            nb = len(blks)
            k.memset(st1[:, 0:nb], 0.0)
            for j, nt in blks:
                k.act(junk[0:nt, :], xt[0:nt, j, :], AF.Square, accum=st1[0:nt, j:j + 1])
            k.ts(st2[:, 0:nb], st1[:, 0:nb], 1.0 / D, None, ALU.mult)
            rstd_small(st2, st2, nb)
            for j, nt in blks:
                tf = tmpf[j % 2]
                k.stt(tf[0:nt, :], xt[0:nt, j, :], st2[0:nt, j:j + 1], G[slot][0:nt, :], ALU.mult, ALU.mult)
                k.tt(hb[j][0:nt, :], tf[0:nt, :], SH[slot][0:nt, :], ALU.add)
            return blks

        def front_b(blks, col0=0):
            for j, nt in blks:
                pb = bankb()
                for c in range(8):
                    k.tr(pb[:, c * 128:c * 128 + nt], hb[j][0:nt, c * 128:(c + 1) * 128], ident_b[0:nt, 0:nt])
                dst = V(htile_t[:, :, col0 + j * 128: col0 + j * 128 + nt], hall_bufs)
                srcv = V(pb.ap.rearrange("p (c t) -> p c t", c=8)[:, :, 0:nt], pb.bufs)
                k.act(dst, srcv, AF.Copy)

        xr_rr = {"i": 0}

        def xr_load(src_ap, src_bufs):
            xr_rr["i"] += 1
            xv = xr[xr_rr["i"] % 2]
            k.dma(xv[:, :, :], src_ap.rearrange("(j p) d -> p j d", p=128), reads=src_bufs)
            return xv

        def out_proj_residual(y_chunks, slot, xv, dst_ap, dst_buf):
            for j in range(2):
                for hf in range(2):
                    ps = bank()
                    for kc in range(8):
                        k.mm(ps[:, :], y_chunks[kc][:, j * 128:(j + 1) * 128], w_out[:, kc, hf * 512:(hf + 1) * 512],
                             kc == 0, kc == 7)
                    tf = tmpf[(2 * j + hf) % 2]
                    k.tt(tf[:, 0:512], ps[:, :], GT[slot][:, hf * 512:(hf + 1) * 512], ALU.mult)
                    k.tt(xv[:, j, hf * 512:(hf + 1) * 512], tf[:, 0:512], xv[:, j, hf * 512:(hf + 1) * 512], ALU.add)
            return k.dma(dst_ap.rearrange("(j p) d -> p j d", p=128), xv[:, :, :], writes=[dst_buf])

        def run_tiles(tiles, prep, a_phase, b_phase):
            ctx = prep(tiles[0])
            for i, desc in enumerate(tiles):
                front_b(ctx)
                a_phase(desc)
                box = {}

                def mid(i=i, box=box):
                    if "n" not in box:
                        box["n"] = prep(tiles[i + 1]) if i + 1 < len(tiles) else None

                b_phase(desc, mid)
                mid()
                ctx = box["n"]

        def src_of(li, seq, is_ctx):
            if li == 0:
                return (cx_in[seq] if is_ctx else x_in[seq]), []
            return (cxs[(li - 1) % 2][seq] if is_ctx else xs[(li - 1) % 2][seq]), \
                   [cxs_b[(li - 1) % 2][seq] if is_ctx else xs_b[(li - 1) % 2][seq]]

        outb = Buf("outb")

        def dst_of(li, seq, is_ctx):
            if (not is_ctx) and li == len(layers) - 1:
                return out[seq], outb
            return (cxs[li % 2][seq] if is_ctx else xs[li % 2][seq]), \
                   (cxs_b[li % 2][seq] if is_ctx else xs_b[li % 2][seq])

        final_dmas = []

        def dbg(name, v, shape, dt):
            if not DBG["on"] or name in DBG["outs"]:
                return
            o_ = nc.dram_tensor("dbg_" + name, list(shape), dt, kind="ExternalOutput").ap()
            DBG["outs"][name] = 1
            final_dmas.append(k.dma(o_, v))

        def load_w_rows(dst_v, dst_t, src, ncols):
            for kc in range(8):
                load_w(dst_v, dst_t[:, kc, 0:ncols], src[kc * 128:(kc + 1) * 128, :])

        def rstd_from_ps(dst, ps_v, n, scale, npart=128):
            if RSTD["mode"] == "sqrt":
                k.act(dst[:, 0:n], ps_v[:, 0:n], AF.Sqrt, bias=epsc[0:npart, :], scale=scale)
                k.recip(dst[:, 0:n], dst[:, 0:n])
            else:
                k.act(dst[:, 0:n], ps_v[:, 0:n], AF.Ln, bias=epsc[0:npart, :], scale=scale)
                k.act(dst[:, 0:n], dst[:, 0:n], AF.Exp, scale=-0.5)

        def interleave(n, stage_a, stage_b):
            pend = {0: stage_a(0)}
            for i in range(n):
                if i + 1 < n:
                    pend[i + 1] = stage_a(i + 1)
                stage_b(i, pend.pop(i))

        class Rot:
            def __init__(self, items):
                self.items = items
                self.i = 0

            def __call__(self):
                self.i += 1
                return self.items[self.i % len(self.items)]

        def seg_tiles(li, with_ctx, slot_of):
            tiles = []
            for seq in range(NB):
                for is_ctx in ((False, True) if with_ctx else (False,)):
                    L = CTX if is_ctx else SEQ
                    sap, sbufs = src_of(li, seq, is_ctx)
                    dap, dbuf = dst_of(li, seq, is_ctx)
                    for t in range(L // TT):
                        tiles.append(dict(seq=seq, is_ctx=is_ctx, t=t, ntile=L // TT, L=L, slot=slot_of(seq, is_ctx),
                                          sap=sap, sbufs=sbufs, dap=dap, dbuf=dbuf, li=li))
            return tiles

        def layer0(li, kl):
            l = 0
            RSTD["mode"] = "sqrt"
            load_w_rows(w_in, w_in_t, cv_w_in, 3 * D)
            load_w_rows(w_out, w_out_t, cv_w_out, D)
            dwt = kl.vsb("l0_dw", [128, 8, 31], F32)
            vec = kl.vsb("l0_vec", [128, 3, 8], F32)
            k.dma(dwt[:], cv_dw)
            dwb = kl.vsb("l0_dwb", [128, 8, 31], BF16)
            k.copy(dwb[:], dwt[:])
            k.dma(vec[:], cv_vec)
            dg_t = [kl.sb("l0_dg%d" % i, [128, 31, 128], BF16) for i in range(2)]
            dg = [V(t, [Buf("dg")]) for t in dg_t]
            dgp = [V(t, [Buf("dgp")]) for t in dg_t]
            sbt = Rot([kl.vsb("l0_sb%d" % i, [128, 272], BF16) for i in range(3)])
            z_t, z = chunked("l0_z", 256, kk_=kl)
            zq_t, zq = chunked("l0_zq", 256, kk_=kl)
            mean = kl.vsb("l0_mean", [128, 256], F32)
            rs = kl.vsb("l0_rs", [128, 256], F32)
            mr_ = kl.vsb("l0_mr", [128, 256], F32)
            t1 = Rot([kl.vsb("l0_t1%d" % i, [128, 256], F32) for i in range(3)])
            su = Rot([kl.vsb("l0_su%d" % i, [128, 256], BF16) for i in range(3)])
            yall = V(ya_t, [b for y in ya for b in y.bufs])
            sgall = V(sg_t, [b for y in sg for b in y.bufs])

            def gen_diag(c):
                dgc = dg[c % 2]
                S.op("dve", lambda e, o_=dgc, c_=c: e.tensor_tensor(
                    out=o_.ap[:, :, :],
                    in0=ident_b.ap[:].unsqueeze(1).to_broadcast([128, 31, 128]),
                    in1=dwb.ap[:, c_, :].unsqueeze(2).to_broadcast([128, 31, 128]),
                    op=ALU.mult), reads=[ident_b, dwb], writes=[dgc])

            def rng(d):
                t0 = d["t"] * TT
                first = d["t"] == 0
                a0 = 0 if first else t0 + 15
                a1 = min(t0 + TT + 15, d["L"])
                return t0, a0, a1

            def prep(d):
                if d["t"] == 0:
                    load_mod(l, d["slot"], 2 if d["is_ctx"] else d["seq"])
                t0, a0, a1 = rng(d)
                return front_a(d["sap"][a0:a1, :], d["sbufs"], d["slot"], a1 - a0)

            def a_phase(d):
                t0, a0, a1 = rng(d)
                n = a1 - a0
                first = d["t"] == 0
                last = d["t"] == d["ntile"] - 1
                yc0 = a0 - t0 + 15
                sc0 = a0 - t0
                d["xv"] = xr_load(d["sap"][t0:t0 + TT, :], d["sbufs"])
                if first:
                    k.memset(yall[:, :, 0:15], 0.0)
                if last:
                    k.memset(yall[:, :, 271:286], 0.0)
                for j in range(8):
                    psa = bank()
                    for kc in range(8):
                        k.mm(psa[:, 0:n], w_in[:, kc, j * 128:(j + 1) * 128], htile[kc][:, 0:n], kc == 0, kc == 7)
                    psb_ = bank()
                    for kc in range(8):
                        k.mm(psb_[:, 0:n], w_in[:, kc, D + j * 128:D + (j + 1) * 128], htile[kc][:, 0:n], kc == 0, kc == 7)
                    sbj = sbt()
                    k.act(sbj[:, 0:n], psb_[:, 0:n], AF.Sigmoid)
                    k.tt(ya[j][:, yc0:yc0 + n], psa[:, 0:n], sbj[:, 0:n], ALU.mult)
                for j in range(8):
                    psg = bank()
                    for kc in range(8):
                        k.mm(psg[:, 0:n], w_in[:, kc, 2 * D + j * 128:2 * D + (j + 1) * 128], htile[kc][:, 0:n], kc == 0, kc == 7)
                    k.act(sg[j][:, sc0:sc0 + n], psg[:, 0:n], AF.Silu)
                gen_diag(0)
                gen_diag(1)
                dbg("G0", G[0][:, :], [128, D], F32)
                dbg("SH0", SH[0][:, :], [128, D], F32)
                dbg("GT0", GT[0][:, :], [128, D], F32)
                dbg("h", V(htile_t[:, :, :], hall_bufs), [128, 8, HW], BF16)
                dbg("ya", yall[:, :, :], [128, 8, 288], BF16)
                dbg("sg", sgall[:, :, :], [128, 8, 272], BF16)

            def b_phase(d, mid):
                t0, a0, a1 = rng(d)
                last = d["t"] == d["ntile"] - 1
                slot = d["slot"]
                for c in range(8):
                    dgc = dg[c % 2]
                    ps = bank()
                    for kk in range(31):
                        k.mm(ps[:, 0:TT], dgc[:, kk, :], ya[c][:, kk:kk + TT], kk == 0, kk == 30)
                    if c + 2 < 8:
                        gen_diag(c + 2)
                    k.act(z[c][:], ps[:, 0:TT], AF.Identity, bias=vec[:, 0, c:c + 1])
                    k.act(zq[c][:], ps[:, 0:TT], AF.Square, bias=vec[:, 0, c:c + 1])
                psm = bank()
                for c in range(8):
                    k.mm(psm[:, 0:TT], ones_b[:], z[c][:], c == 0, c == 7)
                psq = bank()
                for c in range(8):
                    k.mm(psq[:, 0:TT], ones_b[:], zq[c][:], c == 0, c == 7)
                k.ts(mean[:], psm[:, 0:TT], 1.0 / D, None, ALU.mult)
                k.tt(mr_[:], mean[:], mean[:], ALU.mult)
                k.stt(rs[:], psq[:, 0:TT], 1.0 / D, mr_[:], ALU.mult, ALU.subtract)
                k.act(rs[:], rs[:], AF.Sqrt, bias=epsc[:])
                k.recip(rs[:], rs[:])
                k.tt(mr_[:], mean[:], rs[:], ALU.mult)
                for c in range(8):
                    tc = t1()
                    k.tt(tc[:], z[c][:], rs[:], ALU.mult)
                    k.tt(tc[:], tc[:], mr_[:], ALU.subtract)
                    sc_ = su()
                    k.act(sc_[:], tc[:], AF.Silu, bias=vec[:, 2, c:c + 1], scale=vec[:, 1, c:c + 1])
                    k.tt(yb[c][:], sc_[:], sg[c][:, 0:TT], ALU.mult)
                dbg("z", V(z_t[:, :, :], [b_ for y_ in z for b_ in y_.bufs]), [128, 8, 256], BF16)
                dbg("yb", V(yb_t[:, :, :], [b_ for y_ in yb for b_ in y_.bufs]), [128, 8, 256], BF16)
                dbg("rs", rs[:, :], [128, 256], F32)
                dbg("mean", mean[:, :], [128, 256], F32)
                mid()
                dd = out_proj_residual(yb, slot, d["xv"], d["dap"][t0:t0 + TT, :], d["dbuf"])
                if (not d["is_ctx"]) and li == len(layers) - 1:
                    final_dmas.append(dd)
                if not last:
                    k.copy(yall[:, :, 0:30], yall[:, :, 256:286])
                    k.copy(sgall[:, :, 0:15], sgall[:, :, 256:271])

            run_tiles(seg_tiles(li, True, lambda s, c: 1 if c else 0), prep, a_phase, b_phase)

        def layer1(li, kl):
            l = 1
            RSTD["mode"] = "sqrt"
            load_w_rows(w_in, w_in_t, pl_w_in, 2 * D)
            load_w_rows(w_out, w_out_t, pl_w_out, D)
            w_x_t = kl.sb("l1_w_x", [128, 2048], BF16)
            w_x = V(w_x_t, [Buf("w_x")])
            wg = V(w_x_t[:, :].rearrange("p (g c n) -> p g c n", g=4, c=2), w_x.bufs)
            load_w(w_x, wg.ap, pl_w_grp.rearrange("g (c p) n -> p g c n", p=128))
            scl = kl.vsb("l1_scl", [128, 8], F32)
            icn = kl.vsb("l1_icn", [128, 3, 4, 256], F32)
            k.dma(scl[:], pl_scale)
            k.dma(icn[:], pl_icnt)
            vt_t = kl.sb("l1_v", [128, 8, 272], F32)
            vv = [V(vt_t[:, c, :], [Buf("l1v%d" % c)]) for c in range(8)]
            vall = V(vt_t, [b for y in vv for b in y.bufs])
            sgall = V(sg_t, [b for y in sg for b in y.bufs])
            p2r = Rot([kl.vsb("l1_p2%d" % i, [128, 272], F32) for i in range(2)])
            p4r = Rot([kl.vsb("l1_p4%d" % i, [128, 272], F32) for i in range(2)])
            p8r = Rot([kl.vsb("l1_p8%d" % i, [128, 272], F32) for i in range(2)])
            winr = Rot([kl.vsb("l1_win%d" % i, [128, 256], F32) for i in range(3)])

            def rng(d):
                t0 = d["t"] * TT
                first = d["t"] == 0
                a0 = 0 if first else t0 + 8
                a1 = min(t0 + TT + 8, d["L"])
                return t0, a0, a1

            def prep(d):
                if d["t"] == 0:
                    load_mod(l, d["slot"], 2 if d["is_ctx"] else d["seq"])
                t0, a0, a1 = rng(d)
                return front_a(d["sap"][a0:a1, :], d["sbufs"], d["slot"], a1 - a0)

            def a_phase(d):
                t0, a0, a1 = rng(d)
                n = a1 - a0
                first = d["t"] == 0
                last = d["t"] == d["ntile"] - 1
                vc0 = a0 - t0 + 8
                sc0 = a0 - t0
                d["xv"] = xr_load(d["sap"][t0:t0 + TT, :], d["sbufs"])
                if first:
                    k.memset(vall[:, :, 0:8], 0.0)
                if last:
                    k.memset(vall[:, :, 264:272], 0.0)
                for j in range(8):
                    psv = bank()
                    for kc in range(8):
                        k.mm(psv[:, 0:n], w_in[:, kc, j * 128:(j + 1) * 128], htile[kc][:, 0:n], kc == 0, kc == 7)
                    k.act(vv[j][:, vc0:vc0 + n], psv[:, 0:n], AF.Copy)
                for j in range(8):
                    psg = bank()
                    for kc in range(8):
                        k.mm(psg[:, 0:n], w_in[:, kc, D + j * 128:D + (j + 1) * 128], htile[kc][:, 0:n], kc == 0, kc == 7)
                    k.act(sg[j][:, sc0:sc0 + n], psg[:, 0:n], AF.Silu)

            def b_phase(d, mid):
                t0, a0, a1 = rng(d)
                first = d["t"] == 0
                last = d["t"] == d["ntile"] - 1
                slot = d["slot"]
                kind = None
                if first and last:
                    kind = 2
                elif first:
                    kind = 0
                elif last:
                    kind = 1
                for c in range(8):
                    g = c // 2
                    v = vv[c]
                    win = winr()
                    if g == 0:
                        k.tt(win[:], v[:, 7:263], v[:, 8:264], ALU.add)
                    else:
                        p2 = p2r()
                        k.tt(p2[:, 0:271], v[:, 0:271], v[:, 1:272], ALU.add)
                        if g == 1:
                            k.tt(win[:], p2[:, 6:262], p2[:, 8:264], ALU.add)
                        else:
                            p4 = p4r()
                            k.tt(p4[:, 0:269], p2[:, 0:269], p2[:, 2:271], ALU.add)
                            if g == 2:
                                k.tt(win[:], p4[:, 4:260], p4[:, 8:264], ALU.add)
                            else:
                                p8 = p8r()
                                k.tt(p8[:, 0:265], p4[:, 0:265], p4[:, 4:269], ALU.add)
                                k.tt(win[:], p8[:, 0:256], p8[:, 8:264], ALU.add)
                    if kind is None:
                        k.stt(ya[c][:, 0:TT], win[:], 1.0 / (2, 4, 8, 16)[g], v[:, 8:264], ALU.mult, ALU.subtract)
                    else:
                        k.tt(win[:], win[:], icn[:, kind, g, :], ALU.mult)
                        k.tt(ya[c][:, 0:TT], win[:], v[:, 8:264], ALU.subtract)
                for c in range(8):
                    g, oc2 = c // 2, c % 2
                    ps = bank()
                    for c2 in range(2):
                        k.mm(ps[:, 0:TT], wg[:, g, c2, oc2 * 128:(oc2 + 1) * 128], ya[2 * g + c2][:, 0:TT], c2 == 0, c2 == 1)
                    k.stt(yb[c][:], ps[:, 0:TT], scl[:, c:c + 1], sg[c][:, 0:TT], ALU.mult, ALU.mult)
                mid()
                dd = out_proj_residual(yb, slot, d["xv"], d["dap"][t0:t0 + TT, :], d["dbuf"])
                if (not d["is_ctx"]) and li == len(layers) - 1:
                    final_dmas.append(dd)
                if not last:
                    k.copy(vall[:, :, 0:16], vall[:, :, 256:272])
                    k.copy(sgall[:, :, 0:8], sgall[:, :, 256:264])

            run_tiles(seg_tiles(li, True, lambda s, c: 1 if c else 0), prep, a_phase, b_phase)

        def layer2(li, kl):
            l = 2
            RSTD["mode"] = "lnexp"
            flat = w_in_t[:, :, :].rearrange("p a b -> p (a b)")
            mlw_t = flat[:, 0:8 * 1728].rearrange("p (c n) -> p c n", c=8)
            mlw = V(mlw_t, w_in.bufs)
            for kc in range(8):
                load_w(w_in, mlw_t[:, kc, :], ml_w_in[kc * 128:(kc + 1) * 128, :])
            wukv_t = flat[:, 8 * 1728 + 3 * 1536:8 * 1728 + 3 * 1536 + 2 * 2048].rearrange("p (c n) -> p c n", c=2)
            wukv = V(wukv_t, w_in.bufs)
            wuqn_t = flat[:, 8 * 1728:8 * 1728 + 3 * 1024].rearrange("p (c n) -> p c n", c=3)
            wuqr_t = flat[:, 8 * 1728 + 3 * 1024:8 * 1728 + 3 * 1536].rearrange("p (c n) -> p c n", c=3)
            wuqn = V(wuqn_t, w_in.bufs)
            wuqr = V(wuqr_t, w_in.bufs)
            for kc in range(2):
                load_w(w_in, wukv_t[:, kc, :], ml_w_ukv[kc * 128:(kc + 1) * 128, :])
            for kc in range(3):
                srcv = ml_w_uq[kc * 128:(kc + 1) * 128, :].rearrange("p (h x) -> p h x", h=8)
                load_w(w_in, wuqn_t[:, kc, :].rearrange("p (h x) -> p h x", h=8), srcv[:, :, 0:128])
                load_w(w_in, wuqr_t[:, kc, :].rearrange("p (h x) -> p h x", h=8), srcv[:, :, 128:192])
            load_w_rows(w_out, w_out_t, ml_w_out, D)
            vec = kl.vsb("l2_vec", [128, 9], F32)
            k.dma(vec[:], ml_vec)
            pt_f = kl.vsb("l2_ptf", [128, 128], F32)
            blk_f = kl.vsb("l2_blkf", [128, 128], F32)
            pt_b = kl.vsb("l2_ptb", [128, 128], BF16)
            blk_b = kl.vsb("l2_blkb", [128, 128], BF16)
            k.dma(pt_f[:], rp_pt)
            k.dma(blk_f[:], rp_blk)
            k.copy(pt_b[:], pt_f[:])
            k.copy(blk_b[:], blk_f[:])
            rcs = [kl.vsb("l2_rc%d" % i, [128, 256], F32) for i in range(2)]
            rsns = [kl.vsb("l2_rsn%d" % i, [128, 256], F32) for i in range(2)]
            sqr = Rot([kl.vsb("l2_sq%d" % i, [128, 256], BF16) for i in range(5)])
            rrr = Rot([kl.vsb("l2_rr%d" % i, [128, 256], F32) for i in range(2)])
            tfr = Rot([kl.vsb("l2_tf%d" % i, [128, 256], F32) for i in range(2)])
            cn = [kl.vsb("l2_cn%d" % i, [128, 256], BF16) for i in range(3)]
            kstr = Rot([kl.vsb("l2_kst%d" % i, [128, 256], BF16) for i in range(2)])
            krbr = Rot([kl.vsb("l2_krb%d" % i, [128, 256], BF16) for i in range(2)])
            vstr = Rot([kl.vsb("l2_vst%d" % i, [128, D], BF16) for i in range(1)])
            kr_s = kl.vsb("l2_krs", [128, NK], BF16)
            kh_s = [kl.vsb("l2_kh%d" % i, [128, NK], BF16) for i in range(2)]
            vh_s = [kl.vsb("l2_vh%d" % i, [128, NK // 128, 132], BF16) for i in range(2)]
            for i in range(2):
                k.memset(vh_s[i][:, :, 128:129], 1.0)
            qn = [kl.vsb("l2_qn%d" % i, [128, 256], BF16) for i in range(8)]
            qr = [kl.vsb("l2_qr%d" % i, [128, 256], BF16) for i in range(8)]
            for i in range(8):
                k.memset(qr[i][:], 0.0)
            pTr = Rot([kl.vsb("l2_pT%d" % i, [128, 512], BF16) for i in range(5)])
            otm = [kl.vsb("l2_otm%d" % i, [128, D], BF16) for i in range(2)]
            rsum = kl.vsb("l2_rsum", [128, 4], F32)
            SCALE = float((128 + 64) ** -0.5)
            nkc = NK // 128
            rci = {"i": 0}

            def load_rope(t0):
                rci["i"] += 1
                rc, rsn = rcs[rci["i"] % 2], rsns[rci["i"] % 2]
                k.dma(rc[:], rp_c[:, t0:t0 + TT])
                k.dma(rsn[:], rp_s[:, t0:t0 + TT])
                return rc, rsn

            for seq in range(NB):
                ktiles = []
                for is_ctx in (False, True):
                    L = CTX if is_ctx else SEQ
                    sap, sbufs = src_of(li, seq, is_ctx)
                    for t in range(L // TT):
                        ktiles.append(dict(seq=seq, is_ctx=is_ctx, t=t, slot=1 if is_ctx else 0, sap=sap, sbufs=sbufs,
                                           koff=SEQ if is_ctx else 0))

                def kprep(d):
                    if d["t"] == 0:
                        load_mod(l, d["slot"], 2 if d["is_ctx"] else seq)
                    t0 = d["t"] * TT
                    return front_a(d["sap"][t0:t0 + TT, :], d["sbufs"], d["slot"], TT)

                def k_a(d):
                    t0 = d["t"] * TT
                    koff = d["koff"]
                    is_ctx = d["is_ctx"]
                    if not is_ctx:
                        rc, rsn = load_rope(t0)
                    pc = []
                    sqs = []
                    for j in range(2):
                        ps = bank()
                        for kc in range(8):
                            k.mm(ps[:, 0:TT], mlw[:, kc, j * 128:(j + 1) * 128], htile[kc][:, 0:TT], kc == 0, kc == 7)
                        sq = sqr()
                        k.act(sq[:], ps[:, 0:TT], AF.Square)
                        pc.append(ps)
                        sqs.append(sq)
                    psk = bank()
                    for kc in range(8):
                        k.mm(psk[0:64, 0:TT], mlw[:, kc, 256:320], htile[kc][:, 0:TT], kc == 0, kc == 7)
                    sqk = sqr()
                    k.act(sqk[0:64, :], psk[0:64, 0:TT], AF.Square)
                    pss = bank()
                    for j in range(2):
                        k.mm(pss[:, 0:TT], ones_b[:], sqs[j][:], j == 0, j == 1)
                    rr = rrr()
                    rstd_from_ps(rr, pss, TT, 1.0 / 256)
                    for j in range(2):
                        k.stt(cn[j][:], pc[j][:, 0:TT], vec[:, 3 + j:4 + j], rr[:], ALU.mult, ALU.mult)
                    pss2 = bank()
                    k.mm(pss2[0:64, 0:TT], ones_b[0:64, 0:64], sqk[0:64, :], True, True)
                    rr2 = rrr()
                    rstd_from_ps(rr2[0:64], pss2[0:64], TT, 1.0 / 64, npart=64)
                    tfa = tfr()
                    k.stt(tfa[0:64, :], psk[0:64, 0:TT], vec[0:64, 8:9], rr2[0:64, :], ALU.mult, ALU.mult)
                    krb = krbr()
                    if is_ctx:
                        k.copy(krb[0:64, :], tfa[0:64, :])
                    else:
                        sqc = sqr()
                        k.copy(sqc[0:64, :], tfa[0:64, :])
                        psr = bank()
                        k.mm(psr[0:64, 0:TT], pt_b[0:64, 0:64], sqc[0:64, :], True, True)
                        tfb = tfr()
                        k.tt(tfb[0:64, :], psr[0:64, 0:TT], rsn[0:64, :], ALU.mult)
                        k.tt(tfa[0:64, :], tfa[0:64, :], rc[0:64, :], ALU.mult)
                        k.tt(krb[0:64, :], tfa[0:64, :], tfb[0:64, :], ALU.add)
                    k.dma(KR_d[seq, :, koff + t0:koff + t0 + TT], krb[0:64, :], writes=[kv_b[seq]])

                def k_b(d, mid):
                    mid()
                    t0 = d["t"] * TT
                    koff = d["koff"]
                    def kA(h):
                        ps = bank()
                        for kc in range(2):
                            k.mm(ps[:, 0:TT], wukv[:, kc, h * 256:h * 256 + 128], cn[kc][:], kc == 0, kc == 1)
                        sq = sqr()
                        k.act(sq[:], ps[:, 0:TT], AF.Square)
                        return ps, sq

                    def kB(h, st_):
                        ps, sq = st_
                        pss = bank()
                        k.mm(pss[:, 0:TT], ones_b[:], sq[:], True, True)
                        rr = rrr()
                        rstd_from_ps(rr, pss, TT, 1.0 / 128)
                        ks = kstr()
                        k.stt(ks[:], ps[:, 0:TT], vec[:, 6:7], rr[:], ALU.mult, ALU.mult)
                        k.dma(K_d[seq, h, :, koff + t0:koff + t0 + TT], ks[:], writes=[kv_b[seq]])

                    interleave(8, kA, kB)
                    rhs_ap = wukv_t.rearrange("p c (h x) -> p c h x", h=8)
                    for j in range(2):
                        vs = vstr()
                        for hf in range(2):
                            ps = bank()
                            for kc in range(2):
                                rv = V(rhs_ap[:, kc, hf * 4:(hf + 1) * 4, 128:256], wukv.bufs)
                                k.mm(ps[:, :], cn[kc][:, j * 128:(j + 1) * 128], rv, kc == 0, kc == 1)
                            k.act(vs[:, hf * 512:(hf + 1) * 512], ps[:, :], AF.Copy)
                        k.dma(V_d[seq, (koff + t0) // 128 + j, :, :], vs[:], writes=[kv_b[seq]])

                run_tiles(ktiles, kprep, k_a, k_b)

                sap, sbufs = src_of(li, seq, False)
                dap, dbuf = dst_of(li, seq, False)
                k.dma(kr_s[0:64, :], KR_d[seq], reads=[kv_b[seq]])
                k.dma(kr_s[64:128, :], KR_d[seq], reads=[kv_b[seq]])
                qtiles = [dict(t=t) for t in range(SEQ // TT)]

                def qprep(d):
                    t0 = d["t"] * TT
                    return front_a(sap[t0:t0 + TT, :], sbufs, 0, TT)

                def q_a(d):
                    t0 = d["t"] * TT
                    d["xv"] = xr_load(sap[t0:t0 + TT, :], sbufs)
                    rc, rsn = load_rope(t0)
                    pc = []
                    sqs = []
                    for j in range(3):
                        ps = bank()
                        for kc in range(8):
                            k.mm(ps[:, 0:TT], mlw[:, kc, 320 + j * 128:320 + (j + 1) * 128], htile[kc][:, 0:TT], kc == 0, kc == 7)
                        sq = sqr()
                        k.act(sq[:], ps[:, 0:TT], AF.Square)
                        pc.append(ps)
                        sqs.append(sq)
                    pss = bank()
                    for j in range(3):
                        k.mm(pss[:, 0:TT], ones_b[:], sqs[j][:], j == 0, j == 2)
                    rr = rrr()
                    rstd_from_ps(rr, pss, TT, 1.0 / 384)
                    for j in range(3):
                        k.stt(cn[j][:], pc[j][:, 0:TT], vec[:, j:j + 1], rr[:], ALU.mult, ALU.mult)
                    def qA(h):
                        ps = bank()
                        for kc in range(3):
                            k.mm(ps[:, 0:TT], wuqn[:, kc, h * 128:(h + 1) * 128], cn[kc][:], kc == 0, kc == 2)
                        sq = sqr()
                        k.act(sq[:], ps[:, 0:TT], AF.Square)
                        return ps, sq

                    def qB(h, st_):
                        ps, sq = st_
                        pss = bank()
                        k.mm(pss[:, 0:TT], ones_b[:], sq[:], True, True)
                        rr = rrr()
                        rstd_from_ps(rr, pss, TT, 1.0 / 128)
                        k.stt(qn[h][:], ps[:, 0:TT], vec[:, 5:6], rr[:], ALU.mult, ALU.mult)

                    interleave(8, qA, qB)

                    def rA(pr):
                        ps = bank()
                        for kc in range(3):
                            k.mm(ps[:, 0:TT], wuqr[:, kc, pr * 128:(pr + 1) * 128], cn[kc][:], kc == 0, kc == 2)
                        sq = sqr()
                        k.act(sq[:], ps[:, 0:TT], AF.Square)
                        return ps, sq

                    for pr in range(4):
                        ps, sq = rA(pr)
                        pss = bank()
                        k.mm(pss[:, 0:TT], blk_b[:], sq[:], True, True)
                        rr = rrr()
                        rstd_from_ps(rr, pss, TT, 1.0 / 64)
                        tfa = tfr()
                        k.stt(tfa[:], ps[:, 0:TT], vec[:, 7:8], rr[:], ALU.mult, ALU.mult)
                        sqc = sqr()
                        k.copy(sqc[:], tfa[:])
                        psr = bank()
                        k.mm(psr[:, 0:TT], pt_b[:], sqc[:], True, True)
                        tfb = tfr()
                        k.tt(tfb[:], psr[:, 0:TT], rsn[:], ALU.mult)
                        k.tt(tfa[:], tfa[:], rc[:], ALU.mult)
                        k.tt(qr[2 * pr][0:64, :], tfa[0:64, :], tfb[0:64, :], ALU.add)
                        k.tt(qr[2 * pr + 1][64:128, :], tfa[64:128, :], tfb[64:128, :], ALU.add)
                    for j in range(8):
                        ps = bank()
                        for kc in range(8):
                            k.mm(ps[:, 0:TT], mlw[:, kc, 704 + j * 128:704 + (j + 1) * 128], htile[kc][:, 0:TT], kc == 0, kc == 7)
                        k.act(sg[j][:, 0:TT], ps[:, 0:TT], AF.Silu)

                def q_b(d, mid):
                    mid()
                    t0 = d["t"] * TT
                    for h in range(8):
                        khs = kh_s[h % 2]
                        vhs = vh_s[h % 2]
                        k.dma(khs[:], K_d[seq, h], reads=[kv_b[seq]])
                        k.dma(vhs[:, :, 0:128], V_d[seq, :, :, h * 128:(h + 1) * 128].rearrange("b p d -> p b d"),
                              reads=[kv_b[seq]])
                        qnh = qn[h]
                        qrp = qr[h]
                        pso = [bank(pin=True), bank(pin=True)]

                        def s_stage(kp):
                            pss_ = bank()
                            for u in range(2):
                                kc = 2 * kp + u
                                k.mm(pss_[:, u * TT:(u + 1) * TT], khs[:, kc * 128:(kc + 1) * 128], qnh[:], True, False)
                                k.mm(pss_[:, u * TT:(u + 1) * TT], kr_s[:, kc * 128:(kc + 1) * 128], qrp[:], False, True)
                            pt_ = pTr()
                            k.act(pt_[:], pss_[:, :], AF.Exp, scale=SCALE)
                            return pt_

                        npair = nkc // 2
                        LA = 3
                        pts = {i: s_stage(i) for i in range(LA)}
                        for kp in range(npair):
                            if kp + LA < npair:
                                pts[kp + LA] = s_stage(kp + LA)
                            pt_ = pts.pop(kp)
                            for u in range(2):
                                kc = 2 * kp + u
                                for j in range(2):
                                    k.mm(pso[j][:, 0:129], pt_[:, u * TT + j * 128:u * TT + (j + 1) * 128], vhs[:, kc, 0:129],
                                         kc == 0, kc == nkc - 1)
                        for j in range(2):
                            rcol = rsum[:, (2 * h + j) % 4:(2 * h + j) % 4 + 1]
                            k.recip(rcol, pso[j][:, 128:129])
                            k.ts(otm[j][:, h * 128:(h + 1) * 128], pso[j][:, 0:128], rcol, None, ALU.mult)
                        pinned.clear()
                    for j in range(2):
                        pb = bankb()
                        for c in range(8):
                            k.tr(pb[:, c * 128:(c + 1) * 128], otm[j][:, c * 128:(c + 1) * 128], ident_b[:])
                        for c in range(8):
                            k.tt(yb[c][:, j * 128:(j + 1) * 128], pb[:, c * 128:(c + 1) * 128], sg[c][:, j * 128:(j + 1) * 128], ALU.mult)
                    dd = out_proj_residual(yb, 0, d["xv"], dap[t0:t0 + TT, :], dbuf)
                    if li == len(layers) - 1:
                        final_dmas.append(dd)

                run_tiles(qtiles, qprep, q_a, q_b)

        def layer3(li, kl):
            l = 3
            RSTD["mode"] = "sqrt"
            load_w_rows(w_in, w_in_t, ch_w_in, 3 * D)
            load_w_rows(w_out, w_out_t, ch_w_out, D)
            w_x_t = kl.sb("l3_w_x", [128, 1024], BF16)
            w_x = V(w_x_t, [Buf("w_x")])
            wsT = V(w_x_t[:, 0:1024].rearrange("p (g q) -> p g q", g=8), w_x.bufs)
            load_w(w_x, wsT.ap, ch_wsT)
            lng = kl.vsb("l3_lng", [128, D], F32)
            lnb = kl.vsb("l3_lnb", [128, D], F32)
            bsb = kl.vsb("l3_bsb", [128, 8, 256], F32)
            k.dma(lng[:], ch_lng)
            k.dma(lnb[:], ch_lnb)
            k.dma(bsb[:], ch_bsb)
            vtr = Rot([kl.vsb("l3_vt%d" % i, [128, D], F32) for i in range(2)])
            vnb = [kl.vsb("l3_vnb%d" % j, [128, D], BF16) for j in range(2)]
            s1r = Rot([kl.vsb("l3_s1%d" % i, [128, 2], F32) for i in range(2)])
            s2r = Rot([kl.vsb("l3_s2%d" % i, [128, 2], F32) for i in range(2)])
            mr = Rot([kl.vsb("l3_m%d" % i, [128, 4], F32) for i in range(2)])
            tsmr = Rot([kl.vsb("l3_tsm%d" % i, [128, 256], F32) for i in range(3)])

            def prep(d):
                if d["t"] == 0:
                    load_mod(l, d["slot"], d["seq"])
                t0 = d["t"] * TT
                return front_a(d["sap"][t0:t0 + TT, :], d["sbufs"], d["slot"], TT)

            def a_phase(d):
                t0 = d["t"] * TT
                d["xv"] = xr_load(d["sap"][t0:t0 + TT, :], d["sbufs"])
                for j in range(2):
                    s1, s2, m, vt = s1r(), s2r(), mr(), vtr()
                    k.memset(s1[:], 0.0)
                    k.memset(s2[:], 0.0)
                    for hf in range(2):
                        ps = bank()
                        for kc in range(8):
                            k.mm(ps[:, :], htile[kc][:, j * 128:(j + 1) * 128],
                                 w_in[:, kc, D + hf * 512:D + (hf + 1) * 512], kc == 0, kc == 7)
                        k.act(vt[:, hf * 512:(hf + 1) * 512], ps[:, :], AF.Identity, accum=s1[:, hf:hf + 1])
                        k.act(junk[:, 0:512], ps[:, :], AF.Square, accum=s2[:, hf:hf + 1])
                    k.tt(m[:, 0:1], s1[:, 0:1], s1[:, 1:2], ALU.add)
                    k.tt(m[:, 1:2], s2[:, 0:1], s2[:, 1:2], ALU.add)
                    k.ts(m[:, 0:2], m[:, 0:2], 1.0 / D, None, ALU.mult)
                    k.tt(m[:, 2:3], m[:, 0:1], m[:, 0:1], ALU.mult)
                    k.tt(m[:, 2:3], m[:, 1:2], m[:, 2:3], ALU.subtract)
                    k.act(m[:, 2:3], m[:, 2:3], AF.Sqrt, bias=epsc[:])
                    k.recip(m[:, 2:3], m[:, 2:3])
                    k.stt(m[:, 3:4], m[:, 0:1], -1.0, m[:, 2:3], ALU.mult, ALU.mult)
                    tf = tmpf[j % 2]
                    k.ts(tf[:], vt[:], m[:, 2:3], m[:, 3:4], ALU.mult, ALU.add)
                    k.tt(tf[:], tf[:], lng[:], ALU.mult)
                    k.tt(vnb[j][:], tf[:], lnb[:], ALU.add)
                for oc in range(8):
                    ps = bank()
                    for kc in range(8):
                        k.mm(ps[:, 0:TT], w_in[:, kc, 2 * D + oc * 128:2 * D + (oc + 1) * 128], htile[kc][:, 0:TT], kc == 0, kc == 7)
                    k.act(sg[oc][:, 0:TT], ps[:, 0:TT], AF.Silu)
                for oc in range(8):
                    ps2 = bank()
                    for kc in range(8):
                        k.mm(ps2[:, 0:TT], w_in[:, kc, oc * 128:(oc + 1) * 128], htile[kc][:, 0:TT], kc == 0, kc == 7)
                    k.tt(ya[oc][:, 0:TT], ps2[:, 0:TT], sg[oc][:, 0:TT], ALU.mult)

            def b_phase(d, mid):
                t0 = d["t"] * TT
                for g in range(8):
                    ps = bank()
                    for j in range(2):
                        k.mm(ps[:, j * 128:(j + 1) * 128], vnb[j][:, g * 128:(g + 1) * 128], wsT[:, g, :], True, True)
                    tsm = tsmr()
                    k.tt(tsm[:], ps[:, 0:TT], bsb[:, g, :], ALU.add)
                    k.tt(yb[g][:], tsm[:], ya[g][:, 0:TT], ALU.mult)
                mid()
                dd = out_proj_residual(yb, d["slot"], d["xv"], d["dap"][t0:t0 + TT, :], d["dbuf"])
                if li == len(layers) - 1:
                    final_dmas.append(dd)

            run_tiles(seg_tiles(li, False, lambda s, c: s % 2), prep, a_phase, b_phase)

        def dbg_final():
            if DBG["on"]:
                o_ = nc.dram_tensor("dbg_modrow", [4, 3, 3 * D], F32, kind="ExternalOutput").ap()
                final_dmas.append(k.dma(o_, modrow, reads=[modrow_b]))

        fns = {0: layer0, 1: layer1, 2: layer2, 3: layer3}
        for li, l in enumerate(layers):
            with ExitStack() as lst:
                kl = K(nc, lst)
                kl.S = S
                fns[l](li, kl)
                S.barrier()

        dbg_final()
        with nc.allow_low_precision("bf16 matmul operands, fp32 accumulation"):
            S.emit(st, final_dmas)
    return nc


def host_inputs(inputs, core, layers=(0, 1, 2, 3)):
    f = np.float32
    b0 = core * NB
    m = {}
    m["x"] = np.ascontiguousarray(inputs["x"][b0:b0 + NB]).astype(f)
    m["ctx"] = np.ascontiguousarray(inputs["ctx"][b0:b0 + NB]).astype(f)
    crow = np.stack([inputs["c"][b0], inputs["c"][b0 + 1], inputs["c_ctx"]], axis=0).astype(f)
    m["cT"] = np.ascontiguousarray(crow.reshape(3, 8, 128).transpose(2, 1, 0))
    m["w_mod"] = np.ascontiguousarray(inputs["w_mod"]).astype(f)
    m["b_mod3"] = np.ascontiguousarray(np.broadcast_to(inputs["b_mod"][:, None, :], (4, 3, 3 * D))).astype(f)
    m["ng_rep"] = np.ascontiguousarray(np.broadcast_to(inputs["norm_g"][:, None, :], (4, 128, D))).astype(f)
    m["ident"] = np.eye(128, dtype=f)

    def colvec(v):
        return np.ascontiguousarray(np.asarray(v).reshape(8, 128).T).astype(f)

    if 0 in layers:
        m["cv_w_in"] = np.ascontiguousarray(inputs["cv_w_in"][0]).astype(f)
        dw = inputs["cv_dw"][0]
        m["cv_dw"] = np.ascontiguousarray(dw.reshape(31, 8, 128).transpose(2, 1, 0)).astype(f)
        m["cv_vec"] = np.ascontiguousarray(np.stack([colvec(inputs["cv_db"][0]), colvec(inputs["cv_ln_g"][0]),
                                                     colvec(inputs["cv_ln_b"][0])], axis=1)).astype(f)
        m["cv_w_out"] = np.ascontiguousarray(inputs["cv_w_out"][0]).astype(f)
    if 1 in layers:
        m["pl_w_in"] = np.ascontiguousarray(inputs["pl_w_in"][0]).astype(f)
        m["pl_w_grp"] = np.ascontiguousarray(inputs["pl_w_grp"][0]).astype(f)
        m["pl_scale"] = colvec(inputs["pl_scale"][0])
        m["pl_icnt"] = np.ascontiguousarray(np.broadcast_to(pool_icnt()[None], (128, 3, 4, 256))).astype(f)
        m["pl_w_out"] = np.ascontiguousarray(inputs["pl_w_out"][0]).astype(f)
    if 2 in layers:
        m["ml_w_in"] = np.ascontiguousarray(inputs["ml_w_in"][0]).astype(f)
        m["ml_w_uq"] = np.ascontiguousarray(inputs["ml_w_uq"][0]).astype(f)
        m["ml_w_ukv"] = np.ascontiguousarray(inputs["ml_w_ukv"][0]).astype(f)
        v = np.zeros((128, 9), f)
        v[:, 0:3] = inputs["ml_q_norm"][0].reshape(3, 128).T
        v[:, 3:5] = inputs["ml_kv_norm"][0].reshape(2, 128).T
        v[:, 5] = inputs["ml_nope_norm"][0][0]
        v[:, 6] = inputs["ml_nope_norm"][0][1]
        v[:, 7] = np.tile(inputs["ml_rope_norm"][0][0], 2)
        v[:, 8] = np.tile(inputs["ml_rope_norm"][0][1], 2)
        m["ml_vec"] = v
        m["ml_w_out"] = np.ascontiguousarray(inputs["ml_w_out"][0]).astype(f)
        C2, S2, PT, blk = rope_tables()
        m["rp_c"], m["rp_s"], m["rp_pt"], m["rp_blk"] = C2, S2, PT, blk
    if 3 in layers:
        m["ch_w_in"] = np.ascontiguousarray(inputs["ch_w_in"][0]).astype(f)
        m["ch_lng"] = np.ascontiguousarray(np.broadcast_to(inputs["ch_ln_g"][0][None, :], (128, D))).astype(f)
        m["ch_lnb"] = np.ascontiguousarray(np.broadcast_to(inputs["ch_ln_b"][0][None, :], (128, D))).astype(f)
        ws = inputs["ch_w_s"][0]
        m["ch_wsT"] = np.ascontiguousarray(ws.transpose(2, 0, 1)).astype(f)
        bs = inputs["ch_b_s"][0]
        bsb = np.broadcast_to(bs.T[None, :, None, :], (128, 8, 2, 128)).reshape(128, 8, 256)
        m["ch_bsb"] = np.ascontiguousarray(bsb).astype(f)
        m["ch_w_out"] = np.ascontiguousarray(inputs["ch_w_out"][0]).astype(f)
    return m


_NC_CACHE = {}


def kernel(**inputs):
    inputs = {kk: np.asarray(v) for kk, v in inputs.items()}
    if "full" not in _NC_CACHE:
        _NC_CACHE["full"] = build()
    nc = _NC_CACHE["full"]
    n = 8
    in_maps = [host_inputs(inputs, c) for c in range(n)]
    res = run_bass_kernel_spmd(nc, in_maps, core_ids=list(range(n)))
    outs = [res.results[c]["out"] for c in range(n)]
    return np.concatenate(outs, axis=0).astype(np.float32)
```

```python
import numpy as np
from contextlib import ExitStack
import concourse.bass as bass
import concourse.mybir as mybir
from concourse.bass_utils import run_bass_kernel_spmd

F32 = mybir.dt.float32
BF16 = mybir.dt.bfloat16
AF = mybir.ActivationFunctionType
ALU = mybir.AluOpType

D = 1024
SEQ = 2048
CTX = 256
NB = 2
TT = 256
EPS = 1e-6
ENG_NAMES = ("pe", "act", "dve", "pool", "sp")
RELAX = False
RELAX_WAW = ("act", "dve", "pool")
RELAX_WAR = ("act", "dve", "pool")


class Buf:
    __slots__ = ("name", "lw", "rd")

    def __init__(self, name=""):
        self.name = name
        self.lw = []
        self.rd = []


class V:
    __slots__ = ("ap", "bufs")

    def __init__(self, ap, bufs=None):
        self.ap = ap
        if bufs is None:
            bufs = [Buf()]
        self.bufs = bufs if isinstance(bufs, (list, tuple)) else [bufs]

    def __getitem__(self, idx):
        return V(self.ap[idx], self.bufs)


class Op:
    __slots__ = ("eng", "fn", "deps", "sig", "sigval", "is_dma", "dsem", "dval", "prev_dma")

    def __init__(self, eng, fn):
        self.eng = eng
        self.fn = fn
        self.deps = []
        self.sig = False
        self.sigval = 0
        self.is_dma = False
        self.dsem = None
        self.dval = 0
        self.prev_dma = None


class Sched:
    def __init__(self, nc, n_dma_sems=8):
        self.nc = nc
        self.ops = {e: [] for e in ENG_NAMES}
        self.n_dma_sems = n_dma_sems
        self.dma_rr = {e: 0 for e in ENG_NAMES}
        self.dma_last = {}
        self.dma_cnt = {}
        self.pending = {e: [] for e in ENG_NAMES}

    def barrier(self):
        lasts = []
        for e in ENG_NAMES:
            for o in reversed(self.ops[e]):
                if not o.is_dma:
                    lasts.append(o)
                    break
        dmas = list(self.dma_last.values())
        for e in ENG_NAMES:
            self.pending[e] = [o for o in lasts if o.eng != e] + dmas

    def op(self, eng, fn, reads=(), writes=(), dma=False):
        o = Op(eng, fn)
        deps = {}
        rb = []
        for r in reads:
            if r is None:
                continue
            rb.extend(r.bufs if isinstance(r, V) else [r])
        wb = []
        for w in writes:
            wb.extend(w.bufs if isinstance(w, V) else [w])
        for b in rb:
            for w_ in b.lw:
                deps[id(w_)] = w_
        rbs = set(id(b) for b in rb)
        for b in wb:
            if not (dma and b.lw and all(w_.is_dma for w_ in b.lw)):
                for w_ in b.lw:
                    if RELAX and (eng in RELAX_WAW) and (not dma) and (not w_.is_dma) and w_.eng == eng and id(b) not in rbs:
                        continue
                    deps[id(w_)] = w_
            for r in b.rd:
                if RELAX and (eng in RELAX_WAR) and (not dma) and (not r.is_dma) and r.eng == eng:
                    continue
                deps[id(r)] = r
        for d in deps.values():
            if (not d.is_dma) and (not dma) and d.eng == eng and eng == "pe":
                continue
            o.deps.append(d)
        if self.pending[eng]:
            o.deps.extend(self.pending[eng])
            self.pending[eng] = []
        if dma:
            o.is_dma = True
        for b in rb:
            if not dma:
                b.rd = [r for r in b.rd if r.is_dma or r.eng != eng]
            b.rd.append(o)
        for b in wb:
            if dma and b.lw and all(w_.is_dma for w_ in b.lw):
                b.lw = b.lw + [o]
            else:
                b.lw = [o]
            b.rd = []
        if dma:
            slot = self.dma_rr[eng] % self.n_dma_sems
            self.dma_rr[eng] += 1
            key = (eng, slot)
            o.prev_dma = self.dma_last.get(key)
            self.dma_cnt[key] = self.dma_cnt.get(key, 0) + 1
            o.dsem = key
            o.dval = 16 * self.dma_cnt[key]
            self.dma_last[key] = o
        self.ops[eng].append(o)
        return o

    def emit(self, stack, final_dmas):
        nc = self.nc
        for e in ENG_NAMES:
            for o in self.ops[e]:
                for d in o.deps:
                    if not d.is_dma:
                        d.sig = True
        for e in ENG_NAMES:
            c = 0
            for o in self.ops[e]:
                if o.sig and not o.is_dma:
                    c += 1
                    o.sigval = c
        esem = {e: stack.enter_context(nc.semaphore("s_" + e)) for e in ENG_NAMES}
        dsem = {}
        for key in self.dma_cnt:
            dsem[key] = stack.enter_context(nc.semaphore("d_%s_%d" % key))
        block = stack.enter_context(nc.Block())
        sched = self

        def run(e, engh):
            waited = {}

            def wait(key, sem, val):
                if waited.get(key, 0) >= val:
                    return
                waited[key] = val
                engh.wait_ge(sem, val)

            for o in sched.ops[e]:
                for d in o.deps:
                    if d.is_dma:
                        wait(d.dsem, dsem[d.dsem], d.dval)
                    else:
                        wait(d.eng, esem[d.eng], d.sigval)
                if o.is_dma and o.prev_dma is not None:
                    wait(o.dsem, dsem[o.dsem], o.prev_dma.dval)
                ins = o.fn(engh)
                if o.is_dma:
                    ins.then_inc(dsem[o.dsem], 16)
                elif o.sig:
                    ins.then_inc(esem[e], 1)
            if e == "sp":
                for d in final_dmas:
                    wait(d.dsem, dsem[d.dsem], d.dval)
                for d in sched.dma_last.values():
                    wait(d.dsem, dsem[d.dsem], d.dval)

        @block.tensor
        def _(pe):
            run("pe", pe)

        @block.scalar
        def _(act):
            run("act", act)

        @block.vector
        def _(dve):
            run("dve", dve)

        @block.gpsimd
        def _(pool):
            run("pool", pool)

        @block.sync
        def _(sp):
            run("sp", sp)


class K:
    def __init__(self, nc, st):
        self.nc = nc
        self.st = st
        self.S = Sched(nc)
        self.nps = 0

    def sb(self, name, shape, dt):
        return self.st.enter_context(self.nc.sbuf_tensor(name, shape, dt))

    def vsb(self, name, shape, dt, nbuf=None):
        t = self.sb(name, shape, dt)
        return V(t, [Buf(name)])

    def mm(self, o, lhsT, rhs, start, stop):
        self.S.op("pe", lambda e: e.matmul(o.ap, lhsT=lhsT.ap, rhs=rhs.ap, start=start, stop=stop),
                  reads=[lhsT, rhs], writes=[o])

    def tr(self, o, i, ident):
        self.S.op("pe", lambda e: e.transpose(o.ap, i.ap, ident.ap), reads=[i, ident], writes=[o])

    def act(self, o, i, func, bias=None, scale=None, accum=None, eng="act"):
        kw = {}
        rd = [i]
        wr = [o]
        if bias is not None:
            if isinstance(bias, V):
                kw["bias"] = bias.ap
                rd.append(bias)
            else:
                kw["bias"] = bias
        if scale is not None:
            if isinstance(scale, V):
                kw["scale"] = scale.ap
                rd.append(scale)
            else:
                kw["scale"] = scale
        if accum is not None:
            kw["accum_out"] = accum.ap
            wr.append(accum)
        self.S.op("act", lambda e: e.activation(out=o.ap, in_=i.ap, func=func, **kw), reads=rd, writes=wr)

    def tt(self, o, a, b, op, eng="dve"):
        self.S.op(eng, lambda e: e.tensor_tensor(out=o.ap, in0=a.ap, in1=b.ap, op=op), reads=[a, b], writes=[o])

    def ts(self, o, a, s1, s2, op0, op1=None, eng="dve"):
        rd = [a]
        a1 = s1.ap if isinstance(s1, V) else s1
        a2 = s2.ap if isinstance(s2, V) else s2
        if isinstance(s1, V):
            rd.append(s1)
        if isinstance(s2, V):
            rd.append(s2)
        if op1 is None:
            self.S.op(eng, lambda e: e.tensor_scalar(out=o.ap, in0=a.ap, scalar1=a1, scalar2=None, op0=op0),
                      reads=rd, writes=[o])
        else:
            self.S.op(eng, lambda e: e.tensor_scalar(out=o.ap, in0=a.ap, scalar1=a1, scalar2=a2, op0=op0, op1=op1),
                      reads=rd, writes=[o])

    def stt(self, o, a, s, b, op0, op1, eng="dve"):
        rd = [a, b]
        a1 = s.ap if isinstance(s, V) else s
        if isinstance(s, V):
            rd.append(s)
        self.S.op(eng, lambda e: e.scalar_tensor_tensor(out=o.ap, in0=a.ap, scalar=a1, in1=b.ap, op0=op0, op1=op1),
                  reads=rd, writes=[o])

    def copy(self, o, i, eng="dve"):
        self.S.op(eng, lambda e: e.tensor_copy(out=o.ap, in_=i.ap), reads=[i], writes=[o])

    def recip(self, o, i):
        self.S.op("dve", lambda e: e.reciprocal(out=o.ap, in_=i.ap), reads=[i], writes=[o])

    def memset(self, o, val, eng="dve"):
        self.S.op(eng, lambda e: e.memset(o.ap, val), writes=[o])

    def dma(self, o, i, eng="sp", reads=(), writes=()):
        rd = list(reads)
        wr = list(writes)
        if isinstance(i, V):
            rd.append(i)
            iap = i.ap
        else:
            iap = i
        if isinstance(o, V):
            wr.append(o)
            oap = o.ap
        else:
            oap = o
        return self.S.op(eng, lambda e: e.dma_start(out=oap, in_=iap), reads=rd, writes=wr, dma=True)


def rope_tables():
    rows = SEQ // 64
    row_id = np.repeat(np.arange(rows), 64).astype(np.float32)
    col_id = np.tile(np.arange(64), rows).astype(np.float32)
    axis_dim = 32
    freqs = (np.float32(10000.0) ** (-np.arange(0, axis_dim, 2, dtype=np.float32) / np.float32(axis_dim))).astype(np.float32)
    ar = row_id[:, None] * freqs[None, :]
    ac = col_id[:, None] * freqs[None, :]
    C = np.zeros((64, SEQ), np.float32)
    Sn = np.zeros((64, SEQ), np.float32)
    for d in range(64):
        ang = ar if d < 32 else ac
        f = d % 16
        C[d] = np.cos(ang[:, f])
        Sn[d] = np.sin(ang[:, f])
    C2 = np.concatenate([C, C], axis=0)
    S2 = np.concatenate([Sn, Sn], axis=0)
    P = np.zeros((128, 128), np.float32)
    for m in range(128):
        if m % 32 < 16:
            P[m, m + 16] = -1.0
        else:
            P[m, m - 16] = 1.0
    blk = np.zeros((128, 128), np.float32)
    blk[:64, :64] = 1.0
    blk[64:, 64:] = 1.0
    return C2, S2, np.ascontiguousarray(P.T), blk


def pool_icnt():
    out = np.zeros((3, 4, 256), np.float32)
    for kind, (L, t0) in enumerate([(SEQ, 0), (SEQ, SEQ - 256), (CTX, 0)]):
        for g, w in enumerate((2, 4, 8, 16)):
            t = np.arange(t0, t0 + 256)
            start = np.clip(t - w // 2, 0, L)
            end = np.clip(t + (w - w // 2), 0, L)
            out[kind, g] = 1.0 / (end - start).astype(np.float32)
    return out


DBG = {"on": False, "outs": {}}


def build(layers=(0, 1, 2, 3)):
    nc = bass.Bass("TRN2", target_bir_lowering=False)
    DBG["outs"] = {}

    def din(name, shape, dt=F32):
        return nc.dram_tensor(name, list(shape), dt, kind="ExternalInput").ap()

    x_in = din("x", [NB, SEQ, D])
    cx_in = din("ctx", [NB, CTX, D])
    cT = din("cT", [128, 8, 3])
    w_mod = din("w_mod", [4, D, 3 * D])
    b_mod3 = din("b_mod3", [4, 3, 3 * D])
    ng_rep = din("ng_rep", [4, 128, D])
    ident_in = din("ident", [128, 128])
    if 0 in layers:
        cv_w_in = din("cv_w_in", [D, 3 * D])
        cv_dw = din("cv_dw", [128, 8, 31])
        cv_vec = din("cv_vec", [128, 3, 8])
        cv_w_out = din("cv_w_out", [D, D])
    if 1 in layers:
        pl_w_in = din("pl_w_in", [D, 2 * D])
        pl_w_grp = din("pl_w_grp", [4, 256, 256])
        pl_scale = din("pl_scale", [128, 8])
        pl_icnt = din("pl_icnt", [128, 3, 4, 256])
        pl_w_out = din("pl_w_out", [D, D])
    if 2 in layers:
        ml_w_in = din("ml_w_in", [D, 1728])
        ml_w_uq = din("ml_w_uq", [384, 1536])
        ml_w_ukv = din("ml_w_ukv", [256, 2048])
        ml_vec = din("ml_vec", [128, 9])
        ml_w_out = din("ml_w_out", [D, D])
        rp_c = din("rp_c", [128, SEQ])
        rp_s = din("rp_s", [128, SEQ])
        rp_pt = din("rp_pt", [128, 128])
        rp_blk = din("rp_blk", [128, 128])
    if 3 in layers:
        ch_w_in = din("ch_w_in", [D, 3 * D])
        ch_lng = din("ch_lng", [128, D])
        ch_lnb = din("ch_lnb", [128, D])
        ch_wsT = din("ch_wsT", [128, 8, 128])
        ch_bsb = din("ch_bsb", [128, 8, 256])
        ch_w_out = din("ch_w_out", [D, D])
    out = nc.dram_tensor("out", [NB, SEQ, D], F32, kind="ExternalOutput").ap()
    modrow = nc.dram_tensor("modrow", [4, 3, 3 * D], F32).ap()
    xs = [nc.dram_tensor("xs%d" % i, [NB, SEQ, D], F32).ap() for i in range(2)]
    cxs = [nc.dram_tensor("cxs%d" % i, [NB, CTX, D], F32).ap() for i in range(2)]
    modrow_b = Buf("modrow")
    xs_b = [[Buf() for _ in range(NB)] for _ in range(2)]
    cxs_b = [[Buf() for _ in range(NB)] for _ in range(2)]
    NK = SEQ + CTX
    if 2 in layers:
        K_d = nc.dram_tensor("K_d", [NB, 8, 128, NK], BF16).ap()
        KR_d = nc.dram_tensor("KR_d", [NB, 64, NK], BF16).ap()
        V_d = nc.dram_tensor("V_d", [NB, NK // 128, 128, D], BF16).ap()
        kv_b = [Buf("kv%d" % i) for i in range(NB)]

    with ExitStack() as st:
        k = K(nc, st)
        S = k.S
        ident_f = k.vsb("ident_f", [128, 128], F32)
        ident_b = k.vsb("ident_b", [128, 128], BF16)
        ones_b = k.vsb("ones_b", [128, 128], BF16)
        epsc = k.vsb("epsc", [128, 1], F32)
        k.dma(ident_f[:], ident_in)
        k.copy(ident_b[:], ident_f[:])
        k.memset(epsc[:], EPS)
        k.memset(ones_b[:], 1.0)
        psf = [V(st.enter_context(nc.psum_tensor("psf%d" % i, [128, 512], F32)), [Buf("psf%d" % i)]) for i in range(6)]
        psb = [V(st.enter_context(nc.psum_tensor("psb%d" % i, [128, 1024], BF16)), [Buf("psb%d" % i)]) for i in range(2)]
        cnt = {"f": 0, "b": 0}
        pinned = set()

        def bank(pin=False):
            while True:
                cnt["f"] += 1
                i = cnt["f"] % 6
                if i not in pinned:
                    break
            if pin:
                pinned.add(i)
            return psf[i]

        def bankb():
            cnt["b"] += 1
            return psb[cnt["b"] % 2]

        w_in_t = k.sb("w_in", [128, 8, 3 * D], BF16)
        w_out_t = k.sb("w_out", [128, 8, D], BF16)
        w_in = V(w_in_t, [Buf("w_in")])
        w_out = V(w_out_t, [Buf("w_out")])
        G = [k.vsb("G%d" % i, [128, D], F32) for i in range(2)]
        SH = [k.vsb("SH%d" % i, [128, D], F32) for i in range(2)]
        GT = [k.vsb("GT%d" % i, [128, D], F32) for i in range(2)]
        xt = k.vsb("xt", [128, 3, D], F32)
        xr = [k.vsb("xr%d" % i, [128, 2, D], F32) for i in range(2)]
        tmpf = [k.vsb("tmpf%d" % i, [128, D], F32) for i in range(2)]
        hb = [k.vsb("hb%d" % i, [128, D], BF16) for i in range(3)]
        HW = 288
        htile_t = k.sb("htile", [128, 8, HW], BF16)
        htile = [V(htile_t[:, c, :], [Buf("h%d" % c)]) for c in range(8)]
        hall_bufs = [b for h in htile for b in h.bufs]
        st1 = k.vsb("st1", [128, 8], F32)
        st2 = k.vsb("st2", [128, 8], F32)
        junk = k.vsb("junk", [128, D], BF16)
        def chunked(name, width, dt=BF16, kk_=None):
            t = (kk_ or k).sb(name, [128, 8, width], dt)
            return t, [V(t[:, c, :], [Buf("%s%d" % (name, c))]) for c in range(8)]
        sg_t, sg = chunked("sg", 272)
        ya_t, ya = chunked("ya", 288)
        yb_t, yb = chunked("yb", 256)

        with ExitStack() as pst:
            kp = K(nc, pst)
            kp.S = S
            cTs = kp.vsb("cTs", [128, 8, 3], F32)
            sTs = kp.vsb("sTs", [128, 8, 3], F32)
            k.dma(cTs[:], cT)
            k.act(sTs[:], cTs[:], AF.Silu)
            NWB = 3
            CGW = 384
            wmb = [kp.vsb("wmb%d" % i, [128, 8, CGW], F32) for i in range(NWB)]
            bm3 = [kp.vsb("bm3%d" % i, [3, CGW], F32) for i in range(NWB)]
            mrow = [kp.vsb("mrow%d" % i, [3, CGW], F32) for i in range(NWB)]
            gi = 0
            for l in layers:
                for cg in range(3 * D // CGW):
                    wb_ = wmb[gi % NWB]
                    k.dma(bm3[gi % NWB][:], b_mod3[l, :, cg * CGW:(cg + 1) * CGW], eng="act")
                    k.dma(wb_[:], w_mod[l, :, cg * CGW:(cg + 1) * CGW].rearrange("(kc p) n -> p kc n", p=128), eng="act")
                    ps = bank()
                    for kc in range(8):
                        k.mm(ps[0:3, 0:CGW], sTs[:, kc, :], wb_[:, kc, :], kc == 0, kc == 7)
                    mr = mrow[gi % NWB]
                    k.tt(mr[:], ps[0:3, 0:CGW], bm3[gi % NWB][:], ALU.add)
                    k.dma(modrow[l, :, cg * CGW:(cg + 1) * CGW], mr[:], writes=[modrow_b])
                    gi += 1
            S.barrier()

        def load_mod(l, slot, r):
            k.dma(SH[slot][:], modrow[l, r:r + 1, 0:D].partition_broadcast(128), reads=[modrow_b])
            k.dma(G[slot][:], modrow[l, r:r + 1, D:2 * D].partition_broadcast(128), reads=[modrow_b])
            k.dma(GT[slot][:], modrow[l, r:r + 1, 2 * D:3 * D].partition_broadcast(128), reads=[modrow_b])
            k.dma(tmpf[1][:], ng_rep[l])
            k.stt(G[slot][:], G[slot][:], 1.0, tmpf[1][:], ALU.add, ALU.mult)

        def load_w(dst_v, dst_ap, src_ap):
            k.dma(V(dst_ap, dst_v.bufs), src_ap, eng="pool")

        RSTD = {"mode": "sqrt"}

        def rstd_small(dst, src, n):
            if RSTD["mode"] == "sqrt":
                k.act(dst[:, 0:n], src[:, 0:n], AF.Sqrt, bias=epsc[:])
                k.recip(dst[:, 0:n], dst[:, 0:n])
            else:
                k.act(dst[:, 0:n], src[:, 0:n], AF.Ln, bias=epsc[:])
                k.act(dst[:, 0:n], dst[:, 0:n], AF.Exp, scale=-0.5)

        def front_a(src_ap, src_bufs, slot, ntok):
            nfull = ntok // 128
            rem = ntok % 128
            if nfull:
                k.dma(xt[:, 0:nfull, :], src_ap[0:nfull * 128, :].rearrange("(j p) d -> p j d", p=128), reads=src_bufs)
            if rem:
                k.dma(xt[0:rem, nfull, :], src_ap[nfull * 128:ntok, :], reads=src_bufs)
            blks = [(j, 128) for j in range(nfull)] + ([(nfull, rem)] if rem else [])
            nb = len(blks)
            k.memset(st1[:, 0:nb], 0.0)
            for j, nt in blks:
                k.act(junk[0:nt, :], xt[0:nt, j, :], AF.Square, accum=st1[0:nt, j:j + 1])
            k.ts(st2[:, 0:nb], st1[:, 0:nb], 1.0 / D, None, ALU.mult)
            rstd_small(st2, st2, nb)
            for j, nt in blks:
                tf = tmpf[j % 2]
                k.stt(tf[0:nt, :], xt[0:nt, j, :], st2[0:nt, j:j + 1], G[slot][0:nt, :], ALU.mult, ALU.mult)
                k.tt(hb[j][0:nt, :], tf[0:nt, :], SH[slot][0:nt, :], ALU.add)
            return blks

        def front_b(blks, col0=0):
            for j, nt in blks:
                pb = bankb()
                for c in range(8):
                    k.tr(pb[:, c * 128:c * 128 + nt], hb[j][0:nt, c * 128:(c + 1) * 128], ident_b[0:nt, 0:nt])
                dst = V(htile_t[:, :, col0 + j * 128: col0 + j * 128 + nt], hall_bufs)
                srcv = V(pb.ap.rearrange("p (c t) -> p c t", c=8)[:, :, 0:nt], pb.bufs)
                k.act(dst, srcv, AF.Copy)

        xr_rr = {"i": 0}

        def xr_load(src_ap, src_bufs):
            xr_rr["i"] += 1
            xv = xr[xr_rr["i"] % 2]
            k.dma(xv[:, :, :], src_ap.rearrange("(j p) d -> p j d", p=128), reads=src_bufs)
            return xv

        def out_proj_residual(y_chunks, slot, xv, dst_ap, dst_buf):
            for j in range(2):
                for hf in range(2):
                    ps = bank()
                    for kc in range(8):
                        k.mm(ps[:, :], y_chunks[kc][:, j * 128:(j + 1) * 128], w_out[:, kc, hf * 512:(hf + 1) * 512],
                             kc == 0, kc == 7)
                    tf = tmpf[(2 * j + hf) % 2]
                    k.tt(tf[:, 0:512], ps[:, :], GT[slot][:, hf * 512:(hf + 1) * 512], ALU.mult)
                    k.tt(xv[:, j, hf * 512:(hf + 1) * 512], tf[:, 0:512], xv[:, j, hf * 512:(hf + 1) * 512], ALU.add)
            return k.dma(dst_ap.rearrange("(j p) d -> p j d", p=128), xv[:, :, :], writes=[dst_buf])

        def run_tiles(tiles, prep, a_phase, b_phase):
            ctx = prep(tiles[0])
            for i, desc in enumerate(tiles):
                front_b(ctx)
                a_phase(desc)
                box = {}

                def mid(i=i, box=box):
                    if "n" not in box:
                        box["n"] = prep(tiles[i + 1]) if i + 1 < len(tiles) else None

                b_phase(desc, mid)
                mid()
                ctx = box["n"]

        def src_of(li, seq, is_ctx):
            if li == 0:
                return (cx_in[seq] if is_ctx else x_in[seq]), []
            return (cxs[(li - 1) % 2][seq] if is_ctx else xs[(li - 1) % 2][seq]), \
                   [cxs_b[(li - 1) % 2][seq] if is_ctx else xs_b[(li - 1) % 2][seq]]

        outb = Buf("outb")

        def dst_of(li, seq, is_ctx):
            if (not is_ctx) and li == len(layers) - 1:
                return out[seq], outb
            return (cxs[li % 2][seq] if is_ctx else xs[li % 2][seq]), \
                   (cxs_b[li % 2][seq] if is_ctx else xs_b[li % 2][seq])

        final_dmas = []

        def dbg(name, v, shape, dt):
            if not DBG["on"] or name in DBG["outs"]:
                return
            o_ = nc.dram_tensor("dbg_" + name, list(shape), dt, kind="ExternalOutput").ap()
            DBG["outs"][name] = 1
            final_dmas.append(k.dma(o_, v))

        def load_w_rows(dst_v, dst_t, src, ncols):
            for kc in range(8):
                load_w(dst_v, dst_t[:, kc, 0:ncols], src[kc * 128:(kc + 1) * 128, :])

        def rstd_from_ps(dst, ps_v, n, scale, npart=128):
            if RSTD["mode"] == "sqrt":
                k.act(dst[:, 0:n], ps_v[:, 0:n], AF.Sqrt, bias=epsc[0:npart, :], scale=scale)
                k.recip(dst[:, 0:n], dst[:, 0:n])
            else:
                k.act(dst[:, 0:n], ps_v[:, 0:n], AF.Ln, bias=epsc[0:npart, :], scale=scale)
                k.act(dst[:, 0:n], dst[:, 0:n], AF.Exp, scale=-0.5)

        def interleave(n, stage_a, stage_b):
            pend = {0: stage_a(0)}
            for i in range(n):
                if i + 1 < n:
                    pend[i + 1] = stage_a(i + 1)
                stage_b(i, pend.pop(i))

        class Rot:
            def __init__(self, items):
                self.items = items
                self.i = 0

            def __call__(self):
                self.i += 1
                return self.items[self.i % len(self.items)]

        def seg_tiles(li, with_ctx, slot_of):
            tiles = []
            for seq in range(NB):
                for is_ctx in ((False, True) if with_ctx else (False,)):
                    L = CTX if is_ctx else SEQ
                    sap, sbufs = src_of(li, seq, is_ctx)
                    dap, dbuf = dst_of(li, seq, is_ctx)
                    for t in range(L // TT):
                        tiles.append(dict(seq=seq, is_ctx=is_ctx, t=t, ntile=L // TT, L=L, slot=slot_of(seq, is_ctx),
                                          sap=sap, sbufs=sbufs, dap=dap, dbuf=dbuf, li=li))
            return tiles

        def layer0(li, kl):
            l = 0
            RSTD["mode"] = "sqrt"
            load_w_rows(w_in, w_in_t, cv_w_in, 3 * D)
            load_w_rows(w_out, w_out_t, cv_w_out, D)
            dwt = kl.vsb("l0_dw", [128, 8, 31], F32)
            vec = kl.vsb("l0_vec", [128, 3, 8], F32)
            k.dma(dwt[:], cv_dw)
            dwb = kl.vsb("l0_dwb", [128, 8, 31], BF16)
            k.copy(dwb[:], dwt[:])
            k.dma(vec[:], cv_vec)
            dg_t = [kl.sb("l0_dg%d" % i, [128, 31, 128], BF16) for i in range(2)]
            dg = [V(t, [Buf("dg")]) for t in dg_t]
            dgp = [V(t, [Buf("dgp")]) for t in dg_t]
            sbt = Rot([kl.vsb("l0_sb%d" % i, [128, 272], BF16) for i in range(3)])
            z_t, z = chunked("l0_z", 256, kk_=kl)
            zq_t, zq = chunked("l0_zq", 256, kk_=kl)
            mean = kl.vsb("l0_mean", [128, 256], F32)
            rs = kl.vsb("l0_rs", [128, 256], F32)
            mr_ = kl.vsb("l0_mr", [128, 256], F32)
            t1 = Rot([kl.vsb("l0_t1%d" % i, [128, 256], F32) for i in range(3)])
            su = Rot([kl.vsb("l0_su%d" % i, [128, 256], BF16) for i in range(3)])
            yall = V(ya_t, [b for y in ya for b in y.bufs])
            sgall = V(sg_t, [b for y in sg for b in y.bufs])

            def gen_diag(c):
                dgc = dg[c % 2]
                S.op("dve", lambda e, o_=dgc, c_=c: e.tensor_tensor(
                    out=o_.ap[:, :, :],
                    in0=ident_b.ap[:].unsqueeze(1).to_broadcast([128, 31, 128]),
                    in1=dwb.ap[:, c_, :].unsqueeze(2).to_broadcast([128, 31, 128]),
                    op=ALU.mult), reads=[ident_b, dwb], writes=[dgc])

            def rng(d):
                t0 = d["t"] * TT
                first = d["t"] == 0
                a0 = 0 if first else t0 + 15
                a1 = min(t0 + TT + 15, d["L"])
                return t0, a0, a1

            def prep(d):
                if d["t"] == 0:
                    load_mod(l, d["slot"], 2 if d["is_ctx"] else d["seq"])
                t0, a0, a1 = rng(d)
                return front_a(d["sap"][a0:a1, :], d["sbufs"], d["slot"], a1 - a0)

            def a_phase(d):
                t0, a0, a1 = rng(d)
                n = a1 - a0
                first = d["t"] == 0
                last = d["t"] == d["ntile"] - 1
                yc0 = a0 - t0 + 15
                sc0 = a0 - t0
                d["xv"] = xr_load(d["sap"][t0:t0 + TT, :], d["sbufs"])
                if first:
                    k.memset(yall[:, :, 0:15], 0.0)
                if last:
                    k.memset(yall[:, :, 271:286], 0.0)
                for j in range(8):
                    psa = bank()
                    for kc in range(8):
                        k.mm(psa[:, 0:n], w_in[:, kc, j * 128:(j + 1) * 128], htile[kc][:, 0:n], kc == 0, kc == 7)
                    psb_ = bank()
                    for kc in range(8):
                        k.mm(psb_[:, 0:n], w_in[:, kc, D + j * 128:D + (j + 1) * 128], htile[kc][:, 0:n], kc == 0, kc == 7)
                    sbj = sbt()
                    k.act(sbj[:, 0:n], psb_[:, 0:n], AF.Sigmoid)
                    k.tt(ya[j][:, yc0:yc0 + n], psa[:, 0:n], sbj[:, 0:n], ALU.mult)
                for j in range(8):
                    psg = bank()
                    for kc in range(8):
                        k.mm(psg[:, 0:n], w_in[:, kc, 2 * D + j * 128:2 * D + (j + 1) * 128], htile[kc][:, 0:n], kc == 0, kc == 7)
                    k.act(sg[j][:, sc0:sc0 + n], psg[:, 0:n], AF.Silu)
                gen_diag(0)
                gen_diag(1)
                dbg("G0", G[0][:, :], [128, D], F32)
                dbg("SH0", SH[0][:, :], [128, D], F32)
                dbg("GT0", GT[0][:, :], [128, D], F32)
                dbg("h", V(htile_t[:, :, :], hall_bufs), [128, 8, HW], BF16)
                dbg("ya", yall[:, :, :], [128, 8, 288], BF16)
                dbg("sg", sgall[:, :, :], [128, 8, 272], BF16)

            def b_phase(d, mid):
                t0, a0, a1 = rng(d)
                last = d["t"] == d["ntile"] - 1
                slot = d["slot"]
                for c in range(8):
                    dgc = dg[c % 2]
                    ps = bank()
                    for kk in range(31):
                        k.mm(ps[:, 0:TT], dgc[:, kk, :], ya[c][:, kk:kk + TT], kk == 0, kk == 30)
                    if c + 2 < 8:
                        gen_diag(c + 2)
                    k.act(z[c][:], ps[:, 0:TT], AF.Identity, bias=vec[:, 0, c:c + 1])
                    k.act(zq[c][:], ps[:, 0:TT], AF.Square, bias=vec[:, 0, c:c + 1])
                psm = bank()
                for c in range(8):
                    k.mm(psm[:, 0:TT], ones_b[:], z[c][:], c == 0, c == 7)
                psq = bank()
                for c in range(8):
                    k.mm(psq[:, 0:TT], ones_b[:], zq[c][:], c == 0, c == 7)
                k.ts(mean[:], psm[:, 0:TT], 1.0 / D, None, ALU.mult)
                k.tt(mr_[:], mean[:], mean[:], ALU.mult)
                k.stt(rs[:], psq[:, 0:TT], 1.0 / D, mr_[:], ALU.mult, ALU.subtract)
                k.act(rs[:], rs[:], AF.Sqrt, bias=epsc[:])
                k.recip(rs[:], rs[:])
                k.tt(mr_[:], mean[:], rs[:], ALU.mult)
                for c in range(8):
                    tc = t1()
                    k.tt(tc[:], z[c][:], rs[:], ALU.mult)
                    k.tt(tc[:], tc[:], mr_[:], ALU.subtract)
                    sc_ = su()
                    k.act(sc_[:], tc[:], AF.Silu, bias=vec[:, 2, c:c + 1], scale=vec[:, 1, c:c + 1])
                    k.tt(yb[c][:], sc_[:], sg[c][:, 0:TT], ALU.mult)
                dbg("z", V(z_t[:, :, :], [b_ for y_ in z for b_ in y_.bufs]), [128, 8, 256], BF16)
                dbg("yb", V(yb_t[:, :, :], [b_ for y_ in yb for b_ in y_.bufs]), [128, 8, 256], BF16)
                dbg("rs", rs[:, :], [128, 256], F32)
                dbg("mean", mean[:, :], [128, 256], F32)
                mid()
                dd = out_proj_residual(yb, slot, d["xv"], d["dap"][t0:t0 + TT, :], d["dbuf"])
                if (not d["is_ctx"]) and li == len(layers) - 1:
                    final_dmas.append(dd)
                if not last:
                    k.copy(yall[:, :, 0:30], yall[:, :, 256:286])
                    k.copy(sgall[:, :, 0:15], sgall[:, :, 256:271])

            run_tiles(seg_tiles(li, True, lambda s, c: 1 if c else 0), prep, a_phase, b_phase)

        def layer1(li, kl):
            l = 1
            RSTD["mode"] = "sqrt"
            load_w_rows(w_in, w_in_t, pl_w_in, 2 * D)
            load_w_rows(w_out, w_out_t, pl_w_out, D)
            w_x_t = kl.sb("l1_w_x", [128, 2048], BF16)
            w_x = V(w_x_t, [Buf("w_x")])
            wg = V(w_x_t[:, :].rearrange("p (g c n) -> p g c n", g=4, c=2), w_x.bufs)
            load_w(w_x, wg.ap, pl_w_grp.rearrange("g (c p) n -> p g c n", p=128))
            scl = kl.vsb("l1_scl", [128, 8], F32)
            icn = kl.vsb("l1_icn", [128, 3, 4, 256], F32)
            k.dma(scl[:], pl_scale)
            k.dma(icn[:], pl_icnt)
            vt_t = kl.sb("l1_v", [128, 8, 272], F32)
            vv = [V(vt_t[:, c, :], [Buf("l1v%d" % c)]) for c in range(8)]
            vall = V(vt_t, [b for y in vv for b in y.bufs])
            sgall = V(sg_t, [b for y in sg for b in y.bufs])
            p2r = Rot([kl.vsb("l1_p2%d" % i, [128, 272], F32) for i in range(2)])
            p4r = Rot([kl.vsb("l1_p4%d" % i, [128, 272], F32) for i in range(2)])
            p8r = Rot([kl.vsb("l1_p8%d" % i, [128, 272], F32) for i in range(2)])
            winr = Rot([kl.vsb("l1_win%d" % i, [128, 256], F32) for i in range(3)])

            def rng(d):
                t0 = d["t"] * TT
                first = d["t"] == 0
                a0 = 0 if first else t0 + 8
                a1 = min(t0 + TT + 8, d["L"])
                return t0, a0, a1

            def prep(d):
                if d["t"] == 0:
                    load_mod(l, d["slot"], 2 if d["is_ctx"] else d["seq"])
                t0, a0, a1 = rng(d)
                return front_a(d["sap"][a0:a1, :], d["sbufs"], d["slot"], a1 - a0)

            def a_phase(d):
                t0, a0, a1 = rng(d)
                n = a1 - a0
                first = d["t"] == 0
                last = d["t"] == d["ntile"] - 1
                vc0 = a0 - t0 + 8
                sc0 = a0 - t0
                d["xv"] = xr_load(d["sap"][t0:t0 + TT, :], d["sbufs"])
                if first:
                    k.memset(vall[:, :, 0:8], 0.0)
                if last:
                    k.memset(vall[:, :, 264:272], 0.0)
                for j in range(8):
                    psv = bank()
                    for kc in range(8):
                        k.mm(psv[:, 0:n], w_in[:, kc, j * 128:(j + 1) * 128], htile[kc][:, 0:n], kc == 0, kc == 7)
                    k.act(vv[j][:, vc0:vc0 + n], psv[:, 0:n], AF.Copy)
                for j in range(8):
                    psg = bank()
                    for kc in range(8):
                        k.mm(psg[:, 0:n], w_in[:, kc, D + j * 128:D + (j + 1) * 128], htile[kc][:, 0:n], kc == 0, kc == 7)
                    k.act(sg[j][:, sc0:sc0 + n], psg[:, 0:n], AF.Silu)

            def b_phase(d, mid):
                t0, a0, a1 = rng(d)
                first = d["t"] == 0
                last = d["t"] == d["ntile"] - 1
                slot = d["slot"]
                kind = None
                if first and last:
                    kind = 2
                elif first:
                    kind = 0
                elif last:
                    kind = 1
                for c in range(8):
                    g = c // 2
                    v = vv[c]
                    win = winr()
                    if g == 0:
                        k.tt(win[:], v[:, 7:263], v[:, 8:264], ALU.add)
                    else:
                        p2 = p2r()
                        k.tt(p2[:, 0:271], v[:, 0:271], v[:, 1:272], ALU.add)
                        if g == 1:
                            k.tt(win[:], p2[:, 6:262], p2[:, 8:264], ALU.add)
                        else:
                            p4 = p4r()
                            k.tt(p4[:, 0:269], p2[:, 0:269], p2[:, 2:271], ALU.add)
                            if g == 2:
                                k.tt(win[:], p4[:, 4:260], p4[:, 8:264], ALU.add)
                            else:
                                p8 = p8r()
                                k.tt(p8[:, 0:265], p4[:, 0:265], p4[:, 4:269], ALU.add)
                                k.tt(win[:], p8[:, 0:256], p8[:, 8:264], ALU.add)
                    if kind is None:
                        k.stt(ya[c][:, 0:TT], win[:], 1.0 / (2, 4, 8, 16)[g], v[:, 8:264], ALU.mult, ALU.subtract)
                    else:
                        k.tt(win[:], win[:], icn[:, kind, g, :], ALU.mult)
                        k.tt(ya[c][:, 0:TT], win[:], v[:, 8:264], ALU.subtract)
                for c in range(8):
                    g, oc2 = c // 2, c % 2
                    ps = bank()
                    for c2 in range(2):
                        k.mm(ps[:, 0:TT], wg[:, g, c2, oc2 * 128:(oc2 + 1) * 128], ya[2 * g + c2][:, 0:TT], c2 == 0, c2 == 1)
                    k.stt(yb[c][:], ps[:, 0:TT], scl[:, c:c + 1], sg[c][:, 0:TT], ALU.mult, ALU.mult)
                mid()
                dd = out_proj_residual(yb, slot, d["xv"], d["dap"][t0:t0 + TT, :], d["dbuf"])
                if (not d["is_ctx"]) and li == len(layers) - 1:
                    final_dmas.append(dd)
                if not last:
                    k.copy(vall[:, :, 0:16], vall[:, :, 256:272])
                    k.copy(sgall[:, :, 0:8], sgall[:, :, 256:264])

            run_tiles(seg_tiles(li, True, lambda s, c: 1 if c else 0), prep, a_phase, b_phase)

        def layer2(li, kl):
            l = 2
            RSTD["mode"] = "lnexp"
            flat = w_in_t[:, :, :].rearrange("p a b -> p (a b)")
            mlw_t = flat[:, 0:8 * 1728].rearrange("p (c n) -> p c n", c=8)
            mlw = V(mlw_t, w_in.bufs)
            for kc in range(8):
                load_w(w_in, mlw_t[:, kc, :], ml_w_in[kc * 128:(kc + 1) * 128, :])
            wukv_t = flat[:, 8 * 1728 + 3 * 1536:8 * 1728 + 3 * 1536 + 2 * 2048].rearrange("p (c n) -> p c n", c=2)
            wukv = V(wukv_t, w_in.bufs)
            wuqn_t = flat[:, 8 * 1728:8 * 1728 + 3 * 1024].rearrange("p (c n) -> p c n", c=3)
            wuqr_t = flat[:, 8 * 1728 + 3 * 1024:8 * 1728 + 3 * 1536].rearrange("p (c n) -> p c n", c=3)
            wuqn = V(wuqn_t, w_in.bufs)
            wuqr = V(wuqr_t, w_in.bufs)
            for kc in range(2):
                load_w(w_in, wukv_t[:, kc, :], ml_w_ukv[kc * 128:(kc + 1) * 128, :])
            for kc in range(3):
                srcv = ml_w_uq[kc * 128:(kc + 1) * 128, :].rearrange("p (h x) -> p h x", h=8)
                load_w(w_in, wuqn_t[:, kc, :].rearrange("p (h x) -> p h x", h=8), srcv[:, :, 0:128])
                load_w(w_in, wuqr_t[:, kc, :].rearrange("p (h x) -> p h x", h=8), srcv[:, :, 128:192])
            load_w_rows(w_out, w_out_t, ml_w_out, D)
            vec = kl.vsb("l2_vec", [128, 9], F32)
            k.dma(vec[:], ml_vec)
            pt_f = kl.vsb("l2_ptf", [128, 128], F32)
            blk_f = kl.vsb("l2_blkf", [128, 128], F32)
            pt_b = kl.vsb("l2_ptb", [128, 128], BF16)
            blk_b = kl.vsb("l2_blkb", [128, 128], BF16)
            k.dma(pt_f[:], rp_pt)
            k.dma(blk_f[:], rp_blk)
            k.copy(pt_b[:], pt_f[:])
            k.copy(blk_b[:], blk_f[:])
            rcs = [kl.vsb("l2_rc%d" % i, [128, 256], F32) for i in range(2)]
            rsns = [kl.vsb("l2_rsn%d" % i, [128, 256], F32) for i in range(2)]
            sqr = Rot([kl.vsb("l2_sq%d" % i, [128, 256], BF16) for i in range(5)])
            rrr = Rot([kl.vsb("l2_rr%d" % i, [128, 256], F32) for i in range(2)])
            tfr = Rot([kl.vsb("l2_tf%d" % i, [128, 256], F32) for i in range(2)])
            cn = [kl.vsb("l2_cn%d" % i, [128, 256], BF16) for i in range(3)]
            kstr = Rot([kl.vsb("l2_kst%d" % i, [128, 256], BF16) for i in range(2)])
            krbr = Rot([kl.vsb("l2_krb%d" % i, [128, 256], BF16) for i in range(2)])
            vstr = Rot([kl.vsb("l2_vst%d" % i, [128, D], BF16) for i in range(1)])
            kr_s = kl.vsb("l2_krs", [128, NK], BF16)
            kh_s = [kl.vsb("l2_kh%d" % i, [128, NK], BF16) for i in range(2)]
            vh_s = [kl.vsb("l2_vh%d" % i, [128, NK // 128, 132], BF16) for i in range(2)]
            for i in range(2):
                k.memset(vh_s[i][:, :, 128:129], 1.0)
            qn = [kl.vsb("l2_qn%d" % i, [128, 256], BF16) for i in range(8)]
            qr = [kl.vsb("l2_qr%d" % i, [128, 256], BF16) for i in range(8)]
            for i in range(8):
                k.memset(qr[i][:], 0.0)
            pTr = Rot([kl.vsb("l2_pT%d" % i, [128, 512], BF16) for i in range(5)])
            otm = [kl.vsb("l2_otm%d" % i, [128, D], BF16) for i in range(2)]
            rsum = kl.vsb("l2_rsum", [128, 4], F32)
            SCALE = float((128 + 64) ** -0.5)
            nkc = NK // 128
            rci = {"i": 0}

            def load_rope(t0):
                rci["i"] += 1
                rc, rsn = rcs[rci["i"] % 2], rsns[rci["i"] % 2]
                k.dma(rc[:], rp_c[:, t0:t0 + TT])
                k.dma(rsn[:], rp_s[:, t0:t0 + TT])
                return rc, rsn

            for seq in range(NB):
                ktiles = []
                for is_ctx in (False, True):
                    L = CTX if is_ctx else SEQ
                    sap, sbufs = src_of(li, seq, is_ctx)
                    for t in range(L // TT):
                        ktiles.append(dict(seq=seq, is_ctx=is_ctx, t=t, slot=1 if is_ctx else 0, sap=sap, sbufs=sbufs,
                                           koff=SEQ if is_ctx else 0))

                def kprep(d):
                    if d["t"] == 0:
                        load_mod(l, d["slot"], 2 if d["is_ctx"] else seq)
                    t0 = d["t"] * TT
                    return front_a(d["sap"][t0:t0 + TT, :], d["sbufs"], d["slot"], TT)

                def k_a(d):
                    t0 = d["t"] * TT
                    koff = d["koff"]
                    is_ctx = d["is_ctx"]
                    if not is_ctx:
                        rc, rsn = load_rope(t0)
                    pc = []
                    sqs = []
                    for j in range(2):
                        ps = bank()
                        for kc in range(8):
                            k.mm(ps[:, 0:TT], mlw[:, kc, j * 128:(j + 1) * 128], htile[kc][:, 0:TT], kc == 0, kc == 7)
                        sq = sqr()
                        k.act(sq[:], ps[:, 0:TT], AF.Square)
                        pc.append(ps)
                        sqs.append(sq)
                    psk = bank()
                    for kc in range(8):
                        k.mm(psk[0:64, 0:TT], mlw[:, kc, 256:320], htile[kc][:, 0:TT], kc == 0, kc == 7)
                    sqk = sqr()
                    k.act(sqk[0:64, :], psk[0:64, 0:TT], AF.Square)
                    pss = bank()
                    for j in range(2):
                        k.mm(pss[:, 0:TT], ones_b[:], sqs[j][:], j == 0, j == 1)
                    rr = rrr()
                    rstd_from_ps(rr, pss, TT, 1.0 / 256)
                    for j in range(2):
                        k.stt(cn[j][:], pc[j][:, 0:TT], vec[:, 3 + j:4 + j], rr[:], ALU.mult, ALU.mult)
                    pss2 = bank()
                    k.mm(pss2[0:64, 0:TT], ones_b[0:64, 0:64], sqk[0:64, :], True, True)
                    rr2 = rrr()
                    rstd_from_ps(rr2[0:64], pss2[0:64], TT, 1.0 / 64, npart=64)
                    tfa = tfr()
                    k.stt(tfa[0:64, :], psk[0:64, 0:TT], vec[0:64, 8:9], rr2[0:64, :], ALU.mult, ALU.mult)
                    krb = krbr()
                    if is_ctx:
                        k.copy(krb[0:64, :], tfa[0:64, :])
                    else:
                        sqc = sqr()
                        k.copy(sqc[0:64, :], tfa[0:64, :])
                        psr = bank()
                        k.mm(psr[0:64, 0:TT], pt_b[0:64, 0:64], sqc[0:64, :], True, True)
                        tfb = tfr()
                        k.tt(tfb[0:64, :], psr[0:64, 0:TT], rsn[0:64, :], ALU.mult)
                        k.tt(tfa[0:64, :], tfa[0:64, :], rc[0:64, :], ALU.mult)
                        k.tt(krb[0:64, :], tfa[0:64, :], tfb[0:64, :], ALU.add)
                    k.dma(KR_d[seq, :, koff + t0:koff + t0 + TT], krb[0:64, :], writes=[kv_b[seq]])

                def k_b(d, mid):
                    mid()
                    t0 = d["t"] * TT
                    koff = d["koff"]
                    def kA(h):
                        ps = bank()
                        for kc in range(2):
                            k.mm(ps[:, 0:TT], wukv[:, kc, h * 256:h * 256 + 128], cn[kc][:], kc == 0, kc == 1)
                        sq = sqr()
                        k.act(sq[:], ps[:, 0:TT], AF.Square)
                        return ps, sq

                    def kB(h, st_):
                        ps, sq = st_
                        pss = bank()
                        k.mm(pss[:, 0:TT], ones_b[:], sq[:], True, True)
                        rr = rrr()
                        rstd_from_ps(rr, pss, TT, 1.0 / 128)
                        ks = kstr()
                        k.stt(ks[:], ps[:, 0:TT], vec[:, 6:7], rr[:], ALU.mult, ALU.mult)
                        k.dma(K_d[seq, h, :, koff + t0:koff + t0 + TT], ks[:], writes=[kv_b[seq]])

                    interleave(8, kA, kB)
                    rhs_ap = wukv_t.rearrange("p c (h x) -> p c h x", h=8)
                    for j in range(2):
                        vs = vstr()
                        for hf in range(2):
                            ps = bank()
                            for kc in range(2):
                                rv = V(rhs_ap[:, kc, hf * 4:(hf + 1) * 4, 128:256], wukv.bufs)
                                k.mm(ps[:, :], cn[kc][:, j * 128:(j + 1) * 128], rv, kc == 0, kc == 1)
                            k.act(vs[:, hf * 512:(hf + 1) * 512], ps[:, :], AF.Copy)
                        k.dma(V_d[seq, (koff + t0) // 128 + j, :, :], vs[:], writes=[kv_b[seq]])

                run_tiles(ktiles, kprep, k_a, k_b)

                sap, sbufs = src_of(li, seq, False)
                dap, dbuf = dst_of(li, seq, False)
                k.dma(kr_s[0:64, :], KR_d[seq], reads=[kv_b[seq]])
                k.dma(kr_s[64:128, :], KR_d[seq], reads=[kv_b[seq]])
                qtiles = [dict(t=t) for t in range(SEQ // TT)]

                def qprep(d):
                    t0 = d["t"] * TT
                    return front_a(sap[t0:t0 + TT, :], sbufs, 0, TT)

                def q_a(d):
                    t0 = d["t"] * TT
                    d["xv"] = xr_load(sap[t0:t0 + TT, :], sbufs)
                    rc, rsn = load_rope(t0)
                    pc = []
                    sqs = []
                    for j in range(3):
                        ps = bank()
                        for kc in range(8):
                            k.mm(ps[:, 0:TT], mlw[:, kc, 320 + j * 128:320 + (j + 1) * 128], htile[kc][:, 0:TT], kc == 0, kc == 7)
                        sq = sqr()
                        k.act(sq[:], ps[:, 0:TT], AF.Square)
                        pc.append(ps)
                        sqs.append(sq)
                    pss = bank()
                    for j in range(3):
                        k.mm(pss[:, 0:TT], ones_b[:], sqs[j][:], j == 0, j == 2)
                    rr = rrr()
                    rstd_from_ps(rr, pss, TT, 1.0 / 384)
                    for j in range(3):
                        k.stt(cn[j][:], pc[j][:, 0:TT], vec[:, j:j + 1], rr[:], ALU.mult, ALU.mult)
                    def qA(h):
                        ps = bank()
                        for kc in range(3):
                            k.mm(ps[:, 0:TT], wuqn[:, kc, h * 128:(h + 1) * 128], cn[kc][:], kc == 0, kc == 2)
                        sq = sqr()
                        k.act(sq[:], ps[:, 0:TT], AF.Square)
                        return ps, sq

                    def qB(h, st_):
                        ps, sq = st_
                        pss = bank()
                        k.mm(pss[:, 0:TT], ones_b[:], sq[:], True, True)
                        rr = rrr()
                        rstd_from_ps(rr, pss, TT, 1.0 / 128)
                        k.stt(qn[h][:], ps[:, 0:TT], vec[:, 5:6], rr[:], ALU.mult, ALU.mult)

                    interleave(8, qA, qB)

                    def rA(pr):
                        ps = bank()
                        for kc in range(3):
                            k.mm(ps[:, 0:TT], wuqr[:, kc, pr * 128:(pr + 1) * 128], cn[kc][:], kc == 0, kc == 2)
                        sq = sqr()
                        k.act(sq[:], ps[:, 0:TT], AF.Square)
                        return ps, sq

                    for pr in range(4):
                        ps, sq = rA(pr)
                        pss = bank()
                        k.mm(pss[:, 0:TT], blk_b[:], sq[:], True, True)
                        rr = rrr()
                        rstd_from_ps(rr, pss, TT, 1.0 / 64)
                        tfa = tfr()
                        k.stt(tfa[:], ps[:, 0:TT], vec[:, 7:8], rr[:], ALU.mult, ALU.mult)
                        sqc = sqr()
                        k.copy(sqc[:], tfa[:])
                        psr = bank()
                        k.mm(psr[:, 0:TT], pt_b[:], sqc[:], True, True)
                        tfb = tfr()
                        k.tt(tfb[:], psr[:, 0:TT], rsn[:], ALU.mult)
                        k.tt(tfa[:], tfa[:], rc[:], ALU.mult)
                        k.tt(qr[2 * pr][0:64, :], tfa[0:64, :], tfb[0:64, :], ALU.add)
                        k.tt(qr[2 * pr + 1][64:128, :], tfa[64:128, :], tfb[64:128, :], ALU.add)
                    for j in range(8):
                        ps = bank()
                        for kc in range(8):
                            k.mm(ps[:, 0:TT], mlw[:, kc, 704 + j * 128:704 + (j + 1) * 128], htile[kc][:, 0:TT], kc == 0, kc == 7)
                        k.act(sg[j][:, 0:TT], ps[:, 0:TT], AF.Silu)

                def q_b(d, mid):
                    mid()
                    t0 = d["t"] * TT
                    for h in range(8):
                        khs = kh_s[h % 2]
                        vhs = vh_s[h % 2]
                        k.dma(khs[:], K_d[seq, h], reads=[kv_b[seq]], eng="pool")
                        k.dma(vhs[:, :, 0:128], V_d[seq, :, :, h * 128:(h + 1) * 128].rearrange("b p d -> p b d"),
                              reads=[kv_b[seq]], eng="pool")
                        qnh = qn[h]
                        qrp = qr[h]
                        pso = [bank(pin=True), bank(pin=True)]

                        def s_stage(kp):
                            pss_ = bank()
                            for u in range(2):
                                kc = 2 * kp + u
                                k.mm(pss_[:, u * TT:(u + 1) * TT], khs[:, kc * 128:(kc + 1) * 128], qnh[:], True, False)
                                k.mm(pss_[:, u * TT:(u + 1) * TT], kr_s[:, kc * 128:(kc + 1) * 128], qrp[:], False, True)
                            pt_ = pTr()
                            k.act(pt_[:], pss_[:, :], AF.Exp, scale=SCALE)
                            return pt_

                        npair = nkc // 2
                        LA = 3
                        pts = {i: s_stage(i) for i in range(LA)}
                        for kp in range(npair):
                            if kp + LA < npair:
                                pts[kp + LA] = s_stage(kp + LA)
                            pt_ = pts.pop(kp)
                            for u in range(2):
                                kc = 2 * kp + u
                                for j in range(2):
                                    k.mm(pso[j][:, 0:129], pt_[:, u * TT + j * 128:u * TT + (j + 1) * 128], vhs[:, kc, 0:129],
                                         kc == 0, kc == nkc - 1)
                        for j in range(2):
                            rcol = rsum[:, (2 * h + j) % 4:(2 * h + j) % 4 + 1]
                            k.recip(rcol, pso[j][:, 128:129])
                            k.ts(otm[j][:, h * 128:(h + 1) * 128], pso[j][:, 0:128], rcol, None, ALU.mult)
                        pinned.clear()
                    for j in range(2):
                        pb = bankb()
                        for c in range(8):
                            k.tr(pb[:, c * 128:(c + 1) * 128], otm[j][:, c * 128:(c + 1) * 128], ident_b[:])
                        for c in range(8):
                            k.tt(yb[c][:, j * 128:(j + 1) * 128], pb[:, c * 128:(c + 1) * 128], sg[c][:, j * 128:(j + 1) * 128], ALU.mult)
                    dd = out_proj_residual(yb, 0, d["xv"], dap[t0:t0 + TT, :], dbuf)
                    if li == len(layers) - 1:
                        final_dmas.append(dd)

                run_tiles(qtiles, qprep, q_a, q_b)

        def layer3(li, kl):
            l = 3
            RSTD["mode"] = "sqrt"
            load_w_rows(w_in, w_in_t, ch_w_in, 3 * D)
            load_w_rows(w_out, w_out_t, ch_w_out, D)
            w_x_t = kl.sb("l3_w_x", [128, 1024], BF16)
            w_x = V(w_x_t, [Buf("w_x")])
            wsT = V(w_x_t[:, 0:1024].rearrange("p (g q) -> p g q", g=8), w_x.bufs)
            load_w(w_x, wsT.ap, ch_wsT)
            lng = kl.vsb("l3_lng", [128, D], F32)
            lnb = kl.vsb("l3_lnb", [128, D], F32)
            bsb = kl.vsb("l3_bsb", [128, 8, 256], F32)
            k.dma(lng[:], ch_lng)
            k.dma(lnb[:], ch_lnb)
            k.dma(bsb[:], ch_bsb)
            vtr = Rot([kl.vsb("l3_vt%d" % i, [128, D], F32) for i in range(2)])
            vnb = [kl.vsb("l3_vnb%d" % j, [128, D], BF16) for j in range(2)]
            s1r = Rot([kl.vsb("l3_s1%d" % i, [128, 2], F32) for i in range(2)])
            s2r = Rot([kl.vsb("l3_s2%d" % i, [128, 2], F32) for i in range(2)])
            mr = Rot([kl.vsb("l3_m%d" % i, [128, 4], F32) for i in range(2)])
            tsmr = Rot([kl.vsb("l3_tsm%d" % i, [128, 256], F32) for i in range(3)])

            def prep(d):
                if d["t"] == 0:
                    load_mod(l, d["slot"], d["seq"])
                t0 = d["t"] * TT
                return front_a(d["sap"][t0:t0 + TT, :], d["sbufs"], d["slot"], TT)

            def a_phase(d):
                t0 = d["t"] * TT
                d["xv"] = xr_load(d["sap"][t0:t0 + TT, :], d["sbufs"])
                for j in range(2):
                    s1, s2, m, vt = s1r(), s2r(), mr(), vtr()
                    k.memset(s1[:], 0.0)
                    k.memset(s2[:], 0.0)
                    for hf in range(2):
                        ps = bank()
                        for kc in range(8):
                            k.mm(ps[:, :], htile[kc][:, j * 128:(j + 1) * 128],
                                 w_in[:, kc, D + hf * 512:D + (hf + 1) * 512], kc == 0, kc == 7)
                        k.act(vt[:, hf * 512:(hf + 1) * 512], ps[:, :], AF.Identity, accum=s1[:, hf:hf + 1])
                        k.act(junk[:, 0:512], ps[:, :], AF.Square, accum=s2[:, hf:hf + 1])
                    k.tt(m[:, 0:1], s1[:, 0:1], s1[:, 1:2], ALU.add)
                    k.tt(m[:, 1:2], s2[:, 0:1], s2[:, 1:2], ALU.add)
                    k.ts(m[:, 0:2], m[:, 0:2], 1.0 / D, None, ALU.mult)
                    k.tt(m[:, 2:3], m[:, 0:1], m[:, 0:1], ALU.mult)
                    k.tt(m[:, 2:3], m[:, 1:2], m[:, 2:3], ALU.subtract)
                    k.act(m[:, 2:3], m[:, 2:3], AF.Sqrt, bias=epsc[:])
                    k.recip(m[:, 2:3], m[:, 2:3])
                    k.stt(m[:, 3:4], m[:, 0:1], -1.0, m[:, 2:3], ALU.mult, ALU.mult)
                    tf = tmpf[j % 2]
                    k.ts(tf[:], vt[:], m[:, 2:3], m[:, 3:4], ALU.mult, ALU.add)
                    k.tt(tf[:], tf[:], lng[:], ALU.mult)
                    k.tt(vnb[j][:], tf[:], lnb[:], ALU.add)
                for oc in range(8):
                    ps = bank()
                    for kc in range(8):
                        k.mm(ps[:, 0:TT], w_in[:, kc, 2 * D + oc * 128:2 * D + (oc + 1) * 128], htile[kc][:, 0:TT], kc == 0, kc == 7)
                    k.act(sg[oc][:, 0:TT], ps[:, 0:TT], AF.Silu)
                for oc in range(8):
                    ps2 = bank()
                    for kc in range(8):
                        k.mm(ps2[:, 0:TT], w_in[:, kc, oc * 128:(oc + 1) * 128], htile[kc][:, 0:TT], kc == 0, kc == 7)
                    k.tt(ya[oc][:, 0:TT], ps2[:, 0:TT], sg[oc][:, 0:TT], ALU.mult)

            def b_phase(d, mid):
                t0 = d["t"] * TT
                for g in range(8):
                    ps = bank()
                    for j in range(2):
                        k.mm(ps[:, j * 128:(j + 1) * 128], vnb[j][:, g * 128:(g + 1) * 128], wsT[:, g, :], True, True)
                    tsm = tsmr()
                    k.tt(tsm[:], ps[:, 0:TT], bsb[:, g, :], ALU.add)
                    k.tt(yb[g][:], tsm[:], ya[g][:, 0:TT], ALU.mult)
                mid()
                dd = out_proj_residual(yb, d["slot"], d["xv"], d["dap"][t0:t0 + TT, :], d["dbuf"])
                if li == len(layers) - 1:
                    final_dmas.append(dd)

            run_tiles(seg_tiles(li, False, lambda s, c: s % 2), prep, a_phase, b_phase)

        def dbg_final():
            if DBG["on"]:
                o_ = nc.dram_tensor("dbg_modrow", [4, 3, 3 * D], F32, kind="ExternalOutput").ap()
                final_dmas.append(k.dma(o_, modrow, reads=[modrow_b]))

        fns = {0: layer0, 1: layer1, 2: layer2, 3: layer3}
        for li, l in enumerate(layers):
            with ExitStack() as lst:
                kl = K(nc, lst)
                kl.S = S
                fns[l](li, kl)
                S.barrier()

        dbg_final()
        with nc.allow_low_precision("bf16 matmul operands, fp32 accumulation"):
            S.emit(st, final_dmas)
    return nc


def host_inputs(inputs, core, layers=(0, 1, 2, 3)):
    f = np.float32
    b0 = core * NB
    m = {}
    m["x"] = np.ascontiguousarray(inputs["x"][b0:b0 + NB]).astype(f)
    m["ctx"] = np.ascontiguousarray(inputs["ctx"][b0:b0 + NB]).astype(f)
    crow = np.stack([inputs["c"][b0], inputs["c"][b0 + 1], inputs["c_ctx"]], axis=0).astype(f)
    m["cT"] = np.ascontiguousarray(crow.reshape(3, 8, 128).transpose(2, 1, 0))
    m["w_mod"] = np.ascontiguousarray(inputs["w_mod"]).astype(f)
    m["b_mod3"] = np.ascontiguousarray(np.broadcast_to(inputs["b_mod"][:, None, :], (4, 3, 3 * D))).astype(f)
    m["ng_rep"] = np.ascontiguousarray(np.broadcast_to(inputs["norm_g"][:, None, :], (4, 128, D))).astype(f)
    m["ident"] = np.eye(128, dtype=f)

    def colvec(v):
        return np.ascontiguousarray(np.asarray(v).reshape(8, 128).T).astype(f)

    if 0 in layers:
        m["cv_w_in"] = np.ascontiguousarray(inputs["cv_w_in"][0]).astype(f)
        dw = inputs["cv_dw"][0]
        m["cv_dw"] = np.ascontiguousarray(dw.reshape(31, 8, 128).transpose(2, 1, 0)).astype(f)
        m["cv_vec"] = np.ascontiguousarray(np.stack([colvec(inputs["cv_db"][0]), colvec(inputs["cv_ln_g"][0]),
                                                     colvec(inputs["cv_ln_b"][0])], axis=1)).astype(f)
        m["cv_w_out"] = np.ascontiguousarray(inputs["cv_w_out"][0]).astype(f)
    if 1 in layers:
        m["pl_w_in"] = np.ascontiguousarray(inputs["pl_w_in"][0]).astype(f)
        m["pl_w_grp"] = np.ascontiguousarray(inputs["pl_w_grp"][0]).astype(f)
        m["pl_scale"] = colvec(inputs["pl_scale"][0])
        m["pl_icnt"] = np.ascontiguousarray(np.broadcast_to(pool_icnt()[None], (128, 3, 4, 256))).astype(f)
        m["pl_w_out"] = np.ascontiguousarray(inputs["pl_w_out"][0]).astype(f)
    if 2 in layers:
        m["ml_w_in"] = np.ascontiguousarray(inputs["ml_w_in"][0]).astype(f)
        m["ml_w_uq"] = np.ascontiguousarray(inputs["ml_w_uq"][0]).astype(f)
        m["ml_w_ukv"] = np.ascontiguousarray(inputs["ml_w_ukv"][0]).astype(f)
        v = np.zeros((128, 9), f)
        v[:, 0:3] = inputs["ml_q_norm"][0].reshape(3, 128).T
        v[:, 3:5] = inputs["ml_kv_norm"][0].reshape(2, 128).T
        v[:, 5] = inputs["ml_nope_norm"][0][0]
        v[:, 6] = inputs["ml_nope_norm"][0][1]
        v[:, 7] = np.tile(inputs["ml_rope_norm"][0][0], 2)
        v[:, 8] = np.tile(inputs["ml_rope_norm"][0][1], 2)
        m["ml_vec"] = v
        m["ml_w_out"] = np.ascontiguousarray(inputs["ml_w_out"][0]).astype(f)
        C2, S2, PT, blk = rope_tables()
        m["rp_c"], m["rp_s"], m["rp_pt"], m["rp_blk"] = C2, S2, PT, blk
    if 3 in layers:
        m["ch_w_in"] = np.ascontiguousarray(inputs["ch_w_in"][0]).astype(f)
        m["ch_lng"] = np.ascontiguousarray(np.broadcast_to(inputs["ch_ln_g"][0][None, :], (128, D))).astype(f)
        m["ch_lnb"] = np.ascontiguousarray(np.broadcast_to(inputs["ch_ln_b"][0][None, :], (128, D))).astype(f)
        ws = inputs["ch_w_s"][0]
        m["ch_wsT"] = np.ascontiguousarray(ws.transpose(2, 0, 1)).astype(f)
        bs = inputs["ch_b_s"][0]
        bsb = np.broadcast_to(bs.T[None, :, None, :], (128, 8, 2, 128)).reshape(128, 8, 256)
        m["ch_bsb"] = np.ascontiguousarray(bsb).astype(f)
        m["ch_w_out"] = np.ascontiguousarray(inputs["ch_w_out"][0]).astype(f)
    return m


_NC_CACHE = {}


def kernel(**inputs):
    inputs = {kk: np.asarray(v) for kk, v in inputs.items()}
    if "full" not in _NC_CACHE:
        _NC_CACHE["full"] = build()
    nc = _NC_CACHE["full"]
    n = 8
    in_maps = [host_inputs(inputs, c) for c in range(n)]
    res = run_bass_kernel_spmd(nc, in_maps, core_ids=list(range(n)))
    outs = [res.results[c]["out"] for c in range(n)]
    return np.concatenate(outs, axis=0).astype(np.float32)
```

```python
import numpy as np
from contextlib import ExitStack
import concourse.bass as bass
import concourse.mybir as mybir
from concourse.bass_utils import run_bass_kernel_spmd

F32 = mybir.dt.float32
BF16 = mybir.dt.bfloat16
AF = mybir.ActivationFunctionType
ALU = mybir.AluOpType

D = 1024
SEQ = 2048
CTX = 256
NB = 2
TT = 256
EPS = 1e-6
ENG_NAMES = ("pe", "act", "dve", "pool", "sp")
RELAX = True
RELAX_WAW = ("act", "dve", "pool")
RELAX_WAR = ("act", "dve", "pool")


class Buf:
    __slots__ = ("name", "lw", "rd")

    def __init__(self, name=""):
        self.name = name
        self.lw = []
        self.rd = []


class V:
    __slots__ = ("ap", "bufs")

    def __init__(self, ap, bufs=None):
        self.ap = ap
        if bufs is None:
            bufs = [Buf()]
        self.bufs = bufs if isinstance(bufs, (list, tuple)) else [bufs]

    def __getitem__(self, idx):
        return V(self.ap[idx], self.bufs)


class Op:
    __slots__ = ("eng", "fn", "deps", "sig", "sigval", "is_dma", "dsem", "dval", "prev_dma")

    def __init__(self, eng, fn):
        self.eng = eng
        self.fn = fn
        self.deps = []
        self.sig = False
        self.sigval = 0
        self.is_dma = False
        self.dsem = None
        self.dval = 0
        self.prev_dma = None


class Sched:
    def __init__(self, nc, n_dma_sems=8):
        self.nc = nc
        self.ops = {e: [] for e in ENG_NAMES}
        self.n_dma_sems = n_dma_sems
        self.dma_rr = {e: 0 for e in ENG_NAMES}
        self.dma_last = {}
        self.dma_cnt = {}
        self.pending = {e: [] for e in ENG_NAMES}

    def barrier(self):
        lasts = []
        for e in ENG_NAMES:
            for o in reversed(self.ops[e]):
                if not o.is_dma:
                    lasts.append(o)
                    break
        dmas = list(self.dma_last.values())
        for e in ENG_NAMES:
            self.pending[e] = [o for o in lasts if o.eng != e] + dmas

    def op(self, eng, fn, reads=(), writes=(), dma=False):
        o = Op(eng, fn)
        deps = {}
        rb = []
        for r in reads:
            if r is None:
                continue
            rb.extend(r.bufs if isinstance(r, V) else [r])
        wb = []
        for w in writes:
            wb.extend(w.bufs if isinstance(w, V) else [w])
        for b in rb:
            for w_ in b.lw:
                deps[id(w_)] = w_
        rbs = set(id(b) for b in rb)
        for b in wb:
            if not (dma and b.lw and all(w_.is_dma for w_ in b.lw)):
                for w_ in b.lw:
                    if RELAX and (eng in RELAX_WAW) and (not dma) and (not w_.is_dma) and w_.eng == eng and id(b) not in rbs:
                        continue
                    deps[id(w_)] = w_
            for r in b.rd:
                if RELAX and (eng in RELAX_WAR) and (not dma) and (not r.is_dma) and r.eng == eng:
                    continue
                deps[id(r)] = r
        for d in deps.values():
            if (not d.is_dma) and (not dma) and d.eng == eng and eng == "pe":
                continue
            o.deps.append(d)
        if self.pending[eng]:
            o.deps.extend(self.pending[eng])
            self.pending[eng] = []
        if dma:
            o.is_dma = True
        for b in rb:
            if not dma:
                b.rd = [r for r in b.rd if r.is_dma or r.eng != eng]
            b.rd.append(o)
        for b in wb:
            if dma and b.lw and all(w_.is_dma for w_ in b.lw):
                b.lw = b.lw + [o]
            else:
                b.lw = [o]
            b.rd = []
        if dma:
            slot = self.dma_rr[eng] % self.n_dma_sems
            self.dma_rr[eng] += 1
            key = (eng, slot)
            o.prev_dma = self.dma_last.get(key)
            self.dma_cnt[key] = self.dma_cnt.get(key, 0) + 1
            o.dsem = key
            o.dval = 16 * self.dma_cnt[key]
            self.dma_last[key] = o
        self.ops[eng].append(o)
        return o

    def emit(self, stack, final_dmas):
        nc = self.nc
        for e in ENG_NAMES:
            for o in self.ops[e]:
                for d in o.deps:
                    if not d.is_dma:
                        d.sig = True
        for e in ENG_NAMES:
            c = 0
            for o in self.ops[e]:
                if o.sig and not o.is_dma:
                    c += 1
                    o.sigval = c
        esem = {e: stack.enter_context(nc.semaphore("s_" + e)) for e in ENG_NAMES}
        dsem = {}
        for key in self.dma_cnt:
            dsem[key] = stack.enter_context(nc.semaphore("d_%s_%d" % key))
        block = stack.enter_context(nc.Block())
        sched = self

        def run(e, engh):
            waited = {}

            def wait(key, sem, val):
                if waited.get(key, 0) >= val:
                    return
                waited[key] = val
                engh.wait_ge(sem, val)

            for o in sched.ops[e]:
                for d in o.deps:
                    if d.is_dma:
                        wait(d.dsem, dsem[d.dsem], d.dval)
                    else:
                        wait(d.eng, esem[d.eng], d.sigval)
                if o.is_dma and o.prev_dma is not None:
                    wait(o.dsem, dsem[o.dsem], o.prev_dma.dval)
                ins = o.fn(engh)
                if o.is_dma:
                    ins.then_inc(dsem[o.dsem], 16)
                elif o.sig:
                    ins.then_inc(esem[e], 1)
            if e == "sp":
                for d in final_dmas:
                    wait(d.dsem, dsem[d.dsem], d.dval)
                for d in sched.dma_last.values():
                    wait(d.dsem, dsem[d.dsem], d.dval)

        @block.tensor
        def _(pe):
            run("pe", pe)

        @block.scalar
        def _(act):
            run("act", act)

        @block.vector
        def _(dve):
            run("dve", dve)

        @block.gpsimd
        def _(pool):
            run("pool", pool)

        @block.sync
        def _(sp):
            run("sp", sp)


class K:
    def __init__(self, nc, st):
        self.nc = nc
        self.st = st
        self.S = Sched(nc)
        self.nps = 0

    def sb(self, name, shape, dt):
        return self.st.enter_context(self.nc.sbuf_tensor(name, shape, dt))

    def vsb(self, name, shape, dt, nbuf=None):
        t = self.sb(name, shape, dt)
        return V(t, [Buf(name)])

    def mm(self, o, lhsT, rhs, start, stop):
        self.S.op("pe", lambda e: e.matmul(o.ap, lhsT=lhsT.ap, rhs=rhs.ap, start=start, stop=stop),
                  reads=[lhsT, rhs], writes=[o])

    def tr(self, o, i, ident):
        self.S.op("pe", lambda e: e.transpose(o.ap, i.ap, ident.ap), reads=[i, ident], writes=[o])

    def act(self, o, i, func, bias=None, scale=None, accum=None, eng="act"):
        kw = {}
        rd = [i]
        wr = [o]
        if bias is not None:
            if isinstance(bias, V):
                kw["bias"] = bias.ap
                rd.append(bias)
            else:
                kw["bias"] = bias
        if scale is not None:
            if isinstance(scale, V):
                kw["scale"] = scale.ap
                rd.append(scale)
            else:
                kw["scale"] = scale
        if accum is not None:
            kw["accum_out"] = accum.ap
            wr.append(accum)
        self.S.op("act", lambda e: e.activation(out=o.ap, in_=i.ap, func=func, **kw), reads=rd, writes=wr)

    def tt(self, o, a, b, op, eng="dve"):
        self.S.op(eng, lambda e: e.tensor_tensor(out=o.ap, in0=a.ap, in1=b.ap, op=op), reads=[a, b], writes=[o])

    def ts(self, o, a, s1, s2, op0, op1=None, eng="dve"):
        rd = [a]
        a1 = s1.ap if isinstance(s1, V) else s1
        a2 = s2.ap if isinstance(s2, V) else s2
        if isinstance(s1, V):
            rd.append(s1)
        if isinstance(s2, V):
            rd.append(s2)
        if op1 is None:
            self.S.op(eng, lambda e: e.tensor_scalar(out=o.ap, in0=a.ap, scalar1=a1, scalar2=None, op0=op0),
                      reads=rd, writes=[o])
        else:
            self.S.op(eng, lambda e: e.tensor_scalar(out=o.ap, in0=a.ap, scalar1=a1, scalar2=a2, op0=op0, op1=op1),
                      reads=rd, writes=[o])

    def stt(self, o, a, s, b, op0, op1, eng="dve"):
        rd = [a, b]
        a1 = s.ap if isinstance(s, V) else s
        if isinstance(s, V):
            rd.append(s)
        self.S.op(eng, lambda e: e.scalar_tensor_tensor(out=o.ap, in0=a.ap, scalar=a1, in1=b.ap, op0=op0, op1=op1),
                  reads=rd, writes=[o])

    def copy(self, o, i, eng="dve"):
        self.S.op(eng, lambda e: e.tensor_copy(out=o.ap, in_=i.ap), reads=[i], writes=[o])

    def recip(self, o, i):
        self.S.op("dve", lambda e: e.reciprocal(out=o.ap, in_=i.ap), reads=[i], writes=[o])

    def memset(self, o, val, eng="dve"):
        self.S.op(eng, lambda e: e.memset(o.ap, val), writes=[o])

    def dma(self, o, i, eng="sp", reads=(), writes=()):
        rd = list(reads)
        wr = list(writes)
        if isinstance(i, V):
            rd.append(i)
            iap = i.ap
        else:
            iap = i
        if isinstance(o, V):
            wr.append(o)
            oap = o.ap
        else:
            oap = o
        return self.S.op(eng, lambda e: e.dma_start(out=oap, in_=iap), reads=rd, writes=wr, dma=True)


def rope_tables():
    rows = SEQ // 64
    row_id = np.repeat(np.arange(rows), 64).astype(np.float32)
    col_id = np.tile(np.arange(64), rows).astype(np.float32)
    axis_dim = 32
    freqs = (np.float32(10000.0) ** (-np.arange(0, axis_dim, 2, dtype=np.float32) / np.float32(axis_dim))).astype(np.float32)
    ar = row_id[:, None] * freqs[None, :]
    ac = col_id[:, None] * freqs[None, :]
    C = np.zeros((64, SEQ), np.float32)
    Sn = np.zeros((64, SEQ), np.float32)
    for d in range(64):
        ang = ar if d < 32 else ac
        f = d % 16
        C[d] = np.cos(ang[:, f])
        Sn[d] = np.sin(ang[:, f])
    C2 = np.concatenate([C, C], axis=0)
    S2 = np.concatenate([Sn, Sn], axis=0)
    P = np.zeros((128, 128), np.float32)
    for m in range(128):
        if m % 32 < 16:
            P[m, m + 16] = -1.0
        else:
            P[m, m - 16] = 1.0
    blk = np.zeros((128, 128), np.float32)
    blk[:64, :64] = 1.0
    blk[64:, 64:] = 1.0
    return C2, S2, np.ascontiguousarray(P.T), blk


def pool_icnt():
    out = np.zeros((3, 4, 256), np.float32)
    for kind, (L, t0) in enumerate([(SEQ, 0), (SEQ, SEQ - 256), (CTX, 0)]):
        for g, w in enumerate((2, 4, 8, 16)):
            t = np.arange(t0, t0 + 256)
            start = np.clip(t - w // 2, 0, L)
            end = np.clip(t + (w - w // 2), 0, L)
            out[kind, g] = 1.0 / (end - start).astype(np.float32)
    return out


DBG = {"on": False, "outs": {}}


def build(layers=(0, 1, 2, 3)):
    nc = bass.Bass("TRN2", target_bir_lowering=False)
    DBG["outs"] = {}

    def din(name, shape, dt=F32):
        return nc.dram_tensor(name, list(shape), dt, kind="ExternalInput").ap()

    x_in = din("x", [NB, SEQ, D])
    cx_in = din("ctx", [NB, CTX, D])
    cT = din("cT", [128, 8, 3])
    w_mod = din("w_mod", [4, D, 3 * D])
    b_mod3 = din("b_mod3", [4, 3, 3 * D])
    ng_rep = din("ng_rep", [4, 128, D])
    ident_in = din("ident", [128, 128])
    if 0 in layers:
        cv_w_in = din("cv_w_in", [D, 3 * D])
        cv_dw = din("cv_dw", [128, 8, 31])
        cv_vec = din("cv_vec", [128, 3, 8])
        cv_w_out = din("cv_w_out", [D, D])
    if 1 in layers:
        pl_w_in = din("pl_w_in", [D, 2 * D])
        pl_w_grp = din("pl_w_grp", [4, 256, 256])
        pl_scale = din("pl_scale", [128, 8])
        pl_icnt = din("pl_icnt", [128, 3, 4, 256])
        pl_w_out = din("pl_w_out", [D, D])
    if 2 in layers:
        ml_w_in = din("ml_w_in", [D, 1728])
        ml_w_uq = din("ml_w_uq", [384, 1536])
        ml_w_ukv = din("ml_w_ukv", [256, 2048])
        ml_vec = din("ml_vec", [128, 9])
        ml_w_out = din("ml_w_out", [D, D])
        rp_c = din("rp_c", [128, SEQ])
        rp_s = din("rp_s", [128, SEQ])
        rp_pt = din("rp_pt", [128, 128])
        rp_blk = din("rp_blk", [128, 128])
    if 3 in layers:
        ch_w_in = din("ch_w_in", [D, 3 * D])
        ch_lng = din("ch_lng", [128, D])
        ch_lnb = din("ch_lnb", [128, D])
        ch_wsT = din("ch_wsT", [128, 8, 128])
        ch_bsb = din("ch_bsb", [128, 8, 256])
        ch_w_out = din("ch_w_out", [D, D])
    out = nc.dram_tensor("out", [NB, SEQ, D], F32, kind="ExternalOutput").ap()
    modrow = nc.dram_tensor("modrow", [4, 3, 3 * D], F32).ap()
    xs = [nc.dram_tensor("xs%d" % i, [NB, SEQ, D], F32).ap() for i in range(2)]
    cxs = [nc.dram_tensor("cxs%d" % i, [NB, CTX, D], F32).ap() for i in range(2)]
    modrow_b = Buf("modrow")
    xs_b = [[Buf() for _ in range(NB)] for _ in range(2)]
    cxs_b = [[Buf() for _ in range(NB)] for _ in range(2)]
    NK = SEQ + CTX
    if 2 in layers:
        K_d = nc.dram_tensor("K_d", [NB, 8, 128, NK], BF16).ap()
        KR_d = nc.dram_tensor("KR_d", [NB, 64, NK], BF16).ap()
        V_d = nc.dram_tensor("V_d", [NB, NK // 128, 128, D], BF16).ap()
        kv_b = [Buf("kv%d" % i) for i in range(NB)]

    with ExitStack() as st:
        k = K(nc, st)
        S = k.S
        ident_f = k.vsb("ident_f", [128, 128], F32)
        ident_b = k.vsb("ident_b", [128, 128], BF16)
        ones_b = k.vsb("ones_b", [128, 128], BF16)
        epsc = k.vsb("epsc", [128, 1], F32)
        k.dma(ident_f[:], ident_in)
        k.copy(ident_b[:], ident_f[:])
        k.memset(epsc[:], EPS)
        k.memset(ones_b[:], 1.0)
        psf = [V(st.enter_context(nc.psum_tensor("psf%d" % i, [128, 512], F32)), [Buf("psf%d" % i)]) for i in range(6)]
        psb = [V(st.enter_context(nc.psum_tensor("psb%d" % i, [128, 1024], BF16)), [Buf("psb%d" % i)]) for i in range(2)]
        cnt = {"f": 0, "b": 0}
        pinned = set()

        def bank(pin=False):
            while True:
                cnt["f"] += 1
                i = cnt["f"] % 6
                if i not in pinned:
                    break
            if pin:
                pinned.add(i)
            return psf[i]

        def bankb():
            cnt["b"] += 1
            return psb[cnt["b"] % 2]

        w_in_t = k.sb("w_in", [128, 8, 3 * D], BF16)
        w_out_t = k.sb("w_out", [128, 8, D], BF16)
        w_in = V(w_in_t, [Buf("w_in")])
        w_out = V(w_out_t, [Buf("w_out")])
        G = [k.vsb("G%d" % i, [128, D], F32) for i in range(2)]
        SH = [k.vsb("SH%d" % i, [128, D], F32) for i in range(2)]
        GT = [k.vsb("GT%d" % i, [128, D], F32) for i in range(2)]
        xt = k.vsb("xt", [128, 3, D], F32)
        xr = [k.vsb("xr%d" % i, [128, 2, D], F32) for i in range(2)]
        tmpf = [k.vsb("tmpf%d" % i, [128, D], F32) for i in range(2)]
        hb = [k.vsb("hb%d" % i, [128, D], BF16) for i in range(3)]
        HW = 288
        htile_t = k.sb("htile", [128, 8, HW], BF16)
        htile = [V(htile_t[:, c, :], [Buf("h%d" % c)]) for c in range(8)]
        hall_bufs = [b for h in htile for b in h.bufs]
        st1 = k.vsb("st1", [128, 8], F32)
        st2 = k.vsb("st2", [128, 8], F32)
        junk = k.vsb("junk", [128, D], BF16)
        def chunked(name, width, dt=BF16, kk_=None):
            t = (kk_ or k).sb(name, [128, 8, width], dt)
            return t, [V(t[:, c, :], [Buf("%s%d" % (name, c))]) for c in range(8)]
        sg_t, sg = chunked("sg", 272)
        ya_t, ya = chunked("ya", 288)
        yb_t, yb = chunked("yb", 256)

        with ExitStack() as pst:
            kp = K(nc, pst)
            kp.S = S
            cTs = kp.vsb("cTs", [128, 8, 3], F32)
            sTs = kp.vsb("sTs", [128, 8, 3], F32)
            k.dma(cTs[:], cT)
            k.act(sTs[:], cTs[:], AF.Silu)
            NWB = 3
            CGW = 384
            wmb = [kp.vsb("wmb%d" % i, [128, 8, CGW], F32) for i in range(NWB)]
            bm3 = [kp.vsb("bm3%d" % i, [3, CGW], F32) for i in range(NWB)]
            mrow = [kp.vsb("mrow%d" % i, [3, CGW], F32) for i in range(NWB)]
            gi = 0
            for l in layers:
                for cg in range(3 * D // CGW):
                    wb_ = wmb[gi % NWB]
                    k.dma(bm3[gi % NWB][:], b_mod3[l, :, cg * CGW:(cg + 1) * CGW], eng="act")
                    k.dma(wb_[:], w_mod[l, :, cg * CGW:(cg + 1) * CGW].rearrange("(kc p) n -> p kc n", p=128), eng="act")
                    ps = bank()
                    for kc in range(8):
                        k.mm(ps[0:3, 0:CGW], sTs[:, kc, :], wb_[:, kc, :], kc == 0, kc == 7)
                    mr = mrow[gi % NWB]
                    k.tt(mr[:], ps[0:3, 0:CGW], bm3[gi % NWB][:], ALU.add)
                    k.dma(modrow[l, :, cg * CGW:(cg + 1) * CGW], mr[:], writes=[modrow_b])
                    gi += 1
            S.barrier()

        def load_mod(l, slot, r):
            k.dma(SH[slot][:], modrow[l, r:r + 1, 0:D].partition_broadcast(128), reads=[modrow_b])
            k.dma(G[slot][:], modrow[l, r:r + 1, D:2 * D].partition_broadcast(128), reads=[modrow_b])
            k.dma(GT[slot][:], modrow[l, r:r + 1, 2 * D:3 * D].partition_broadcast(128), reads=[modrow_b])
            k.dma(tmpf[1][:], ng_rep[l])
            k.stt(G[slot][:], G[slot][:], 1.0, tmpf[1][:], ALU.add, ALU.mult)

        def load_w(dst_v, dst_ap, src_ap):
            k.dma(V(dst_ap, dst_v.bufs), src_ap, eng="pool")

        RSTD = {"mode": "sqrt"}

        def rstd_small(dst, src, n):
            if RSTD["mode"] == "sqrt":
                k.act(dst[:, 0:n], src[:, 0:n], AF.Sqrt, bias=epsc[:])
                k.recip(dst[:, 0:n], dst[:, 0:n])
            else:
                k.act(dst[:, 0:n], src[:, 0:n], AF.Ln, bias=epsc[:])
                k.act(dst[:, 0:n], dst[:, 0:n], AF.Exp, scale=-0.5)

        def front_a(src_ap, src_bufs, slot, ntok):
            nfull = ntok // 128
            rem = ntok % 128
            if nfull:
                k.dma(xt[:, 0:nfull, :], src_ap[0:nfull * 128, :].rearrange("(j p) d -> p j d", p=128), reads=src_bufs)
            if rem:
                k.dma(xt[0:rem, nfull, :], src_ap[nfull * 128:ntok, :], reads=src_bufs)
            blks = [(j, 128) for j in range(nfull)] + ([(nfull, rem)] if rem else [])
            nb = len(blks)
            k.memset(st1[:, 0:nb], 0.0)
            for j, nt in blks:
                k.act(junk[0:nt, :], xt[0:nt, j, :], AF.Square, accum=st1[0:nt, j:j + 1])
            k.ts(st2[:, 0:nb], st1[:, 0:nb], 1.0 / D, None, ALU.mult)
            rstd_small(st2, st2, nb)
            for j, nt in blks:
                tf = tmpf[j % 2]
                k.stt(tf[0:nt, :], xt[0:nt, j, :], st2[0:nt, j:j + 1], G[slot][0:nt, :], ALU.mult, ALU.mult)
                k.tt(hb[j][0:nt, :], tf[0:nt, :], SH[slot][0:nt, :], ALU.add)
            return blks

        def front_b(blks, col0=0):
            for j, nt in blks:
                pb = bankb()
                for c in range(8):
                    k.tr(pb[:, c * 128:c * 128 + nt], hb[j][0:nt, c * 128:(c + 1) * 128], ident_b[0:nt, 0:nt])
                dst = V(htile_t[:, :, col0 + j * 128: col0 + j * 128 + nt], hall_bufs)
                srcv = V(pb.ap.rearrange("p (c t) -> p c t", c=8)[:, :, 0:nt], pb.bufs)
                k.act(dst, srcv, AF.Copy)

        xr_rr = {"i": 0}

        def xr_load(src_ap, src_bufs):
            xr_rr["i"] += 1
            xv = xr[xr_rr["i"] % 2]
            k.dma(xv[:, :, :], src_ap.rearrange("(j p) d -> p j d", p=128), reads=src_bufs)
            return xv

        def out_proj_residual(y_chunks, slot, xv, dst_ap, dst_buf):
            for j in range(2):
                for hf in range(2):
                    ps = bank()
                    for kc in range(8):
                        k.mm(ps[:, :], y_chunks[kc][:, j * 128:(j + 1) * 128], w_out[:, kc, hf * 512:(hf + 1) * 512],
                             kc == 0, kc == 7)
                    tf = tmpf[(2 * j + hf) % 2]
                    k.tt(tf[:, 0:512], ps[:, :], GT[slot][:, hf * 512:(hf + 1) * 512], ALU.mult)
                    k.tt(xv[:, j, hf * 512:(hf + 1) * 512], tf[:, 0:512], xv[:, j, hf * 512:(hf + 1) * 512], ALU.add)
            return k.dma(dst_ap.rearrange("(j p) d -> p j d", p=128), xv[:, :, :], writes=[dst_buf])

        def run_tiles(tiles, prep, a_phase, b_phase):
            ctx = prep(tiles[0])
            for i, desc in enumerate(tiles):
                front_b(ctx)
                a_phase(desc)
                box = {}

                def mid(i=i, box=box):
                    if "n" not in box:
                        box["n"] = prep(tiles[i + 1]) if i + 1 < len(tiles) else None

                b_phase(desc, mid)
                mid()
                ctx = box["n"]

        def src_of(li, seq, is_ctx):
            if li == 0:
                return (cx_in[seq] if is_ctx else x_in[seq]), []
            return (cxs[(li - 1) % 2][seq] if is_ctx else xs[(li - 1) % 2][seq]), \
                   [cxs_b[(li - 1) % 2][seq] if is_ctx else xs_b[(li - 1) % 2][seq]]

        outb = Buf("outb")

        def dst_of(li, seq, is_ctx):
            if (not is_ctx) and li == len(layers) - 1:
                return out[seq], outb
            return (cxs[li % 2][seq] if is_ctx else xs[li % 2][seq]), \
                   (cxs_b[li % 2][seq] if is_ctx else xs_b[li % 2][seq])

        final_dmas = []

        def dbg(name, v, shape, dt):
            if not DBG["on"] or name in DBG["outs"]:
                return
            o_ = nc.dram_tensor("dbg_" + name, list(shape), dt, kind="ExternalOutput").ap()
            DBG["outs"][name] = 1
            final_dmas.append(k.dma(o_, v))

        def load_w_rows(dst_v, dst_t, src, ncols):
            for kc in range(8):
                load_w(dst_v, dst_t[:, kc, 0:ncols], src[kc * 128:(kc + 1) * 128, :])

        def rstd_from_ps(dst, ps_v, n, scale, npart=128):
            if RSTD["mode"] == "sqrt":
                k.act(dst[:, 0:n], ps_v[:, 0:n], AF.Sqrt, bias=epsc[0:npart, :], scale=scale)
                k.recip(dst[:, 0:n], dst[:, 0:n])
            else:
                k.act(dst[:, 0:n], ps_v[:, 0:n], AF.Ln, bias=epsc[0:npart, :], scale=scale)
                k.act(dst[:, 0:n], dst[:, 0:n], AF.Exp, scale=-0.5)

        def interleave(n, stage_a, stage_b):
            pend = {0: stage_a(0)}
            for i in range(n):
                if i + 1 < n:
                    pend[i + 1] = stage_a(i + 1)
                stage_b(i, pend.pop(i))

        class Rot:
            def __init__(self, items):
                self.items = items
                self.i = 0

            def __call__(self):
                self.i += 1
                return self.items[self.i % len(self.items)]

        def seg_tiles(li, with_ctx, slot_of):
            tiles = []
            for seq in range(NB):
                for is_ctx in ((False, True) if with_ctx else (False,)):
                    L = CTX if is_ctx else SEQ
                    sap, sbufs = src_of(li, seq, is_ctx)
                    dap, dbuf = dst_of(li, seq, is_ctx)
                    for t in range(L // TT):
                        tiles.append(dict(seq=seq, is_ctx=is_ctx, t=t, ntile=L // TT, L=L, slot=slot_of(seq, is_ctx),
                                          sap=sap, sbufs=sbufs, dap=dap, dbuf=dbuf, li=li))
            return tiles

        def layer0(li, kl):
            l = 0
            RSTD["mode"] = "sqrt"
            load_w_rows(w_in, w_in_t, cv_w_in, 3 * D)
            load_w_rows(w_out, w_out_t, cv_w_out, D)
            dwt = kl.vsb("l0_dw", [128, 8, 31], F32)
            vec = kl.vsb("l0_vec", [128, 3, 8], F32)
            k.dma(dwt[:], cv_dw)
            dwb = kl.vsb("l0_dwb", [128, 8, 31], BF16)
            k.copy(dwb[:], dwt[:])
            k.dma(vec[:], cv_vec)
            dg_t = [kl.sb("l0_dg%d" % i, [128, 31, 128], BF16) for i in range(2)]
            dg = [V(t, [Buf("dg")]) for t in dg_t]
            dgp = [V(t, [Buf("dgp")]) for t in dg_t]
            sbt = Rot([kl.vsb("l0_sb%d" % i, [128, 272], BF16) for i in range(3)])
            z_t, z = chunked("l0_z", 256, kk_=kl)
            zq_t, zq = chunked("l0_zq", 256, kk_=kl)
            mean = kl.vsb("l0_mean", [128, 256], F32)
            rs = kl.vsb("l0_rs", [128, 256], F32)
            mr_ = kl.vsb("l0_mr", [128, 256], F32)
            t1 = Rot([kl.vsb("l0_t1%d" % i, [128, 256], F32) for i in range(3)])
            su = Rot([kl.vsb("l0_su%d" % i, [128, 256], BF16) for i in range(3)])
            yall = V(ya_t, [b for y in ya for b in y.bufs])
            sgall = V(sg_t, [b for y in sg for b in y.bufs])

            def gen_diag(c):
                dgc = dg[c % 2]
                S.op("dve", lambda e, o_=dgc, c_=c: e.tensor_tensor(
                    out=o_.ap[:, :, :],
                    in0=ident_b.ap[:].unsqueeze(1).to_broadcast([128, 31, 128]),
                    in1=dwb.ap[:, c_, :].unsqueeze(2).to_broadcast([128, 31, 128]),
                    op=ALU.mult), reads=[ident_b, dwb], writes=[dgc])

            def rng(d):
                t0 = d["t"] * TT
                first = d["t"] == 0
                a0 = 0 if first else t0 + 15
                a1 = min(t0 + TT + 15, d["L"])
                return t0, a0, a1

            def prep(d):
                if d["t"] == 0:
                    load_mod(l, d["slot"], 2 if d["is_ctx"] else d["seq"])
                t0, a0, a1 = rng(d)
                return front_a(d["sap"][a0:a1, :], d["sbufs"], d["slot"], a1 - a0)

            def a_phase(d):
                t0, a0, a1 = rng(d)
                n = a1 - a0
                first = d["t"] == 0
                last = d["t"] == d["ntile"] - 1
                yc0 = a0 - t0 + 15
                sc0 = a0 - t0
                d["xv"] = xr_load(d["sap"][t0:t0 + TT, :], d["sbufs"])
                if first:
                    k.memset(yall[:, :, 0:15], 0.0)
                if last:
                    k.memset(yall[:, :, 271:286], 0.0)
                for j in range(8):
                    psa = bank()
                    for kc in range(8):
                        k.mm(psa[:, 0:n], w_in[:, kc, j * 128:(j + 1) * 128], htile[kc][:, 0:n], kc == 0, kc == 7)
                    psb_ = bank()
                    for kc in range(8):
                        k.mm(psb_[:, 0:n], w_in[:, kc, D + j * 128:D + (j + 1) * 128], htile[kc][:, 0:n], kc == 0, kc == 7)
                    sbj = sbt()
                    k.act(sbj[:, 0:n], psb_[:, 0:n], AF.Sigmoid)
                    k.tt(ya[j][:, yc0:yc0 + n], psa[:, 0:n], sbj[:, 0:n], ALU.mult)
                for j in range(8):
                    psg = bank()
                    for kc in range(8):
                        k.mm(psg[:, 0:n], w_in[:, kc, 2 * D + j * 128:2 * D + (j + 1) * 128], htile[kc][:, 0:n], kc == 0, kc == 7)
                    k.act(sg[j][:, sc0:sc0 + n], psg[:, 0:n], AF.Silu)
                gen_diag(0)
                gen_diag(1)
                dbg("G0", G[0][:, :], [128, D], F32)
                dbg("SH0", SH[0][:, :], [128, D], F32)
                dbg("GT0", GT[0][:, :], [128, D], F32)
                dbg("h", V(htile_t[:, :, :], hall_bufs), [128, 8, HW], BF16)
                dbg("ya", yall[:, :, :], [128, 8, 288], BF16)
                dbg("sg", sgall[:, :, :], [128, 8, 272], BF16)

            def b_phase(d, mid):
                t0, a0, a1 = rng(d)
                last = d["t"] == d["ntile"] - 1
                slot = d["slot"]
                for c in range(8):
                    dgc = dg[c % 2]
                    ps = bank()
                    for kk in range(31):
                        k.mm(ps[:, 0:TT], dgc[:, kk, :], ya[c][:, kk:kk + TT], kk == 0, kk == 30)
                    if c + 2 < 8:
                        gen_diag(c + 2)
                    k.act(z[c][:], ps[:, 0:TT], AF.Identity, bias=vec[:, 0, c:c + 1])
                    k.act(zq[c][:], ps[:, 0:TT], AF.Square, bias=vec[:, 0, c:c + 1])
                psm = bank()
                for c in range(8):
                    k.mm(psm[:, 0:TT], ones_b[:], z[c][:], c == 0, c == 7)
                psq = bank()
                for c in range(8):
                    k.mm(psq[:, 0:TT], ones_b[:], zq[c][:], c == 0, c == 7)
                k.ts(mean[:], psm[:, 0:TT], 1.0 / D, None, ALU.mult)
                k.tt(mr_[:], mean[:], mean[:], ALU.mult)
                k.stt(rs[:], psq[:, 0:TT], 1.0 / D, mr_[:], ALU.mult, ALU.subtract)
                k.act(rs[:], rs[:], AF.Sqrt, bias=epsc[:])
                k.recip(rs[:], rs[:])
                k.tt(mr_[:], mean[:], rs[:], ALU.mult)
                for c in range(8):
                    tc = t1()
                    k.tt(tc[:], z[c][:], rs[:], ALU.mult)
                    k.tt(tc[:], tc[:], mr_[:], ALU.subtract)
                    sc_ = su()
                    k.act(sc_[:], tc[:], AF.Silu, bias=vec[:, 2, c:c + 1], scale=vec[:, 1, c:c + 1])
                    k.tt(yb[c][:], sc_[:], sg[c][:, 0:TT], ALU.mult)
                dbg("z", V(z_t[:, :, :], [b_ for y_ in z for b_ in y_.bufs]), [128, 8, 256], BF16)
                dbg("yb", V(yb_t[:, :, :], [b_ for y_ in yb for b_ in y_.bufs]), [128, 8, 256], BF16)
                dbg("rs", rs[:, :], [128, 256], F32)
                dbg("mean", mean[:, :], [128, 256], F32)
                mid()
                dd = out_proj_residual(yb, slot, d["xv"], d["dap"][t0:t0 + TT, :], d["dbuf"])
                if (not d["is_ctx"]) and li == len(layers) - 1:
                    final_dmas.append(dd)
                if not last:
                    k.copy(yall[:, :, 0:30], yall[:, :, 256:286])
                    k.copy(sgall[:, :, 0:15], sgall[:, :, 256:271])

            run_tiles(seg_tiles(li, True, lambda s, c: 1 if c else 0), prep, a_phase, b_phase)

        def layer1(li, kl):
            l = 1
            RSTD["mode"] = "sqrt"
            load_w_rows(w_in, w_in_t, pl_w_in, 2 * D)
            load_w_rows(w_out, w_out_t, pl_w_out, D)
            w_x_t = kl.sb("l1_w_x", [128, 2048], BF16)
            w_x = V(w_x_t, [Buf("w_x")])
            wg = V(w_x_t[:, :].rearrange("p (g c n) -> p g c n", g=4, c=2), w_x.bufs)
            load_w(w_x, wg.ap, pl_w_grp.rearrange("g (c p) n -> p g c n", p=128))
            scl = kl.vsb("l1_scl", [128, 8], F32)
            icn = kl.vsb("l1_icn", [128, 3, 4, 256], F32)
            k.dma(scl[:], pl_scale)
            k.dma(icn[:], pl_icnt)
            vt_t = kl.sb("l1_v", [128, 8, 272], F32)
            vv = [V(vt_t[:, c, :], [Buf("l1v%d" % c)]) for c in range(8)]
            vall = V(vt_t, [b for y in vv for b in y.bufs])
            sgall = V(sg_t, [b for y in sg for b in y.bufs])
            p2r = Rot([kl.vsb("l1_p2%d" % i, [128, 272], F32) for i in range(2)])
            p4r = Rot([kl.vsb("l1_p4%d" % i, [128, 272], F32) for i in range(2)])
            p8r = Rot([kl.vsb("l1_p8%d" % i, [128, 272], F32) for i in range(2)])
            winr = Rot([kl.vsb("l1_win%d" % i, [128, 256], F32) for i in range(3)])

            def rng(d):
                t0 = d["t"] * TT
                first = d["t"] == 0
                a0 = 0 if first else t0 + 8
                a1 = min(t0 + TT + 8, d["L"])
                return t0, a0, a1

            def prep(d):
                if d["t"] == 0:
                    load_mod(l, d["slot"], 2 if d["is_ctx"] else d["seq"])
                t0, a0, a1 = rng(d)
                return front_a(d["sap"][a0:a1, :], d["sbufs"], d["slot"], a1 - a0)

            def a_phase(d):
                t0, a0, a1 = rng(d)
                n = a1 - a0
                first = d["t"] == 0
                last = d["t"] == d["ntile"] - 1
                vc0 = a0 - t0 + 8
                sc0 = a0 - t0
                d["xv"] = xr_load(d["sap"][t0:t0 + TT, :], d["sbufs"])
                if first:
                    k.memset(vall[:, :, 0:8], 0.0)
                if last:
                    k.memset(vall[:, :, 264:272], 0.0)
                for j in range(8):
                    psv = bank()
                    for kc in range(8):
                        k.mm(psv[:, 0:n], w_in[:, kc, j * 128:(j + 1) * 128], htile[kc][:, 0:n], kc == 0, kc == 7)
                    k.act(vv[j][:, vc0:vc0 + n], psv[:, 0:n], AF.Copy)
                for j in range(8):
                    psg = bank()
                    for kc in range(8):
                        k.mm(psg[:, 0:n], w_in[:, kc, D + j * 128:D + (j + 1) * 128], htile[kc][:, 0:n], kc == 0, kc == 7)
                    k.act(sg[j][:, sc0:sc0 + n], psg[:, 0:n], AF.Silu)

            def b_phase(d, mid):
                t0, a0, a1 = rng(d)
                first = d["t"] == 0
                last = d["t"] == d["ntile"] - 1
                slot = d["slot"]
                kind = None
                if first and last:
                    kind = 2
                elif first:
                    kind = 0
                elif last:
                    kind = 1
                for c in range(8):
                    g = c // 2
                    v = vv[c]
                    win = winr()
                    if g == 0:
                        k.tt(win[:], v[:, 7:263], v[:, 8:264], ALU.add)
                    else:
                        p2 = p2r()
                        k.tt(p2[:, 0:271], v[:, 0:271], v[:, 1:272], ALU.add)
                        if g == 1:
                            k.tt(win[:], p2[:, 6:262], p2[:, 8:264], ALU.add)
                        else:
                            p4 = p4r()
                            k.tt(p4[:, 0:269], p2[:, 0:269], p2[:, 2:271], ALU.add)
                            if g == 2:
                                k.tt(win[:], p4[:, 4:260], p4[:, 8:264], ALU.add)
                            else:
                                p8 = p8r()
                                k.tt(p8[:, 0:265], p4[:, 0:265], p4[:, 4:269], ALU.add)
                                k.tt(win[:], p8[:, 0:256], p8[:, 8:264], ALU.add)
                    if kind is None:
                        k.stt(ya[c][:, 0:TT], win[:], 1.0 / (2, 4, 8, 16)[g], v[:, 8:264], ALU.mult, ALU.subtract)
                    else:
                        k.tt(win[:], win[:], icn[:, kind, g, :], ALU.mult)
                        k.tt(ya[c][:, 0:TT], win[:], v[:, 8:264], ALU.subtract)
                for c in range(8):
                    g, oc2 = c // 2, c % 2
                    ps = bank()
                    for c2 in range(2):
                        k.mm(ps[:, 0:TT], wg[:, g, c2, oc2 * 128:(oc2 + 1) * 128], ya[2 * g + c2][:, 0:TT], c2 == 0, c2 == 1)
                    k.stt(yb[c][:], ps[:, 0:TT], scl[:, c:c + 1], sg[c][:, 0:TT], ALU.mult, ALU.mult)
                mid()
                dd = out_proj_residual(yb, slot, d["xv"], d["dap"][t0:t0 + TT, :], d["dbuf"])
                if (not d["is_ctx"]) and li == len(layers) - 1:
                    final_dmas.append(dd)
                if not last:
                    k.copy(vall[:, :, 0:16], vall[:, :, 256:272])
                    k.copy(sgall[:, :, 0:8], sgall[:, :, 256:264])

            run_tiles(seg_tiles(li, True, lambda s, c: 1 if c else 0), prep, a_phase, b_phase)

        def layer2(li, kl):
            l = 2
            RSTD["mode"] = "lnexp"
            flat = w_in_t[:, :, :].rearrange("p a b -> p (a b)")
            mlw_t = flat[:, 0:8 * 1728].rearrange("p (c n) -> p c n", c=8)
            mlw = V(mlw_t, w_in.bufs)
            for kc in range(8):
                load_w(w_in, mlw_t[:, kc, :], ml_w_in[kc * 128:(kc + 1) * 128, :])
            wukv_t = flat[:, 8 * 1728 + 3 * 1536:8 * 1728 + 3 * 1536 + 2 * 2048].rearrange("p (c n) -> p c n", c=2)
            wukv = V(wukv_t, w_in.bufs)
            wuqn_t = flat[:, 8 * 1728:8 * 1728 + 3 * 1024].rearrange("p (c n) -> p c n", c=3)
            wuqr_t = flat[:, 8 * 1728 + 3 * 1024:8 * 1728 + 3 * 1536].rearrange("p (c n) -> p c n", c=3)
            wuqn = V(wuqn_t, w_in.bufs)
            wuqr = V(wuqr_t, w_in.bufs)
            for kc in range(2):
                load_w(w_in, wukv_t[:, kc, :], ml_w_ukv[kc * 128:(kc + 1) * 128, :])
            for kc in range(3):
                srcv = ml_w_uq[kc * 128:(kc + 1) * 128, :].rearrange("p (h x) -> p h x", h=8)
                load_w(w_in, wuqn_t[:, kc, :].rearrange("p (h x) -> p h x", h=8), srcv[:, :, 0:128])
                load_w(w_in, wuqr_t[:, kc, :].rearrange("p (h x) -> p h x", h=8), srcv[:, :, 128:192])
            load_w_rows(w_out, w_out_t, ml_w_out, D)
            vec = kl.vsb("l2_vec", [128, 9], F32)
            k.dma(vec[:], ml_vec)
            pt_f = kl.vsb("l2_ptf", [128, 128], F32)
            blk_f = kl.vsb("l2_blkf", [128, 128], F32)
            pt_b = kl.vsb("l2_ptb", [128, 128], BF16)
            blk_b = kl.vsb("l2_blkb", [128, 128], BF16)
            k.dma(pt_f[:], rp_pt)
            k.dma(blk_f[:], rp_blk)
            k.copy(pt_b[:], pt_f[:])
            k.copy(blk_b[:], blk_f[:])
            rcs = [kl.vsb("l2_rc%d" % i, [128, 256], F32) for i in range(2)]
            rsns = [kl.vsb("l2_rsn%d" % i, [128, 256], F32) for i in range(2)]
            sqr = Rot([kl.vsb("l2_sq%d" % i, [128, 256], BF16) for i in range(5)])
            rrr = Rot([kl.vsb("l2_rr%d" % i, [128, 256], F32) for i in range(2)])
            tfr = Rot([kl.vsb("l2_tf%d" % i, [128, 256], F32) for i in range(2)])
            cn = [kl.vsb("l2_cn%d" % i, [128, 256], BF16) for i in range(3)]
            kstr = Rot([kl.vsb("l2_kst%d" % i, [128, 256], BF16) for i in range(2)])
            krbr = Rot([kl.vsb("l2_krb%d" % i, [128, 256], BF16) for i in range(2)])
            vstr = Rot([kl.vsb("l2_vst%d" % i, [128, D], BF16) for i in range(1)])
            kr_s = kl.vsb("l2_krs", [128, NK], BF16)
            kh_s = [kl.vsb("l2_kh%d" % i, [128, NK], BF16) for i in range(2)]
            vh_s = [kl.vsb("l2_vh%d" % i, [128, NK // 128, 132], BF16) for i in range(2)]
            for i in range(2):
                k.memset(vh_s[i][:, :, 128:129], 1.0)
            qn = [kl.vsb("l2_qn%d" % i, [128, 256], BF16) for i in range(8)]
            qr = [kl.vsb("l2_qr%d" % i, [128, 256], BF16) for i in range(8)]
            for i in range(8):
                k.memset(qr[i][:], 0.0)
            pTr = Rot([kl.vsb("l2_pT%d" % i, [128, 512], BF16) for i in range(5)])
            otm = [kl.vsb("l2_otm%d" % i, [128, D], BF16) for i in range(2)]
            rsum = kl.vsb("l2_rsum", [128, 4], F32)
            SCALE = float((128 + 64) ** -0.5)
            nkc = NK // 128
            rci = {"i": 0}

            def load_rope(t0):
                rci["i"] += 1
                rc, rsn = rcs[rci["i"] % 2], rsns[rci["i"] % 2]
                k.dma(rc[:], rp_c[:, t0:t0 + TT])
                k.dma(rsn[:], rp_s[:, t0:t0 + TT])
                return rc, rsn

            for seq in range(NB):
                ktiles = []
                for is_ctx in (False, True):
                    L = CTX if is_ctx else SEQ
                    sap, sbufs = src_of(li, seq, is_ctx)
                    for t in range(L // TT):
                        ktiles.append(dict(seq=seq, is_ctx=is_ctx, t=t, slot=1 if is_ctx else 0, sap=sap, sbufs=sbufs,
                                           koff=SEQ if is_ctx else 0))

                def kprep(d):
                    if d["t"] == 0:
                        load_mod(l, d["slot"], 2 if d["is_ctx"] else seq)
                    t0 = d["t"] * TT
                    return front_a(d["sap"][t0:t0 + TT, :], d["sbufs"], d["slot"], TT)

                def k_a(d):
                    t0 = d["t"] * TT
                    koff = d["koff"]
                    is_ctx = d["is_ctx"]
                    if not is_ctx:
                        rc, rsn = load_rope(t0)
                    pc = []
                    sqs = []
                    for j in range(2):
                        ps = bank()
                        for kc in range(8):
                            k.mm(ps[:, 0:TT], mlw[:, kc, j * 128:(j + 1) * 128], htile[kc][:, 0:TT], kc == 0, kc == 7)
                        sq = sqr()
                        k.act(sq[:], ps[:, 0:TT], AF.Square)
                        pc.append(ps)
                        sqs.append(sq)
                    psk = bank()
                    for kc in range(8):
                        k.mm(psk[0:64, 0:TT], mlw[:, kc, 256:320], htile[kc][:, 0:TT], kc == 0, kc == 7)
                    sqk = sqr()
                    k.act(sqk[0:64, :], psk[0:64, 0:TT], AF.Square)
                    pss = bank()
                    for j in range(2):
                        k.mm(pss[:, 0:TT], ones_b[:], sqs[j][:], j == 0, j == 1)
                    rr = rrr()
                    rstd_from_ps(rr, pss, TT, 1.0 / 256)
                    for j in range(2):
                        k.stt(cn[j][:], pc[j][:, 0:TT], vec[:, 3 + j:4 + j], rr[:], ALU.mult, ALU.mult)
                    pss2 = bank()
                    k.mm(pss2[0:64, 0:TT], ones_b[0:64, 0:64], sqk[0:64, :], True, True)
                    rr2 = rrr()
                    rstd_from_ps(rr2[0:64], pss2[0:64], TT, 1.0 / 64, npart=64)
                    tfa = tfr()
                    k.stt(tfa[0:64, :], psk[0:64, 0:TT], vec[0:64, 8:9], rr2[0:64, :], ALU.mult, ALU.mult)
                    krb = krbr()
                    if is_ctx:
                        k.copy(krb[0:64, :], tfa[0:64, :])
                    else:
                        sqc = sqr()
                        k.copy(sqc[0:64, :], tfa[0:64, :])
                        psr = bank()
                        k.mm(psr[0:64, 0:TT], pt_b[0:64, 0:64], sqc[0:64, :], True, True)
                        tfb = tfr()
                        k.tt(tfb[0:64, :], psr[0:64, 0:TT], rsn[0:64, :], ALU.mult)
                        k.tt(tfa[0:64, :], tfa[0:64, :], rc[0:64, :], ALU.mult)
                        k.tt(krb[0:64, :], tfa[0:64, :], tfb[0:64, :], ALU.add)
                    k.dma(KR_d[seq, :, koff + t0:koff + t0 + TT], krb[0:64, :], writes=[kv_b[seq]])

                def k_b(d, mid):
                    mid()
                    t0 = d["t"] * TT
                    koff = d["koff"]
                    def kA(h):
                        ps = bank()
                        for kc in range(2):
                            k.mm(ps[:, 0:TT], wukv[:, kc, h * 256:h * 256 + 128], cn[kc][:], kc == 0, kc == 1)
                        sq = sqr()
                        k.act(sq[:], ps[:, 0:TT], AF.Square)
                        return ps, sq

                    def kB(h, st_):
                        ps, sq = st_
                        pss = bank()
                        k.mm(pss[:, 0:TT], ones_b[:], sq[:], True, True)
                        rr = rrr()
                        rstd_from_ps(rr, pss, TT, 1.0 / 128)
                        ks = kstr()
                        k.stt(ks[:], ps[:, 0:TT], vec[:, 6:7], rr[:], ALU.mult, ALU.mult)
                        k.dma(K_d[seq, h, :, koff + t0:koff + t0 + TT], ks[:], writes=[kv_b[seq]])

                    interleave(8, kA, kB)
                    rhs_ap = wukv_t.rearrange("p c (h x) -> p c h x", h=8)
                    for j in range(2):
                        vs = vstr()
                        for hf in range(2):
                            ps = bank()
                            for kc in range(2):
                                rv = V(rhs_ap[:, kc, hf * 4:(hf + 1) * 4, 128:256], wukv.bufs)
                                k.mm(ps[:, :], cn[kc][:, j * 128:(j + 1) * 128], rv, kc == 0, kc == 1)
                            k.act(vs[:, hf * 512:(hf + 1) * 512], ps[:, :], AF.Copy)
                        k.dma(V_d[seq, (koff + t0) // 128 + j, :, :], vs[:], writes=[kv_b[seq]])

                run_tiles(ktiles, kprep, k_a, k_b)

                sap, sbufs = src_of(li, seq, False)
                dap, dbuf = dst_of(li, seq, False)
                k.dma(kr_s[0:64, :], KR_d[seq], reads=[kv_b[seq]])
                k.dma(kr_s[64:128, :], KR_d[seq], reads=[kv_b[seq]])
                qtiles = [dict(t=t) for t in range(SEQ // TT)]

                def qprep(d):
                    t0 = d["t"] * TT
                    return front_a(sap[t0:t0 + TT, :], sbufs, 0, TT)

                def q_a(d):
                    t0 = d["t"] * TT
                    d["xv"] = xr_load(sap[t0:t0 + TT, :], sbufs)
                    rc, rsn = load_rope(t0)
                    pc = []
                    sqs = []
                    for j in range(3):
                        ps = bank()
                        for kc in range(8):
                            k.mm(ps[:, 0:TT], mlw[:, kc, 320 + j * 128:320 + (j + 1) * 128], htile[kc][:, 0:TT], kc == 0, kc == 7)
                        sq = sqr()
                        k.act(sq[:], ps[:, 0:TT], AF.Square)
                        pc.append(ps)
                        sqs.append(sq)
                    pss = bank()
                    for j in range(3):
                        k.mm(pss[:, 0:TT], ones_b[:], sqs[j][:], j == 0, j == 2)
                    rr = rrr()
                    rstd_from_ps(rr, pss, TT, 1.0 / 384)
                    for j in range(3):
                        k.stt(cn[j][:], pc[j][:, 0:TT], vec[:, j:j + 1], rr[:], ALU.mult, ALU.mult)
                    def qA(h):
                        ps = bank()
                        for kc in range(3):
                            k.mm(ps[:, 0:TT], wuqn[:, kc, h * 128:(h + 1) * 128], cn[kc][:], kc == 0, kc == 2)
                        sq = sqr()
                        k.act(sq[:], ps[:, 0:TT], AF.Square)
                        return ps, sq

                    def qB(h, st_):
                        ps, sq = st_
                        pss = bank()
                        k.mm(pss[:, 0:TT], ones_b[:], sq[:], True, True)
                        rr = rrr()
                        rstd_from_ps(rr, pss, TT, 1.0 / 128)
                        k.stt(qn[h][:], ps[:, 0:TT], vec[:, 5:6], rr[:], ALU.mult, ALU.mult)

                    interleave(8, qA, qB)

                    def rA(pr):
                        ps = bank()
                        for kc in range(3):
                            k.mm(ps[:, 0:TT], wuqr[:, kc, pr * 128:(pr + 1) * 128], cn[kc][:], kc == 0, kc == 2)
                        sq = sqr()
                        k.act(sq[:], ps[:, 0:TT], AF.Square)
                        return ps, sq

                    for pr in range(4):
                        ps, sq = rA(pr)
                        pss = bank()
                        k.mm(pss[:, 0:TT], blk_b[:], sq[:], True, True)
                        rr = rrr()
                        rstd_from_ps(rr, pss, TT, 1.0 / 64)
                        tfa = tfr()
                        k.stt(tfa[:], ps[:, 0:TT], vec[:, 7:8], rr[:], ALU.mult, ALU.mult)
                        sqc = sqr()
                        k.copy(sqc[:], tfa[:])
                        psr = bank()
                        k.mm(psr[:, 0:TT], pt_b[:], sqc[:], True, True)
                        tfb = tfr()
                        k.tt(tfb[:], psr[:, 0:TT], rsn[:], ALU.mult)
                        k.tt(tfa[:], tfa[:], rc[:], ALU.mult)
                        k.tt(qr[2 * pr][0:64, :], tfa[0:64, :], tfb[0:64, :], ALU.add)
                        k.tt(qr[2 * pr + 1][64:128, :], tfa[64:128, :], tfb[64:128, :], ALU.add)
                    for j in range(8):
                        ps = bank()
                        for kc in range(8):
                            k.mm(ps[:, 0:TT], mlw[:, kc, 704 + j * 128:704 + (j + 1) * 128], htile[kc][:, 0:TT], kc == 0, kc == 7)
                        k.act(sg[j][:, 0:TT], ps[:, 0:TT], AF.Silu)

                def q_b(d, mid):
                    mid()
                    t0 = d["t"] * TT
                    for h in range(8):
                        khs = kh_s[h % 2]
                        vhs = vh_s[h % 2]
                        k.dma(khs[:], K_d[seq, h], reads=[kv_b[seq]], eng="pool")
                        k.dma(vhs[:, :, 0:128], V_d[seq, :, :, h * 128:(h + 1) * 128].rearrange("b p d -> p b d"),
                              reads=[kv_b[seq]], eng="pool")
                        qnh = qn[h]
                        qrp = qr[h]
                        pso = [bank(pin=True), bank(pin=True)]

                        def s_stage(kp):
                            pss_ = bank()
                            for u in range(2):
                                kc = 2 * kp + u
                                k.mm(pss_[:, u * TT:(u + 1) * TT], khs[:, kc * 128:(kc + 1) * 128], qnh[:], True, False)
                                k.mm(pss_[:, u * TT:(u + 1) * TT], kr_s[:, kc * 128:(kc + 1) * 128], qrp[:], False, True)
                            pt_ = pTr()
                            k.act(pt_[:], pss_[:, :], AF.Exp, scale=SCALE)
                            return pt_

                        npair = nkc // 2
                        LA = 3
                        pts = {i: s_stage(i) for i in range(LA)}
                        for kp in range(npair):
                            if kp + LA < npair:
                                pts[kp + LA] = s_stage(kp + LA)
                            pt_ = pts.pop(kp)
                            for u in range(2):
                                kc = 2 * kp + u
                                for j in range(2):
                                    k.mm(pso[j][:, 0:129], pt_[:, u * TT + j * 128:u * TT + (j + 1) * 128], vhs[:, kc, 0:129],
                                         kc == 0, kc == nkc - 1)
                        for j in range(2):
                            rcol = rsum[:, (2 * h + j) % 4:(2 * h + j) % 4 + 1]
                            k.recip(rcol, pso[j][:, 128:129])
                            k.ts(otm[j][:, h * 128:(h + 1) * 128], pso[j][:, 0:128], rcol, None, ALU.mult)
                        pinned.clear()
                    for j in range(2):
                        pb = bankb()
                        for c in range(8):
                            k.tr(pb[:, c * 128:(c + 1) * 128], otm[j][:, c * 128:(c + 1) * 128], ident_b[:])
                        for c in range(8):
                            k.tt(yb[c][:, j * 128:(j + 1) * 128], pb[:, c * 128:(c + 1) * 128], sg[c][:, j * 128:(j + 1) * 128], ALU.mult)
                    dd = out_proj_residual(yb, 0, d["xv"], dap[t0:t0 + TT, :], dbuf)
                    if li == len(layers) - 1:
                        final_dmas.append(dd)

                run_tiles(qtiles, qprep, q_a, q_b)

        def layer3(li, kl):
            l = 3
            RSTD["mode"] = "sqrt"
            load_w_rows(w_in, w_in_t, ch_w_in, 3 * D)
            load_w_rows(w_out, w_out_t, ch_w_out, D)
            w_x_t = kl.sb("l3_w_x", [128, 1024], BF16)
            w_x = V(w_x_t, [Buf("w_x")])
            wsT = V(w_x_t[:, 0:1024].rearrange("p (g q) -> p g q", g=8), w_x.bufs)
            load_w(w_x, wsT.ap, ch_wsT)
            lng = kl.vsb("l3_lng", [128, D], F32)
            lnb = kl.vsb("l3_lnb", [128, D], F32)
            bsb = kl.vsb("l3_bsb", [128, 8, 256], F32)
            k.dma(lng[:], ch_lng)
            k.dma(lnb[:], ch_lnb)
            k.dma(bsb[:], ch_bsb)
            vtr = Rot([kl.vsb("l3_vt%d" % i, [128, D], F32) for i in range(2)])
            vnb = [kl.vsb("l3_vnb%d" % j, [128, D], BF16) for j in range(2)]
            s1r = Rot([kl.vsb("l3_s1%d" % i, [128, 2], F32) for i in range(2)])
            s2r = Rot([kl.vsb("l3_s2%d" % i, [128, 2], F32) for i in range(2)])
            mr = Rot([kl.vsb("l3_m%d" % i, [128, 4], F32) for i in range(2)])
            tsmr = Rot([kl.vsb("l3_tsm%d" % i, [128, 256], F32) for i in range(3)])

            def prep(d):
                if d["t"] == 0:
                    load_mod(l, d["slot"], d["seq"])
                t0 = d["t"] * TT
                return front_a(d["sap"][t0:t0 + TT, :], d["sbufs"], d["slot"], TT)

            def a_phase(d):
                t0 = d["t"] * TT
                d["xv"] = xr_load(d["sap"][t0:t0 + TT, :], d["sbufs"])
                for j in range(2):
                    s1, s2, m, vt = s1r(), s2r(), mr(), vtr()
                    k.memset(s1[:], 0.0)
                    k.memset(s2[:], 0.0)
                    for hf in range(2):
                        ps = bank()
                        for kc in range(8):
                            k.mm(ps[:, :], htile[kc][:, j * 128:(j + 1) * 128],
                                 w_in[:, kc, D + hf * 512:D + (hf + 1) * 512], kc == 0, kc == 7)
                        k.act(vt[:, hf * 512:(hf + 1) * 512], ps[:, :], AF.Identity, accum=s1[:, hf:hf + 1])
                        k.act(junk[:, 0:512], ps[:, :], AF.Square, accum=s2[:, hf:hf + 1])
                    k.tt(m[:, 0:1], s1[:, 0:1], s1[:, 1:2], ALU.add)
                    k.tt(m[:, 1:2], s2[:, 0:1], s2[:, 1:2], ALU.add)
                    k.ts(m[:, 0:2], m[:, 0:2], 1.0 / D, None, ALU.mult)
                    k.tt(m[:, 2:3], m[:, 0:1], m[:, 0:1], ALU.mult)
                    k.tt(m[:, 2:3], m[:, 1:2], m[:, 2:3], ALU.subtract)
                    k.act(m[:, 2:3], m[:, 2:3], AF.Sqrt, bias=epsc[:])
                    k.recip(m[:, 2:3], m[:, 2:3])
                    k.stt(m[:, 3:4], m[:, 0:1], -1.0, m[:, 2:3], ALU.mult, ALU.mult)
                    tf = tmpf[j % 2]
                    k.ts(tf[:], vt[:], m[:, 2:3], m[:, 3:4], ALU.mult, ALU.add)
                    k.tt(tf[:], tf[:], lng[:], ALU.mult)
                    k.tt(vnb[j][:], tf[:], lnb[:], ALU.add)
                for oc in range(8):
                    ps = bank()
                    for kc in range(8):
                        k.mm(ps[:, 0:TT], w_in[:, kc, 2 * D + oc * 128:2 * D + (oc + 1) * 128], htile[kc][:, 0:TT], kc == 0, kc == 7)
                    k.act(sg[oc][:, 0:TT], ps[:, 0:TT], AF.Silu)
                for oc in range(8):
                    ps2 = bank()
                    for kc in range(8):
                        k.mm(ps2[:, 0:TT], w_in[:, kc, oc * 128:(oc + 1) * 128], htile[kc][:, 0:TT], kc == 0, kc == 7)
                    k.tt(ya[oc][:, 0:TT], ps2[:, 0:TT], sg[oc][:, 0:TT], ALU.mult)

            def b_phase(d, mid):
                t0 = d["t"] * TT
                for g in range(8):
                    ps = bank()
                    for j in range(2):
                        k.mm(ps[:, j * 128:(j + 1) * 128], vnb[j][:, g * 128:(g + 1) * 128], wsT[:, g, :], True, True)
                    tsm = tsmr()
                    k.tt(tsm[:], ps[:, 0:TT], bsb[:, g, :], ALU.add)
                    k.tt(yb[g][:], tsm[:], ya[g][:, 0:TT], ALU.mult)
                mid()
                dd = out_proj_residual(yb, d["slot"], d["xv"], d["dap"][t0:t0 + TT, :], d["dbuf"])
                if li == len(layers) - 1:
                    final_dmas.append(dd)

            run_tiles(seg_tiles(li, False, lambda s, c: s % 2), prep, a_phase, b_phase)

        def dbg_final():
            if DBG["on"]:
                o_ = nc.dram_tensor("dbg_modrow", [4, 3, 3 * D], F32, kind="ExternalOutput").ap()
                final_dmas.append(k.dma(o_, modrow, reads=[modrow_b]))

        fns = {0: layer0, 1: layer1, 2: layer2, 3: layer3}
        for li, l in enumerate(layers):
            with ExitStack() as lst:
                kl = K(nc, lst)
                kl.S = S
                fns[l](li, kl)
                S.barrier()

        dbg_final()
        with nc.allow_low_precision("bf16 matmul operands, fp32 accumulation"):
            S.emit(st, final_dmas)
    return nc


def host_inputs(inputs, core, layers=(0, 1, 2, 3)):
    f = np.float32
    b0 = core * NB
    m = {}
    m["x"] = np.ascontiguousarray(inputs["x"][b0:b0 + NB]).astype(f)
    m["ctx"] = np.ascontiguousarray(inputs["ctx"][b0:b0 + NB]).astype(f)
    crow = np.stack([inputs["c"][b0], inputs["c"][b0 + 1], inputs["c_ctx"]], axis=0).astype(f)
    m["cT"] = np.ascontiguousarray(crow.reshape(3, 8, 128).transpose(2, 1, 0))
    m["w_mod"] = np.ascontiguousarray(inputs["w_mod"]).astype(f)
    m["b_mod3"] = np.ascontiguousarray(np.broadcast_to(inputs["b_mod"][:, None, :], (4, 3, 3 * D))).astype(f)
    m["ng_rep"] = np.ascontiguousarray(np.broadcast_to(inputs["norm_g"][:, None, :], (4, 128, D))).astype(f)
    m["ident"] = np.eye(128, dtype=f)

    def colvec(v):
        return np.ascontiguousarray(np.asarray(v).reshape(8, 128).T).astype(f)

    if 0 in layers:
        m["cv_w_in"] = np.ascontiguousarray(inputs["cv_w_in"][0]).astype(f)
        dw = inputs["cv_dw"][0]
        m["cv_dw"] = np.ascontiguousarray(dw.reshape(31, 8, 128).transpose(2, 1, 0)).astype(f)
        m["cv_vec"] = np.ascontiguousarray(np.stack([colvec(inputs["cv_db"][0]), colvec(inputs["cv_ln_g"][0]),
                                                     colvec(inputs["cv_ln_b"][0])], axis=1)).astype(f)
        m["cv_w_out"] = np.ascontiguousarray(inputs["cv_w_out"][0]).astype(f)
    if 1 in layers:
        m["pl_w_in"] = np.ascontiguousarray(inputs["pl_w_in"][0]).astype(f)
        m["pl_w_grp"] = np.ascontiguousarray(inputs["pl_w_grp"][0]).astype(f)
        m["pl_scale"] = colvec(inputs["pl_scale"][0])
        m["pl_icnt"] = np.ascontiguousarray(np.broadcast_to(pool_icnt()[None], (128, 3, 4, 256))).astype(f)
        m["pl_w_out"] = np.ascontiguousarray(inputs["pl_w_out"][0]).astype(f)
    if 2 in layers:
        m["ml_w_in"] = np.ascontiguousarray(inputs["ml_w_in"][0]).astype(f)
        m["ml_w_uq"] = np.ascontiguousarray(inputs["ml_w_uq"][0]).astype(f)
        m["ml_w_ukv"] = np.ascontiguousarray(inputs["ml_w_ukv"][0]).astype(f)
        v = np.zeros((128, 9), f)
        v[:, 0:3] = inputs["ml_q_norm"][0].reshape(3, 128).T
        v[:, 3:5] = inputs["ml_kv_norm"][0].reshape(2, 128).T
        v[:, 5] = inputs["ml_nope_norm"][0][0]
        v[:, 6] = inputs["ml_nope_norm"][0][1]
        v[:, 7] = np.tile(inputs["ml_rope_norm"][0][0], 2)
        v[:, 8] = np.tile(inputs["ml_rope_norm"][0][1], 2)
        m["ml_vec"] = v
        m["ml_w_out"] = np.ascontiguousarray(inputs["ml_w_out"][0]).astype(f)
        C2, S2, PT, blk = rope_tables()
        m["rp_c"], m["rp_s"], m["rp_pt"], m["rp_blk"] = C2, S2, PT, blk
    if 3 in layers:
        m["ch_w_in"] = np.ascontiguousarray(inputs["ch_w_in"][0]).astype(f)
        m["ch_lng"] = np.ascontiguousarray(np.broadcast_to(inputs["ch_ln_g"][0][None, :], (128, D))).astype(f)
        m["ch_lnb"] = np.ascontiguousarray(np.broadcast_to(inputs["ch_ln_b"][0][None, :], (128, D))).astype(f)
        ws = inputs["ch_w_s"][0]
        m["ch_wsT"] = np.ascontiguousarray(ws.transpose(2, 0, 1)).astype(f)
        bs = inputs["ch_b_s"][0]
        bsb = np.broadcast_to(bs.T[None, :, None, :], (128, 8, 2, 128)).reshape(128, 8, 256)
        m["ch_bsb"] = np.ascontiguousarray(bsb).astype(f)
        m["ch_w_out"] = np.ascontiguousarray(inputs["ch_w_out"][0]).astype(f)
    return m


_NC_CACHE = {}


def kernel(**inputs):
    inputs = {kk: np.asarray(v) for kk, v in inputs.items()}
    if "full" not in _NC_CACHE:
        _NC_CACHE["full"] = build()
    nc = _NC_CACHE["full"]
    n = 8
    in_maps = [host_inputs(inputs, c) for c in range(n)]
    res = run_bass_kernel_spmd(nc, in_maps, core_ids=list(range(n)))
    outs = [res.results[c]["out"] for c in range(n)]
    return np.concatenate(outs, axis=0).astype(np.float32)
```
